# Optimizing a Trainium2 kernel written in Bass

```python
import math
import jax, jax.numpy as jnp
from jax import lax
import numpy as np

D_MODEL = 2048
BATCH = 8
SEQ = 2048
DEPTH = 1

Q_BLOCK = 128
ROPE_THETA = 10000.0
NORM_EPS = 1e-6

MLA_HEADS = 8
MLA_NOPE_DIM = 128
MLA_ROPE_DIM = 64
MLA_V_DIM = 128
MLA_Q_LORA = 512
MLA_KV_LORA = 256
MLA_OUT = MLA_HEADS * MLA_V_DIM

DIFF_HEADS = 4
DIFF_HEAD_DIM = 128
DIFF_V_DIM = 2 * DIFF_HEAD_DIM
DIFF_W = DIFF_HEADS * 2 * DIFF_HEAD_DIM
DIFF_OUT = DIFF_HEADS * DIFF_V_DIM

MIX_WIDTH = MLA_OUT + DIFF_OUT

IN_SPLITS = (MLA_Q_LORA, MLA_KV_LORA, MLA_ROPE_DIM, DIFF_W, DIFF_W, DIFF_W)
IN_WIDTH = sum(IN_SPLITS)

MEM_TOKENS = 256
X_HEADS = 4
X_HEAD_DIM = 128
X_WIDTH = X_HEADS * X_HEAD_DIM

FFN_HIDDEN = int(math.ceil(8 * D_MODEL / 3 / 256) * 256)

kernel_name = "hybrid_mla_diffattn_parallel_heads_encoder"


def rms_norm(x, g, eps=NORM_EPS):
    xf = x.astype(jnp.float32)
    y = xf * lax.rsqrt(jnp.mean(xf * xf, axis=-1, keepdims=True) + eps)
    return (y * g.astype(jnp.float32)).astype(x.dtype)


def rope_tables(positions, dim):
    inv_freq = ROPE_THETA ** (-jnp.arange(0, dim, 2, dtype=jnp.float32) / dim)
    ang = positions.astype(jnp.float32)[..., None] * inv_freq
    return jnp.cos(ang), jnp.sin(ang)


def apply_rope(t, cos, sin):
    shp = cos.shape[:2] + (1,) * (t.ndim - 3) + cos.shape[-1:]
    c, s = cos.reshape(shp), sin.reshape(shp)
    half = t.shape[-1] // 2
    t1 = t[..., :half].astype(jnp.float32)
    t2 = t[..., half:].astype(jnp.float32)
    return jnp.concatenate([t1 * c - t2 * s, t1 * s + t2 * c], axis=-1).astype(t.dtype)


def _to_blocks(t):
    b, s = t.shape[:2]
    return jnp.moveaxis(t.reshape((b, s // Q_BLOCK, Q_BLOCK) + t.shape[2:]), 1, 0)


def _from_blocks(t):
    t = jnp.moveaxis(t, 0, 1)
    return t.reshape((t.shape[0], t.shape[1] * t.shape[2]) + t.shape[3:])


def mla_attention(q_nope, q_rope, k_nope, k_rope, v):
    scale = (MLA_NOPE_DIM + MLA_ROPE_DIM) ** -0.5

    def block(qs):
        qn_b, qr_b = qs
        s = (jnp.einsum('bqhd,bkhd->bhqk', qn_b, k_nope)
             + jnp.einsum('bqhr,bkr->bhqk', qr_b, k_rope))
        p = jax.nn.softmax(s.astype(jnp.float32) * scale, axis=-1).astype(v.dtype)
        return jnp.einsum('bhqk,bkhd->bqhd', p, v)

    return _from_blocks(lax.map(block, (_to_blocks(q_nope), _to_blocks(q_rope))))


def diff_attention(q1, q2, k1, k2, v, lam):
    scale = DIFF_HEAD_DIM ** -0.5

    def block(qs):
        q1_b, q2_b = qs
        p1 = jax.nn.softmax(jnp.einsum('bqhd,bkhd->bhqk', q1_b, k1).astype(jnp.float32) * scale, axis=-1)
        p2 = jax.nn.softmax(jnp.einsum('bqhd,bkhd->bhqk', q2_b, k2).astype(jnp.float32) * scale, axis=-1)
        p = (p1 - lam * p2).astype(v.dtype)
        return jnp.einsum('bhqk,bkhe->bqhe', p, v)

    return _from_blocks(lax.map(block, (_to_blocks(q1), _to_blocks(q2))))


def setup_inputs(seed: int = 0) -> dict:
    key = jax.random.key(seed)
    ks = iter(jax.random.split(key, 40))

    def dense(shape):
        return jax.random.normal(next(ks), shape, jnp.float32) * (shape[-2] ** -0.5)

    def gain(n):
        return 1.0 + 0.02 * jax.random.normal(next(ks), (DEPTH, n), jnp.float32)

    x = jax.random.normal(next(ks), (BATCH, SEQ, D_MODEL), jnp.float32)
    mem = jax.random.normal(next(ks), (BATCH, MEM_TOKENS, D_MODEL), jnp.float32)
    offsets = jax.random.randint(next(ks), (BATCH, 1), 0, 1024, dtype=jnp.int32)
    positions = jnp.arange(SEQ, dtype=jnp.int32)[None, :] + offsets

    return {
        "x": x,
        "mem": mem,
        "positions": positions,
        "g_mix": gain(D_MODEL),
        "w_in": dense((DEPTH, D_MODEL, IN_WIDTH)),
        "g_q_lat": gain(MLA_Q_LORA),
        "w_uq": dense((DEPTH, MLA_Q_LORA, MLA_HEADS * (MLA_NOPE_DIM + MLA_ROPE_DIM))),
        "g_kv_lat": gain(MLA_KV_LORA),
        "w_ukv": dense((DEPTH, MLA_KV_LORA, MLA_HEADS * (MLA_NOPE_DIM + MLA_V_DIM))),
        "lambda_q1": 0.1 * jax.random.normal(next(ks), (DEPTH, DIFF_HEAD_DIM), jnp.float32),
        "lambda_k1": 0.1 * jax.random.normal(next(ks), (DEPTH, DIFF_HEAD_DIM), jnp.float32),
        "lambda_q2": 0.1 * jax.random.normal(next(ks), (DEPTH, DIFF_HEAD_DIM), jnp.float32),
        "lambda_k2": 0.1 * jax.random.normal(next(ks), (DEPTH, DIFF_HEAD_DIM), jnp.float32),
        "g_diff_sub": gain(DIFF_V_DIM),
        "w_out": dense((DEPTH, MIX_WIDTH, D_MODEL)),
        "g_xattn": gain(D_MODEL),
        "g_mem": gain(D_MODEL),
        "w_xq": dense((DEPTH, D_MODEL, X_WIDTH)),
        "w_xk": dense((DEPTH, D_MODEL, X_WIDTH)),
        "w_xv": dense((DEPTH, D_MODEL, X_WIDTH)),
        "w_xo": dense((DEPTH, X_WIDTH, D_MODEL)),
        "g_ffn": gain(D_MODEL),
        "w_gate": dense((DEPTH, D_MODEL, FFN_HIDDEN)),
        "w_up": dense((DEPTH, D_MODEL, FFN_HIDDEN)),
        "w_down": dense((DEPTH, FFN_HIDDEN, D_MODEL)),
        "g_final": 1.0 + 0.02 * jax.random.normal(next(ks), (D_MODEL,), jnp.float32),
    }


def reference(x, mem, positions, g_mix, w_in, g_q_lat, w_uq, g_kv_lat, w_ukv,
              lambda_q1, lambda_k1, lambda_q2, lambda_k2, g_diff_sub, w_out,
              g_xattn, g_mem, w_xq, w_xk, w_xv, w_xo,
              g_ffn, w_gate, w_up, w_down, g_final):
    b, s, _ = x.shape
    m_len = mem.shape[1]
    cos_r, sin_r = rope_tables(positions, MLA_ROPE_DIM)
    cos_d, sin_d = rope_tables(positions, DIFF_HEAD_DIM)
    split_pts = [int(v) for v in np.cumsum(IN_SPLITS)[:-1]]

    for layer in range(DEPTH):
        h = rms_norm(x, g_mix[layer])
        proj = h @ w_in[layer]
        c_q, c_kv, k_rope, dq, dk, dv = jnp.split(proj, split_pts, axis=-1)

        q = (rms_norm(c_q, g_q_lat[layer]) @ w_uq[layer]).reshape(b, s, MLA_HEADS, MLA_NOPE_DIM + MLA_ROPE_DIM)
        q_nope = q[..., :MLA_NOPE_DIM]
        q_rope = apply_rope(q[..., MLA_NOPE_DIM:], cos_r, sin_r)
        kv = (rms_norm(c_kv, g_kv_lat[layer]) @ w_ukv[layer]).reshape(b, s, MLA_HEADS, MLA_NOPE_DIM + MLA_V_DIM)
        k_nope, v_mla = kv[..., :MLA_NOPE_DIM], kv[..., MLA_NOPE_DIM:]
        k_rope = apply_rope(k_rope, cos_r, sin_r)
        out_mla = mla_attention(q_nope, q_rope, k_nope, k_rope, v_mla).reshape(b, s, MLA_OUT)

        dq = apply_rope(dq.reshape(b, s, DIFF_HEADS, 2, DIFF_HEAD_DIM), cos_d, sin_d)
        dk = apply_rope(dk.reshape(b, s, DIFF_HEADS, 2, DIFF_HEAD_DIM), cos_d, sin_d)
        dv = dv.reshape(b, s, DIFF_HEADS, DIFF_V_DIM)
        lambda_init = 0.8 - 0.6 * math.exp(-0.3 * layer)
        lam = (jnp.exp(jnp.sum(lambda_q1[layer].astype(jnp.float32) * lambda_k1[layer].astype(jnp.float32)))
               - jnp.exp(jnp.sum(lambda_q2[layer].astype(jnp.float32) * lambda_k2[layer].astype(jnp.float32)))
               + lambda_init)
        o_diff = diff_attention(dq[..., 0, :], dq[..., 1, :], dk[..., 0, :], dk[..., 1, :], dv, lam)
        o_diff = rms_norm(o_diff, g_diff_sub[layer], eps=1e-5) * (1.0 - lambda_init)
        out_diff = o_diff.reshape(b, s, DIFF_OUT)

        x = x + jnp.concatenate([out_mla, out_diff], axis=-1) @ w_out[layer]

        hx = rms_norm(x, g_xattn[layer])
        hm = rms_norm(mem, g_mem[layer])
        xq = (hx @ w_xq[layer]).reshape(b, s, X_HEADS, X_HEAD_DIM)
        xk = (hm @ w_xk[layer]).reshape(b, m_len, X_HEADS, X_HEAD_DIM)
        xv = (hm @ w_xv[layer]).reshape(b, m_len, X_HEADS, X_HEAD_DIM)
        sc = jnp.einsum('bqhd,bkhd->bhqk', xq, xk).astype(jnp.float32) * (X_HEAD_DIM ** -0.5)
        p = jax.nn.softmax(sc, axis=-1).astype(xv.dtype)
        xo = jnp.einsum('bhqk,bkhd->bqhd', p, xv).reshape(b, s, X_WIDTH)
        x = x + xo @ w_xo[layer]

        hf = rms_norm(x, g_ffn[layer])
        x = x + (jax.nn.silu(hf @ w_gate[layer]) * (hf @ w_up[layer])) @ w_down[layer]

    return rms_norm(x, g_final)
```

```python
import math
import contextlib
import numpy as np
import concourse.bass as bass
import concourse.mybir as mybir
from concourse.bass_utils import run_bass_kernel_spmd

F32 = mybir.dt.float32
BF16 = mybir.dt.bfloat16
I32 = mybir.dt.int32
AF = mybir.ActivationFunctionType
ALU = mybir.AluOpType
AX = mybir.AxisListType

S = 2048
D = 2048
NT = 16
NC16 = 16
FF = 5632
NFC = 44
MEM = 256
IN_W = 3904
EPS = 1e-6

ENGS = ["pe", "act", "dve", "pool", "sp"]
EPOCH = 2000
NDMASEM = 16


class Prog:
    def __init__(self):
        self.ops = {e: [] for e in ENGS}
        self.res = {}
        self.dma_n = {"sp": 0, "pool": 0}

    @staticmethod
    def _joint(r):
        return r is not None and not r[1] and r[0] and all(t[0] == "d" for t in r[0]) and len(r[0]) < 48

    def _deps(self, reads, writes, is_dma=False):
        toks = []
        for k in reads:
            r = self.res.get(k)
            if r:
                toks.extend(r[0])
        for k in writes:
            r = self.res.get(k)
            if r:
                if is_dma and self._joint(r):
                    toks.extend(r[2])
                else:
                    toks.extend(r[0])
                    toks.extend(r[1].values())
        return toks

    def _commit(self, tok, reads, writes):
        for k in reads:
            if k in writes:
                continue
            r = self.res.setdefault(k, [[], {}, []])
            if tok[0] == "e":
                r[1][tok[1]] = tok
            else:
                r[1][(tok[1], tok[2] % NDMASEM)] = tok
        for k in writes:
            r = self.res.get(k)
            if tok[0] == "d" and self._joint(r):
                r[0].append(tok)
            else:
                inherited = (list(r[0]) + list(r[1].values())) if (r and tok[0] == "d") else []
                self.res[k] = [[tok], {}, inherited]

    def op(self, eng, meth, reads=(), writes=(), **kw):
        if eng != "pe":
            psr = [k for k in reads if k.startswith("ps") and k not in writes]
            if psr:
                writes = list(writes) + psr
        toks = self._deps(reads, writes)
        idx = len(self.ops[eng])
        self.ops[eng].append({"waits": toks, "fn": (lambda e, m=meth, kw=kw: getattr(e, m)(**kw)), "sig": False})
        self._commit(("e", eng, idx), reads, writes)

    def dma(self, q, out, in_, reads=(), writes=()):
        toks = self._deps(reads, writes, is_dma=True)
        k = self.dma_n[q]
        self.dma_n[q] += 1
        if k >= NDMASEM:
            toks.append(("d", q, k - NDMASEM))
        self.ops[q].append({"waits": toks, "fn": (lambda e, o=out, i=in_: e.dma_start(out=o, in_=i)),
                            "sig": False, "dma": k})
        self._commit(("d", q, k), reads, writes)

    def barrier(self):
        toks = []
        for e in ENGS:
            for i in range(len(self.ops[e]) - 1, -1, -1):
                o = self.ops[e][i]
                if o["fn"] is not None and "dma" not in o:
                    toks.append(("e", e, i))
                    break
        for q in ("sp", "pool"):
            n = self.dma_n[q]
            for k in range(max(0, n - NDMASEM), n):
                toks.append(("d", q, k))
        for e in ENGS:
            self.ops[e].append({"waits": list(toks), "fn": None, "sig": False})

    def finalize(self):
        for e in ENGS:
            for o in self.ops[e]:
                for t in o["waits"]:
                    if t[0] == "e" and not (t[1] == e and e in ("pe", "sp")):
                        self.ops[t[1]][t[2]]["sig"] = True
        self.cnt = {}
        for e in ENGS:
            c = 0
            for o in self.ops[e]:
                if o["sig"]:
                    c += 1
                    o["cnt"] = c
            self.cnt[e] = c
        return {e: max(1, (self.cnt[e] + EPOCH - 1) // EPOCH) for e in ENGS}

    def run(self, e, handle, sems):
        seen_e = {}
        seen_d = {}
        for o in self.ops[e]:
            for t in o["waits"]:
                if t[0] == "e":
                    _, pe_, idx = t
                    if pe_ == e and e in ("pe", "sp"):
                        continue
                    if seen_e.get(pe_, -1) >= idx:
                        continue
                    seen_e[pe_] = idx
                    c = self.ops[pe_][idx]["cnt"]
                    handle.wait_ge(sems["eng"][pe_][(c - 1) // EPOCH], (c - 1) % EPOCH + 1)
                else:
                    _, q, k = t
                    slot = k % NDMASEM
                    val = 16 * (k // NDMASEM + 1)
                    if seen_d.get((q, slot), 0) >= val:
                        continue
                    seen_d[(q, slot)] = val
                    handle.wait_ge(sems["dma"][q][slot], val)
            if o["fn"] is None:
                continue
            inst = o["fn"](handle)
            if "dma" in o:
                inst.then_inc(sems["dma"][e][o["dma"] % NDMASEM], 16)
            elif o["sig"]:
                c = o["cnt"]
                inst.then_inc(sems["eng"][e][(c - 1) // EPOCH], 1)
        if e == "sp":
            for q in ("sp", "pool"):
                n = self.dma_n[q]
                for k in range(max(0, n - NDMASEM), n):
                    handle.wait_ge(sems["dma"][q][k % NDMASEM], 16 * (k // NDMASEM + 1))


class _Stop(Exception):
    pass


_STOP = [99]


def _phase(n):
    if n > _STOP[0]:
        raise _Stop()


class Arena:
    def __init__(self, ap, size):
        self.ap, self.size, self.off, self.n = ap, size, 0, 0

    def reset(self):
        self.off = 0

    def _take(self, nel):
        self.off = (self.off + 15) // 16 * 16
        v = self.ap[:, self.off:self.off + nel]
        self.off += nel
        assert self.off <= self.size, ("arena overflow", self.off, self.size)
        self.n += 1
        return v

    def bf(self, *free):
        n = int(np.prod(free))
        v = self._take(n)
        if len(free) == 2:
            v = v.rearrange("p (a b) -> p a b", a=free[0])
        return v, f"A{self.n}"

    def f32(self, *free):
        n = int(np.prod(free))
        v = self._take(2 * n).bitcast(F32)
        if len(free) == 2:
            v = v.rearrange("p (a b) -> p a b", a=free[0])
        return v, f"A{self.n}"

    def i32(self, *free):
        n = int(np.prod(free))
        v = self._take(2 * n).bitcast(I32)
        return v, f"A{self.n}"


C_GMIX, C_GX, C_GMEM, C_GFFN, C_GQ, C_GKV, C_GSUB = 0, 16, 32, 48, 64, 68, 70
NCOL = 72
ARENA_EL = 100 * 1024


def build_program():
    nc = bass.Bass("TRN2", target_bir_lowering=False)

    def din(name, shape, dt=F32):
        return nc.dram_tensor(name, list(shape), dt, kind="ExternalInput").ap()

    def dscr(name, shape, dt):
        return nc.dram_tensor(name, list(shape), dt).ap()

    x = din("x", [S, D])
    mem = din("mem", [MEM, D])
    pos = din("pos", [1, S], I32)
    w_in = din("w_in", [D, IN_W])
    w_uq = din("w_uq", [512, 1536])
    w_ukv = din("w_ukv", [256, 2048])
    w_out = din("w_out", [D, D])
    w_xq = din("w_xq", [D, 512])
    w_xk = din("w_xk", [D, 512])
    w_xv = din("w_xv", [D, 512])
    w_xo = din("w_xo", [512, D])
    w_gate = din("w_gate", [D, FF])
    w_up = din("w_up", [D, FF])
    w_down = din("w_down", [FF, D])
    cols_d = din("cols", [128, NCOL])
    lam_d = din("lamv", [4, 128])
    gfin_d = din("gfin", [1, D])
    cf_d = din("cf", [128, 4])
    cb_d = din("cb", [128, 512])
    y = nc.dram_tensor("y", [S, D], F32, kind="ExternalOutput").ap()

    QKd = dscr("s_qkd", [16, 128, S], BF16)
    Vd = dscr("s_vd", [4, 128, NT, 256], BF16)
    CQ = dscr("s_cq", [4, 128, S], BF16)
    CKV = dscr("s_ckv", [2, 128, S], BF16)
    KR = dscr("s_kr", [64, S], BF16)
    QN = dscr("s_qn", [8, 128, S], BF16)
    QR = dscr("s_qr", [4, 128, S], BF16)
    KN = dscr("s_kn", [8, 128, S], BF16)
    Vm = dscr("s_vm", [8, 128, NT, 128], BF16)
    MIX = dscr("s_mix", [16, 128, S], BF16)
    X1 = dscr("s_x1", [S, D], F32)
    X2 = dscr("s_x2", [S, D], F32)
    HF = dscr("s_hf", [16, 128, S], BF16)

    P = Prog()
    es = contextlib.ExitStack()
    with es:
        def sb(name, shape, dt):
            return es.enter_context(nc.sbuf_tensor(name, list(shape), dt))

        arena_t = sb("arena", [128, ARENA_EL], BF16)
        AR = Arena(arena_t, ARENA_EL)
        cb = sb("cb_sb", [128, 512], BF16)
        cols = sb("cols_sb", [128, NCOL], F32)
        cf = sb("cf_sb", [128, 4], F32)
        sm = sb("sm", [128, 64], F32)
        lamb = sb("lamb", [128, 4, 128], F32)
        psA = es.enter_context(nc.psum_tensor("psA", [128, 2048], F32))
        psB = es.enter_context(nc.psum_tensor("psB", [128, 2048], F32))

        ident = cb[:, 0:128]
        ones = cb[:, 128:256]
        swd = cb[:, 256:384]
        swm = cb[:, 384:512]

        def bank(b):
            t = psA if b < 4 else psB
            return t[:, (b % 4) * 512:(b % 4 + 1) * 512]

        def bk(b):
            return f"ps{b}"

        def banks4(g):
            return (psA if g == 0 else psB), [f"ps{4 * g + i}" for i in range(4)]

        def bankpair_bf(b):
            t = psA if b < 4 else psB
            return t[:, (b % 4) * 512:(b % 4 + 2) * 512].bitcast(BF16)

        EPS6, EPS5, NEGLAM, GS0, GS1 = 0, 1, 2, 3, 4
        SS0 = 8
        uid = [0]

        def key(prefix):
            uid[0] += 1
            return f"{prefix}{uid[0]}"

        try:
            P.dma("pool", cb[:], cb_d, writes=["cb"])
            P.dma("sp", cols[:], cols_d, writes=["cols"])
            P.dma("sp", cf[:], cf_d, writes=["cf"])
            for i in range(4):
                P.dma("sp", lamb[:, i, :], lam_d[i:i + 1, :].partition_broadcast(128), writes=[f"lamb{i}"])
            P.op("dve", "memset", writes=["sm_eps6", "sm_eps5", "sm_neglam", "sm_gs", "sm5", "sm6", "sm7"] + [f"sm_ss{i}" for i in range(4)], ap=sm[:], constant=0.0)
            P.op("dve", "memset", writes=["sm_eps6"], ap=sm[:, EPS6:EPS6 + 1], constant=1e-6)
            P.op("dve", "memset", writes=["sm_eps5"], ap=sm[:, EPS5:EPS5 + 1], constant=1e-5)
            P.op("dve", "tensor_tensor", reads=["lamb0", "lamb1"], writes=["lamb0"], out=lamb[:, 0, :], in0=lamb[:, 0, :], in1=lamb[:, 1, :], op=ALU.mult)
            P.op("dve", "tensor_tensor", reads=["lamb2", "lamb3"], writes=["lamb2"], out=lamb[:, 2, :], in0=lamb[:, 2, :], in1=lamb[:, 3, :], op=ALU.mult)
            P.op("dve", "reduce_sum", reads=["lamb0"], writes=["sm5"], out=sm[:, 5:6], in_=lamb[:, 0, :], axis=AX.X)
            P.op("dve", "reduce_sum", reads=["lamb2"], writes=["sm6"], out=sm[:, 6:7], in_=lamb[:, 2, :], axis=AX.X)
            P.op("act", "activation", reads=["sm5"], writes=["sm5"], out=sm[:, 5:6], in_=sm[:, 5:6], func=AF.Exp)
            P.op("act", "activation", reads=["sm6"], writes=["sm6"], out=sm[:, 6:7], in_=sm[:, 6:7], func=AF.Exp)
            P.op("dve", "tensor_tensor", reads=["sm5", "sm6"], writes=["sm7"], out=sm[:, 7:8], in0=sm[:, 6:7], in1=sm[:, 5:6], op=ALU.subtract)
            P.op("dve", "tensor_scalar", reads=["sm7"], writes=["sm_neglam"], out=sm[:, NEGLAM:NEGLAM + 1], in0=sm[:, 7:8], scalar1=-0.2, scalar2=None, op0=ALU.add)
            P.op("dve", "tensor_scalar", reads=["cols"], writes=["sm_gs"], out=sm[:, GS0:GS0 + 2], in0=cols[:, C_GSUB:C_GSUB + 2], scalar1=0.8, scalar2=None, op0=ALU.mult)

            ssn = [0]

            def rstd_cols(src_ap, junk_ap, n_feat, eps_col, rk, wk_junk):
                slot = ssn[0] % 4
                ssn[0] += 1
                c0 = SS0 + 3 * slot
                kss = f"sm_ss{slot}"
                P.op("act", "activation", reads=[kss], writes=[kss], out=sm[:, c0:c0 + 1], in_=sm[:, c0:c0 + 1], func=AF.Copy, scale=0.0)
                P.op("act", "activation", reads=rk, writes=[kss] + wk_junk, out=junk_ap, in_=src_ap, func=AF.Square, accum_out=sm[:, c0:c0 + 1])
                P.op("act", "activation", reads=[kss, "sm_eps6", "sm_eps5"], writes=[kss], out=sm[:, c0 + 1:c0 + 2], in_=sm[:, c0:c0 + 1], func=AF.Ln, bias=sm[:, eps_col:eps_col + 1], scale=1.0 / n_feat)
                P.op("act", "activation", reads=[kss], writes=[kss], out=sm[:, c0 + 2:c0 + 3], in_=sm[:, c0 + 1:c0 + 2], func=AF.Exp, scale=-0.5)
                return sm[:, c0 + 2:c0 + 3], kss

            trn = [0]

            def norm_s1(src, ksrc, xnb, kxnb, eng="dve"):
                r, kr = rstd_cols(src, xnb, D, EPS6, [ksrc], [kxnb])
                P.op(eng, "tensor_scalar", reads=[ksrc, kr], writes=[kxnb], out=xnb, in0=src, scalar1=r, scalar2=None, op0=ALU.mult)

            def norm_s2(xnb, kxnb, gcol0, dst3, kdst, b0=None):
                if b0 is None:
                    b0 = 6 if trn[0] % 2 else 4
                    trn[0] += 1
                pst = bankpair_bf(b0)
                for c in range(16):
                    P.op("pe", "transpose", reads=[kxnb, "cb"], writes=[bk(b0), bk(b0 + 1)], out=pst[:, c * 128:(c + 1) * 128], in_=xnb[:, c * 128:(c + 1) * 128], identity=ident)
                P.op("dve", "tensor_tensor", reads=[bk(b0), bk(b0 + 1), "cols"], writes=[kdst], out=dst3, in0=pst.rearrange("p (c t) -> p c t", c=16), in1=cols[:, gcol0:gcol0 + 16].unsqueeze(2).to_broadcast([128, 16, 128]), op=ALU.mult)

            def norm_to_T(src, ksrc, xnb, kxnb, gcol0, dst3, kdst):
                norm_s1(src, ksrc, xnb, kxnb)
                norm_s2(xnb, kxnb, gcol0, dst3, kdst)

            def dma_split(q, dst, src, reads, writes, grp=2):
                if len(dst.shape) == 3 and dst.shape[1] > grp:
                    for c0 in range(0, dst.shape[1], grp):
                        c1 = min(dst.shape[1], c0 + grp)
                        P.dma(q, dst[:, c0:c1, :], src[:, c0:c1, :], reads=reads, writes=writes)
                else:
                    P.dma(q, dst, src, reads=reads, writes=writes)

            def load_w(dst, kdst, src):
                dma_split("pool", dst, src, [], [kdst])

            def wview(w, r0, nr_chunks, c0, ncols):
                return w[r0:r0 + nr_chunks * 128, c0:c0 + ncols].rearrange("(c p) n -> p c n", p=128)

            AR.reset()
            hT, khT = AR.bf(16, S)
            xnb2 = [AR.bf(D) for _ in range(2)]
            wblk = [AR.bf(16, 512) for _ in range(2)]
            sqb = [AR.bf(512) for _ in range(6)]
            tbf = [AR.bf(512) for _ in range(2)]
            stage = [AR.bf(S) for _ in range(2)]
            vstage = [AR.bf(512) for _ in range(2)]
            xt2 = [AR.f32(D) for _ in range(2)]
            tmpf, ktmpf = AR.f32(6, 512)
            cosd, kcosd = AR.f32(S)
            sind, ksind = AR.f32(S)
            cosm, kcosm = AR.f32(S)
            sinm, ksinm = AR.f32(S)
            rp8, krp8 = AR.f32(8, 512)
            ropeu = [(rp8[:, j, :], f"{krp8}_{j}") for j in range(0, 2)]
            ropew = [(rp8[:, j, :], f"{krp8}_{j}") for j in range(2, 4)]
            rsl = [(rp8[:, j, :], f"{krp8}_{j}") for j in range(4, 6)]
            lnl = [(rp8[:, j, :], f"{krp8}_{j}") for j in range(6, 8)]

            rp8f = rp8.rearrange("p a b -> p (a b)")
            posf, kposf = rp8f[:, 0:S], "tab_posf"
            posi, kposi = rp8f[:, S:2 * S].bitcast(I32), "tab_posi"
            ua, kua = tmpf.rearrange("p a b -> p (a b)")[:, 0:S], ktmpf

            def tab_setup():
                P.dma("sp", posi, pos.partition_broadcast(128), writes=[kposi])
                P.op("dve", "tensor_copy", reads=[kposi], writes=[kposf], out=posf, in_=posi)

            def make_tables_gen(cdst, kc, sdst, ks, fcol, sign_col):
                P.op("dve", "tensor_scalar", reads=[kposf, "cf"], writes=[kua], out=ua, in0=posf, scalar1=cf[:, fcol:fcol + 1], scalar2=None, op0=ALU.mult)
                yield
                P.op("dve", "tensor_copy", reads=[kua], writes=[kposi], out=posi, in_=ua)
                yield
                P.op("dve", "tensor_copy", reads=[kposi], writes=[ks], out=sdst, in_=posi)
                yield
                P.op("dve", "tensor_tensor", reads=[kua, ks], writes=[kua], out=ua, in0=ua, in1=sdst, op=ALU.subtract)
                yield
                P.op("dve", "tensor_scalar", reads=[kua], writes=[ks], out=sdst, in0=ua, scalar1=0.5, scalar2=None, op0=ALU.is_gt)
                yield
                P.op("dve", "tensor_tensor", reads=[kua, ks], writes=[kua], out=ua, in0=ua, in1=sdst, op=ALU.subtract)
                yield
                P.op("dve", "tensor_scalar", reads=[kua], writes=[ks], out=sdst, in0=ua, scalar1=-0.5, scalar2=None, op0=ALU.is_lt)
                yield
                P.op("dve", "tensor_tensor", reads=[kua, ks], writes=[kua], out=ua, in0=ua, in1=sdst, op=ALU.add)
                yield
                P.op("dve", "tensor_scalar", reads=[kua], writes=[kc], out=cdst, in0=ua, scalar1=0.25, scalar2=None, op0=ALU.add)
                yield
                P.op("act", "activation", reads=[kua], writes=[ks], out=sdst, in_=ua, func=AF.Sin, scale=6.28318)
                yield
                P.op("dve", "tensor_scalar", reads=[kc], writes=[kua], out=ua, in0=cdst, scalar1=0.5, scalar2=None, op0=ALU.is_gt)
                yield
                P.op("dve", "tensor_tensor", reads=[kua, kc], writes=[kc], out=cdst, in0=cdst, in1=ua, op=ALU.subtract)
                yield
                P.op("act", "activation", reads=[kc], writes=[kc], out=cdst, in_=cdst, func=AF.Sin, scale=6.28318)
                yield
                P.op("dve", "tensor_scalar", reads=[ks, "cf"], writes=[ks], out=sdst, in0=sdst, scalar1=cf[:, sign_col:sign_col + 1], scalar2=None, op0=ALU.mult)
                yield

            def make_tables(*a):
                for _ in make_tables_gen(*a):
                    pass

            _phase(1)
            def tabgen():
                tab_setup()
                yield
                yield from make_tables_gen(cosd, kcosd, sind, ksind, 0, 2)
                yield from make_tables_gen(cosm, kcosm, sinm, ksinm, 1, 3)
            tg = tabgen()

            blocks = [(0, 512), (512, 320)] + [(832 + 512 * i, 512) for i in range(4)] + [(2880 + 512 * i, 512) for i in range(2)]

            def load_blk(i):
                c0, n = blocks[i]
                dst, kd = wblk[i % 2]
                load_w(dst[:, :, 0:n], kd, wview(w_in, 0, 16, c0, n))
                if i == 1:
                    load_w(dst[:, :, 320:384], kd, wview(w_in, 0, 16, 768, 64))

            load_blk(6)
            load_blk(7)
            def xload(t):
                P.dma("sp", xt2[t % 2][0], x[t * 128:(t + 1) * 128, :], writes=[xt2[t % 2][1]])

            def s1(t):
                norm_s1(xt2[t % 2][0], xt2[t % 2][1], xnb2[t % 2][0], xnb2[t % 2][1])

            def s2(t):
                norm_s2(xnb2[t % 2][0], xnb2[t % 2][1], C_GMIX, hT[:, :, t * 128:(t + 1) * 128], f"{khT}_{t}")

            xload(0)
            xload(1)
            s1(0)
            xload(2)
            s1(1)
            xload(3)
            s2(0)
            for t in range(NT):
                if t + 2 < NT:
                    s1(t + 2)
                    if t + 4 < NT:
                        xload(t + 4)
                if t + 1 < NT:
                    s2(t + 1)
                for bi in (6, 7):
                    wb_, kwb = wblk[bi % 2]
                    b_ = (2 * t + bi) % 3
                    for c in range(16):
                        P.op("pe", "matmul", reads=[kwb, f"{khT}_{t}"], writes=[bk(b_)], out=bank(b_), lhsT=hT[:, c, t * 128:(t + 1) * 128], rhs=wb_[:, c, 0:512], start=c == 0, stop=c == 15)
                    vs_ap, kvs = vstage[bi % 2]
                    P.op("act", "activation", reads=[bk(b_)], writes=[kvs], out=vs_ap, in_=bank(b_), func=AF.Copy)
                    h0 = 2 * (bi - 6)
                    P.dma("sp", Vd[h0:h0 + 2, :, t, :].rearrange("h p d -> p h d"), vs_ap.rearrange("p (h d) -> p h d", h=2),
                          reads=[kvs], writes=["Vd"])
                next(tg, None)
                next(tg, None)
            for _ in tg:
                pass
            P.barrier()

            _phase(2)
            rope_pend = []

            def rope_flush():
                while rope_pend:
                    rope_pend.pop(0)()

            def rope_tile(src_bank, npart, ctab, stab, kct, kst, swap, tb, dst, kdst, slot):
                tb_ap, ktb = tbf[slot]
                u_ap, ku = ropeu[slot]
                w_ap, kw = ropew[slot]
                sl = slice(tb * 512, (tb + 1) * 512)
                src = bank(src_bank)[0:npart, :]
                P.op("act", "activation", reads=[bk(src_bank)], writes=[ktb], out=tb_ap[0:npart, :], in_=src, func=AF.Copy)
                P.op("dve", "tensor_tensor", reads=[bk(src_bank), kct], writes=[ku], out=u_ap[0:npart, :], in0=src, in1=ctab[0:npart, sl], op=ALU.mult)
                rope_flush()

                def part2():
                    P.op("pe", "matmul", reads=[ktb, "cb"], writes=[bk(7)], out=bank(7)[0:npart, :], lhsT=swap[0:npart, 0:npart], rhs=tb_ap[0:npart, :], start=True, stop=True)
                    P.op("dve", "tensor_tensor", reads=[bk(7), kst], writes=[kw], out=w_ap[0:npart, :], in0=bank(7)[0:npart, :], in1=stab[0:npart, sl], op=ALU.mult)
                    P.op("dve", "tensor_tensor", reads=[ku, kw], writes=[kdst], out=dst, in0=u_ap[0:npart, :], in1=w_ap[0:npart, :], op=ALU.add)
                rope_pend.append(part2)

            load_blk(0)
            _phase(2.02)
            load_blk(1)
            _phase(2.05)
            (w0, kw0), (w1, kw1) = wblk[0], wblk[1]
            for tb in range(4):
                tsl = slice(tb * 512, (tb + 1) * 512)
                for ch in range(7):
                    wsrc, kws = (w0, kw0) if ch < 4 else (w1, kw1)
                    cofs = ch * 128 if ch < 4 else (ch - 4) * 128
                    m = 128 if ch < 6 else 64
                    if tb == 0 and ch == 1: _phase(2.06)
                    if tb == 0 and ch == 6: _phase(2.07)
                    for c in range(16):
                        P.op("pe", "matmul", reads=[kws] + [f"{khT}_{tt}" for tt in range(4 * tb, 4 * tb + 4)], writes=[bk(ch)], out=bank(ch), lhsT=wsrc[:, c, cofs:cofs + 128], rhs=hT[:, c, tsl], start=c == 0, stop=c == 15)
                if tb == 0: _phase(2.1)
                st_ap, kst_ = stage[tb % 2]
                rope_tile(6, 128, cosm, sinm, kcosm, ksinm, swm, tb, st_ap[:, 0:512], kst_, tb % 2)
                if tb == 0: _phase(2.2)
                rope_flush()
                P.dma("sp", KR[:, tsl], st_ap[0:64, 0:512], reads=[kst_], writes=["KR"])
                if tb == 0: _phase(2.3)
                for ch in range(6):
                    sq_ap, ksq = sqb[ch]
                    P.op("dve", "tensor_copy", reads=[bk(ch)], writes=[ktmpf], out=tmpf[:, ch, :], in_=bank(ch))
                    P.op("act", "activation", reads=[ktmpf], writes=[ksq], out=sq_ap, in_=tmpf[:, ch, :], func=AF.Square)
                if tb == 0: _phase(2.4)
                for ch in range(4):
                    P.op("pe", "matmul", reads=[sqb[ch][1], "cb"], writes=[bk(6)], out=bank(6), lhsT=ones, rhs=sqb[ch][0], start=ch == 0, stop=ch == 3)
                for ch in range(4, 6):
                    P.op("pe", "matmul", reads=[sqb[ch][1], "cb"], writes=[bk(7)], out=bank(7), lhsT=ones, rhs=sqb[ch][0], start=ch == 4, stop=ch == 5)
                if tb == 0: _phase(2.5)
                for j, (bnk, nf) in enumerate(((6, 512), (7, 256))):
                    ln_ap, kln = lnl[j]
                    rs_ap, krs = rsl[j]
                    P.op("act", "activation", reads=[bk(bnk), "sm_eps6"], writes=[kln], out=ln_ap, in_=bank(bnk), func=AF.Ln, bias=sm[:, EPS6:EPS6 + 1], scale=1.0 / nf)
                    P.op("act", "activation", reads=[kln], writes=[krs], out=rs_ap, in_=ln_ap, func=AF.Exp, scale=-0.5)
                if tb == 0: _phase(2.6)
                st2_ap, kst2 = stage[(tb + 1) % 2]
                st3 = st2_ap.rearrange("p (a b) -> p a b", a=4)
                for ch in range(4):
                    P.op("dve", "scalar_tensor_tensor", reads=[ktmpf, "cols", rsl[0][1]], writes=[kst2], out=st3[:, ch, :], in0=tmpf[:, ch, :], scalar=cols[:, C_GQ + ch:C_GQ + ch + 1], in1=rsl[0][0], op0=ALU.mult, op1=ALU.mult)
                P.dma("sp", CQ[:, :, tsl].rearrange("c p t -> p c t"), st3, reads=[kst2], writes=["CQ"])
                if tb == 0: _phase(2.7)
                vs_ap, kvs = vstage[0]
                vs_b, kvs_b = vstage[1]
                for j, (dst_, kd_) in enumerate(((vs_ap, kvs), (vs_b, kvs_b))):
                    ch = 4 + j
                    P.op("dve", "scalar_tensor_tensor", reads=[ktmpf, "cols", rsl[1][1]], writes=[kd_], out=dst_, in0=tmpf[:, ch, :], scalar=cols[:, C_GKV + j:C_GKV + j + 1], in1=rsl[1][0], op0=ALU.mult, op1=ALU.mult)
                    P.dma("sp", CKV[j, :, tsl], dst_, reads=[kd_], writes=["CKV"])

            _phase(3)
            nblk = len(blocks)
            for bi in range(2, 6):
                if bi == 2:
                    load_blk(2)
                if bi + 1 < 6:
                    load_blk(bi + 1)
                wb_, kwb = wblk[bi % 2]
                for j in range(4):
                    chunk = (bi - 2) * 4 + j
                    st_ap, kst_ = stage[chunk % 2]
                    for tb in range(4):
                        tsl = slice(tb * 512, (tb + 1) * 512)
                        sb_ = (chunk * 4 + tb) % 3
                        for c in range(16):
                            P.op("pe", "matmul", reads=[kwb] + [f"{khT}_{tt}" for tt in range(4 * tb, 4 * tb + 4)], writes=[bk(sb_)], out=bank(sb_), lhsT=wb_[:, c, j * 128:(j + 1) * 128], rhs=hT[:, c, tsl], start=c == 0, stop=c == 15)
                        rope_tile(sb_, 128, cosd, sind, kcosd, ksind, swd, tb, st_ap[:, tsl], kst_, (chunk * 4 + tb) % 2)
                    rope_flush()
                    P.dma("sp", QKd[chunk], st_ap, reads=[kst_], writes=["QKd"])
            _phase(5)
            P.barrier()
            AR.reset()
            cq, kcq = AR.bf(4, S)
            ckv, kckv = AR.bf(2, S)
            wuq, kwuq = AR.bf(4, 1536)
            wukv, kwukv = AR.bf(2, 2048)
            tbf = [AR.bf(512) for _ in range(2)]
            stage = [AR.bf(S) for _ in range(2)]
            vstage = [AR.bf(512) for _ in range(2)]
            cosm2, kcosm2 = AR.f32(S)
            sinm2, ksinm2 = AR.f32(S)
            ropeu = [AR.f32(512) for _ in range(2)]
            ropew = [AR.f32(512) for _ in range(2)]
            xt2 = [AR.f32(D) for _ in range(2)]
            tmpf, ktmpf = AR.f32(6, 512)
            posi = xt2[0][0].bitcast(I32)
            kposi = xt2[0][1]
            posf, kposf = xt2[1]
            ua, kua = tmpf.rearrange("p a b -> p (a b)")[:, 0:S], ktmpf
            P.dma("sp", posi, pos.partition_broadcast(128), writes=[kposi])
            P.op("dve", "tensor_copy", reads=[kposi], writes=[kposf], out=posf, in_=posi)
            make_tables(cosm2, kcosm2, sinm2, ksinm2, 1, 3)

            P.dma("sp", cq, CQ.rearrange("c p t -> p c t"), reads=["CQ"], writes=[kcq])
            P.dma("sp", ckv, CKV.rearrange("c p t -> p c t"), reads=["CKV"], writes=[kckv])
            load_w(wuq, kwuq, wview(w_uq, 0, 4, 0, 1536))
            load_w(wukv, kwukv, wview(w_ukv, 0, 2, 0, 2048))
            cnt = [0]

            def fm_proj(wt, kwt, ncc, col0, src, ksrc, tb, b_):
                tsl = slice(tb * 512, (tb + 1) * 512)
                for c in range(ncc):
                    P.op("pe", "matmul", reads=[kwt, ksrc], writes=[bk(b_)], out=bank(b_), lhsT=wt[:, c, col0:col0 + 128], rhs=src[:, c, tsl], start=c == 0, stop=c == ncc - 1)

            evn = [0]

            def evac(b_, dst, kdst):
                evn[0] += 1
                if evn[0] % 2:
                    P.op("act", "activation", reads=[bk(b_)], writes=[kdst], out=dst, in_=bank(b_), func=AF.Copy)
                else:
                    P.op("dve", "tensor_copy", reads=[bk(b_)], writes=[kdst], out=dst, in_=bank(b_))

            for h in range(8):
                st_ap, kst_ = stage[cnt[0] % 2]
                for tb in range(4):
                    b_ = (cnt[0] * 4 + tb) % 3
                    fm_proj(wuq, kwuq, 4, h * 128, cq, kcq, tb, b_)
                    evac(b_, st_ap[:, tb * 512:(tb + 1) * 512], kst_)
                P.dma("sp", QN[h], st_ap, reads=[kst_], writes=["QN"])
                cnt[0] += 1
            for i in range(4):
                st_ap, kst_ = stage[cnt[0] % 2]
                for tb in range(4):
                    b_ = (cnt[0] * 4 + tb) % 3
                    fm_proj(wuq, kwuq, 4, 1024 + i * 128, cq, kcq, tb, b_)
                    rope_tile(b_, 128, cosm2, sinm2, kcosm2, ksinm2, swm, tb, st_ap[:, tb * 512:(tb + 1) * 512], kst_, tb % 2)
                rope_flush()
                P.dma("sp", QR[i], st_ap, reads=[kst_], writes=["QR"])
                cnt[0] += 1
            for h in range(8):
                st_ap, kst_ = stage[cnt[0] % 2]
                for tb in range(4):
                    b_ = (cnt[0] * 4 + tb) % 3
                    fm_proj(wukv, kwukv, 2, h * 128, ckv, kckv, tb, b_)
                    evac(b_, st_ap[:, tb * 512:(tb + 1) * 512], kst_)
                P.dma("sp", KN[h], st_ap, reads=[kst_], writes=["KN"])
                cnt[0] += 1
            for t in range(NT):
                for half in range(2):
                    b_ = (t * 2 + half) % 3
                    for c in range(2):
                        P.op("pe", "matmul", reads=[kwukv, kckv], writes=[bk(b_)], out=bank(b_), lhsT=ckv[:, c, t * 128:(t + 1) * 128], rhs=wukv[:, c, 1024 + half * 512:1024 + (half + 1) * 512], start=c == 0, stop=c == 1)
                    vs_ap, kvs = vstage[(t * 2 + half) % 2]
                    evac(b_, vs_ap, kvs)
                    P.dma("sp", Vm[half * 4:half * 4 + 4, :, t, :].rearrange("h p d -> p h d"), vs_ap.rearrange("p (h d) -> p h d", h=4),
                          reads=[kvs], writes=["Vm"])

            attn_pend = []

            def attn_flush(kp=99):
                keep = []
                for st, fn in list(attn_pend):
                    if st <= kp:
                        fn()
                    else:
                        keep.append((st, fn))
                attn_pend[:] = keep

            def attention(qv, kq, kv_, kk, nkt, vfun, kvv, ndc, softmaxes, scale, q0, tail, PT):
                nkp = nkt // 2
                for si, chunks in enumerate(softmaxes):
                    def S_(kp):
                        for kt in (2 * kp, 2 * kp + 1):
                            b_ = (kp % 2) * 2 + (kt % 2)
                            for ci, (ch, K) in enumerate(chunks):
                                P.op("pe", "matmul", reads=[kk, kq], writes=[bk(b_)], out=bank(b_), lhsT=kv_[0:K, ch, kt * 128:(kt + 1) * 128], rhs=qv[0:K, ch, q0:q0 + 512], start=ci == 0, stop=ci == len(chunks) - 1)

                    def E_(kp):
                        pt_ap, kpt = PT[kp % 3]
                        b0 = (kp % 2) * 2
                        src = psA[:, b0 * 512:(b0 + 2) * 512]
                        P.op("act", "activation", reads=[bk(b0), bk(b0 + 1)], writes=[kpt], out=pt_ap, in_=src, func=AF.Exp, scale=scale)

                    def PV_(kp):
                        pt_ap, kpt = PT[kp % 3]
                        for kt in (2 * kp, 2 * kp + 1):
                            rhs = pt_ap[:, (kt % 2) * 512:(kt % 2 + 1) * 512]
                            for dc in range(ndc):
                                P.op("pe", "matmul", reads=[kvv, kpt], writes=[bk(4 + dc)], out=bank(4 + dc), lhsT=vfun(kt, dc), rhs=rhs, start=kt == 0, stop=kt == nkt - 1)
                            P.op("pe", "matmul", reads=["cb", kpt], writes=[bk(6)], out=bank(6), lhsT=ones, rhs=rhs, start=kt == 0, stop=kt == nkt - 1)

                    S_(0)
                    E_(0)
                    for kp in range(1, nkp):
                        S_(kp)
                        E_(kp)
                        PV_(kp - 1)
                        attn_flush(kp)
                    PV_(nkp - 1)
                    tail(si)

            _phase(6)
            P.barrier()
            AR.reset()
            qb2 = [AR.bf(2, S) for _ in range(2)]
            kb2 = [AR.bf(2, S) for _ in range(2)]
            vb2 = [AR.bf(16, 256) for _ in range(2)]
            PT = [AR.bf(1024) for _ in range(3)]
            ostg = [AR.bf(2, 512) for _ in range(2)]
            sqd = [AR.bf(2, 512) for _ in range(2)]
            rden = [AR.f32(512) for _ in range(2)]
            o1n = [AR.f32(2, 512) for _ in range(2)]
            ot = [AR.f32(2, 512) for _ in range(2)]
            lnd = [AR.f32(512) for _ in range(2)]
            rsd = [AR.f32(512) for _ in range(2)]
            wo, kwo = AR.bf(16, D)
            for g in range(4):
                load_w(wo[:, 4 * g:4 * g + 4, :], f"{kwo}_{g}", wview(w_out, 4 * g * 128, 4, 0, D))
            kwo_all = [f"{kwo}_{g}" for g in range(4)]

            jobs = [("d", h) for h in range(4)] + [("m", h) for h in range(8)]

            def load_job(ji):
                kind, h = jobs[ji]
                (q_, kq_), (k_, kk_), (v_, kv2) = qb2[ji % 2], kb2[ji % 2], vb2[ji % 2]
                if kind == "d":
                    for j in range(2):
                        P.dma("sp", q_[:, j, :], QKd[2 * h + j], reads=["QKd"], writes=[kq_])
                        P.dma("sp", k_[:, j, :], QKd[8 + 2 * h + j], reads=["QKd"], writes=[kk_])
                    P.dma("sp", v_, Vd[h], reads=["Vd"], writes=[kv2])
                else:
                    P.dma("sp", q_[:, 0, :], QN[h], reads=["QN"], writes=[kq_])
                    P.dma("sp", q_[0:64, 1, :], QR[h // 2, (h % 2) * 64:(h % 2) * 64 + 64, :], reads=["QR"], writes=[kq_])
                    P.dma("sp", k_[:, 0, :], KN[h], reads=["KN"], writes=[kk_])
                    P.dma("sp", k_[0:64, 1, :], KR, reads=["KR"], writes=[kk_])
                    P.dma("sp", v_[:, :, 0:128], Vm[h], reads=["Vm"], writes=[kv2])

            tcount = [0]
            load_job(0)
            for ji, (kind, h) in enumerate(jobs):
                if ji + 1 < len(jobs):
                    load_job(ji + 1)
                (q_, kq_), (k_, kk_), (v_, kv2) = qb2[ji % 2], kb2[ji % 2], vb2[ji % 2]
                for qi in range(4):
                    q0 = qi * 512
                    if kind == "m":
                        def tail_m(si, h=h, q0=q0):
                            s_ = tcount[0] % 2
                            tcount[0] += 1
                            rd, krd = rden[s_]
                            o2, ko2 = ot[s_]
                            og, kog = ostg[s_]
                            P.op("act", "activation", reads=[bk(6)], writes=[krd], out=rd, in_=bank(6), func=AF.Copy)
                            P.op("dve", "tensor_copy", reads=[bk(4)], writes=[ko2], out=o2[:, 0, :], in_=bank(4))

                            def later():
                                P.op("dve", "reciprocal", reads=[krd], writes=[krd], out=rd, in_=rd)
                                P.op("dve", "tensor_tensor", reads=[ko2, krd], writes=[kog], out=og[:, 0, :], in0=o2[:, 0, :], in1=rd, op=ALU.mult)
                                P.dma("sp", MIX[h, :, q0:q0 + 512], og[:, 0, :], reads=[kog], writes=["MIX"])
                            attn_pend.append((2, later))
                        attention(q_, kq_, k_, kk_, 16, lambda kt, dc, v_=v_: v_[:, kt, 0:128], kv2, 1,
                                  [[(0, 128), (1, 64)]], 192.0 ** -0.5, q0, tail_m, PT)
                    else:
                        s_ = tcount[0] % 2
                        tcount[0] += 1

                        def tail_d(si, h=h, q0=q0, s_=s_):
                            rd, krd = rden[si]
                            o1, ko1 = o1n[s_]
                            o2, ko2 = ot[s_]
                            dst, kdst = (o1, ko1) if si == 0 else (o2, ko2)
                            P.op("act", "activation", reads=[bk(6)], writes=[krd], out=rd, in_=bank(6), func=AF.Copy)
                            P.op("dve", "tensor_copy", reads=[bk(4)], writes=[kdst], out=dst[:, 0, :], in_=bank(4))
                            P.op("dve", "tensor_copy", reads=[bk(5)], writes=[kdst], out=dst[:, 1, :], in_=bank(5))

                            def norm_part():
                                P.op("dve", "reciprocal", reads=[krd], writes=[krd], out=rd, in_=rd)
                                for dc in range(2):
                                    P.op("dve", "tensor_tensor", reads=[kdst, krd], writes=[kdst], out=dst[:, dc, :], in0=dst[:, dc, :], in1=rd, op=ALU.mult)
                                if si == 1:
                                    for dc in range(2):
                                        P.op("dve", "scalar_tensor_tensor", reads=[ko2, ko1, "sm_neglam"], writes=[ko2], out=o2[:, dc, :], in0=o2[:, dc, :], scalar=sm[:, NEGLAM:NEGLAM + 1], in1=o1[:, dc, :], op0=ALU.mult, op1=ALU.add)
                            attn_pend.append((2, norm_part))
                            if si == 0:
                                return
                            sq_, ksq_ = sqd[s_]
                            og, kog = ostg[s_]
                            ln_, kln_ = lnd[s_]
                            rs_, krs_ = rsd[s_]

                            def sq_part():
                                for dc in range(2):
                                    P.op("act", "activation", reads=[ko2], writes=[ksq_], out=sq_[:, dc, :], in_=o2[:, dc, :], func=AF.Square)
                                for dc in range(2):
                                    P.op("pe", "matmul", reads=[ksq_, "cb"], writes=[bk(7)], out=bank(7), lhsT=ones, rhs=sq_[:, dc, :], start=dc == 0, stop=dc == 1)

                            def fin_part():
                                P.op("act", "activation", reads=[bk(7), "sm_eps5"], writes=[kln_], out=ln_, in_=bank(7), func=AF.Ln, bias=sm[:, EPS5:EPS5 + 1], scale=1.0 / 256)
                                P.op("act", "activation", reads=[kln_], writes=[krs_], out=rs_, in_=ln_, func=AF.Exp, scale=-0.5)
                                for dc in range(2):
                                    P.op("dve", "scalar_tensor_tensor", reads=[ko2, krs_, "sm_gs"], writes=[kog], out=og[:, dc, :], in0=o2[:, dc, :], scalar=sm[:, GS0 + dc:GS0 + dc + 1], in1=rs_, op0=ALU.mult, op1=ALU.mult)
                                P.dma("sp", MIX[8 + 2 * h:8 + 2 * h + 2, :, q0:q0 + 512].rearrange("c p t -> p c t"), og, reads=[kog], writes=["MIX"])
                            attn_pend.append((4, sq_part))
                            attn_pend.append((6, fin_part))
                        attention(q_, kq_, k_, kk_, 16, lambda kt, dc, v_=v_: v_[:, kt, dc * 128:(dc + 1) * 128], kv2, 2,
                                  [[(0, 128)], [(1, 128)]], 128.0 ** -0.5, q0, tail_d, PT)

            attn_flush()
            _phase(7)
            P.barrier()
            AR.reset()
            mixT, kmix = AR.bf(16, S)
            xt2 = [AR.f32(D) for _ in range(2)]
            for g in range(4):
                P.dma("sp", mixT[:, 4 * g:4 * g + 4, :], MIX[4 * g:4 * g + 4].rearrange("c p t -> p c t"), reads=["MIX"], writes=[f"{kmix}_{g}"])
            kmix_all = [f"{kmix}_{g}" for g in range(4)]
            P.dma("sp", xt2[0][0], x[0:128, :], writes=[xt2[0][1]])
            for t in range(NT):
                if t + 1 < NT:
                    P.dma("sp", xt2[(t + 1) % 2][0], x[(t + 1) * 128:(t + 2) * 128, :], writes=[xt2[(t + 1) % 2][1]])
                pst, pk = banks4(t % 2)
                for n in range(4):
                    for c in range(16):
                        P.op("pe", "matmul", reads=[kmix_all[c // 4], kwo_all[c // 4]], writes=[pk[n]], out=pst[:, n * 512:(n + 1) * 512], lhsT=mixT[:, c, t * 128:(t + 1) * 128], rhs=wo[:, c, n * 512:(n + 1) * 512], start=c == 0, stop=c == 15)
                xs, kxs = xt2[t % 2]
                P.op("dve", "tensor_tensor", reads=pk + [kxs], writes=[kxs], out=xs, in0=pst[:, :], in1=xs, op=ALU.add)
                P.dma("sp", X1[t * 128:(t + 1) * 128, :], xs, reads=[kxs], writes=["X1"])

            _phase(8)
            P.barrier()
            AR.reset()
            wxa, kwxa = AR.bf(16, 512)
            wxo, kwxo = AR.bf(4, D)
            xkT, kxk = AR.bf(4, MEM)
            xv, kxv = AR.bf(2, 512)
            hxT, khx = AR.bf(16, 512)
            xqT, kxq = AR.bf(4, 512)
            xoT, kxo = AR.bf(4, 512)
            hfs, khfs = AR.bf(16, 512)
            _pt = AR.bf(1024)
            PT = [_pt, _pt, _pt]
            xnb4 = [AR.bf(D) for _ in range(4)]
            xnb2 = xnb4[0:2]
            x1b2 = [AR.f32(4, D) for _ in range(2)]
            rden4 = [AR.f32(512) for _ in range(4)]
            xof = [AR.f32(512) for _ in range(4)]
            xt2 = [(x1b2[1][0][:, 0, :], x1b2[1][1] + "_0")]
            mark_p2 = AR.off
            hmT, khm = AR.bf(16, MEM)
            wxb, kwxb = AR.bf(16, 512)

            load_w(wxa, kwxa, wview(w_xk, 0, 16, 0, 512))
            load_w(wxb, kwxb, wview(w_xv, 0, 16, 0, 512))
            load_w(wxo, kwxo, wview(w_xo, 0, 4, 0, D))
            for mt in range(2):
                xs, kxs = xt2[0]
                P.dma("sp", xs, mem[mt * 128:(mt + 1) * 128, :], writes=[kxs])
                xn, kxn = xnb2[mt % 2]
                norm_to_T(xs, kxs, xn, kxn, C_GMEM, hmT[:, :, mt * 128:(mt + 1) * 128], khm)
            for h in range(4):
                for c in range(16):
                    P.op("pe", "matmul", reads=[kwxa, khm], writes=[bk(h % 2)], out=bank(h % 2)[:, 0:MEM], lhsT=wxa[:, c, h * 128:(h + 1) * 128], rhs=hmT[:, c, :], start=c == 0, stop=c == 15)
                P.op("act", "activation", reads=[bk(h % 2)], writes=[kxk], out=xkT[:, h, :], in_=bank(h % 2)[:, 0:MEM], func=AF.Copy)
            for mt in range(2):
                for c in range(16):
                    P.op("pe", "matmul", reads=[kwxb, khm], writes=[bk(2 + mt)], out=bank(2 + mt), lhsT=hmT[:, c, mt * 128:(mt + 1) * 128], rhs=wxb[:, c, :], start=c == 0, stop=c == 15)
                P.op("act", "activation", reads=[bk(2 + mt)], writes=[kxv], out=xv[:, mt, :], in_=bank(2 + mt), func=AF.Copy)
            load_w(wxa, kwxa, wview(w_xq, 0, 16, 0, 512))

            P.barrier()
            AR.off = mark_p2
            xnbh = [AR.bf(D) for _ in range(3)]
            XN_ENG = "dve"

            def A1(tb):
                xb, kxb = x1b2[tb % 2]
                for i in range(4):
                    t = tb * 4 + i
                    P.dma("sp", xb[:, i, :], X1[t * 128:(t + 1) * 128, :], reads=["X1"], writes=[f"{kxb}_{i}"])
                for i in range(4):
                    norm_s1(xb[:, i, :], f"{kxb}_{i}", xnb4[i][0], xnb4[i][1])

            def A2(tb, i, b0=None):
                norm_s2(xnb4[i][0], xnb4[i][1], C_GX, hxT[:, :, i * 128:(i + 1) * 128], khx, b0=b0)

            def XQ(h):
                for c in range(16):
                    P.op("pe", "matmul", reads=[kwxa, khx], writes=[bk(h % 2)], out=bank(h % 2), lhsT=wxa[:, c, h * 128:(h + 1) * 128], rhs=hxT[:, c, :], start=c == 0, stop=c == 15)
                P.op("act", "activation", reads=[bk(h % 2)], writes=[kxq], out=xqT[:, h, :], in_=bank(h % 2), func=AF.Copy)

            A1(0)
            for i in range(4):
                A2(0, i)
            for h in range(4):
                XQ(h)
            for tb in range(4):
                xb, kxb = x1b2[tb % 2]
                for h in range(4):
                    def tail_x(si, h=h):
                        rd, krd = rden4[h]
                        xo_, kxo_ = xof[h]
                        P.op("act", "activation", reads=[bk(6)], writes=[krd], out=rd, in_=bank(6), func=AF.Copy)
                        P.op("dve", "tensor_copy", reads=[bk(4)], writes=[kxo_], out=xo_, in_=bank(4))

                        def later(h=h, rd=rd, krd=krd, xo_=xo_, kxo_=kxo_):
                            P.op("act", "activation", reads=[krd], writes=[krd], out=rd, in_=rd, func=AF.Ln)
                            P.op("act", "activation", reads=[krd], writes=[krd], out=rd, in_=rd, func=AF.Exp, scale=-1.0)
                            P.op("dve", "tensor_tensor", reads=[kxo_, krd], writes=[f"{kxo}_{h}"], out=xoT[:, h, :], in0=xo_, in1=rd, op=ALU.mult)
                        attn_pend.append((99, later))
                    attention(xqT, kxq, xkT, kxk, 2, lambda kt, dc, h=h: xv[:, kt, h * 128:(h + 1) * 128], kxv, 1,
                              [[(h, 128)]], 128.0 ** -0.5, 0, tail_x, PT)
                attn_flush()
                if tb + 1 < 4:
                    A1(tb + 1)
                for i in range(4):
                    t = tb * 4 + i
                    pst, pk = banks4(i % 2)
                    for n in range(4):
                        for h in range(4):
                            P.op("pe", "matmul", reads=[f"{kxo}_{h}", kwxo], writes=[pk[n]], out=pst[:, n * 512:(n + 1) * 512], lhsT=xoT[:, h, i * 128:(i + 1) * 128], rhs=wxo[:, h, n * 512:(n + 1) * 512], start=h == 0, stop=h == 3)
                    if tb + 1 < 4:
                        A2(tb + 1, i, b0=4 * ((i + 1) % 2))
                    xs = xb[:, i, :]
                    kxs = f"{kxb}_{i}"
                    P.op("dve", "tensor_tensor", reads=pk + [kxs], writes=[kxs], out=xs, in0=pst[:, :], in1=xs, op=ALU.add)
                    P.dma("sp", X2[t * 128:(t + 1) * 128, :], xs, reads=[kxs], writes=["X2"])
                    norm_s1(xs, kxs, xnbh[i % 3][0], xnbh[i % 3][1], eng=XN_ENG)
                    if i >= 2:
                        j = i - 2
                        norm_s2(xnbh[j % 3][0], xnbh[j % 3][1], C_GFFN, hfs[:, :, j * 128:(j + 1) * 128], khfs, b0=4 * (i % 2) + 2)
                if tb + 1 < 4:
                    XQ(0)
                    XQ(1)
                norm_s2(xnbh[2][0], xnbh[2][1], C_GFFN, hfs[:, :, 2 * 128:3 * 128], khfs)
                if tb + 1 < 4:
                    XQ(2)
                norm_s2(xnbh[0][0], xnbh[0][1], C_GFFN, hfs[:, :, 3 * 128:4 * 128], khfs)
                if tb + 1 < 4:
                    XQ(3)
                P.dma("sp", HF[:, :, tb * 512:(tb + 1) * 512].rearrange("c p t -> p c t"), hfs, reads=[khfs], writes=["HF"])

            _phase(9)
            def final_norm(t, xs, kxs, junk, kjunk, gfin, kgf):
                P.dma("sp", xs, X2[t * 128:(t + 1) * 128, :], reads=["X2"], writes=[kxs])
                r, kr = rstd_cols(xs, junk, D, EPS6, [kxs], [kjunk])
                P.op("dve", "scalar_tensor_tensor", reads=[kxs, kr, kgf], writes=[kxs], out=xs, in0=xs, scalar=r, in1=gfin, op0=ALU.mult, op1=ALU.mult)
                P.dma("sp", y[t * 128:(t + 1) * 128, :], xs, reads=[kxs], writes=["y"])

            TB = 1024
            for ps_ in range(2):
                P.barrier()
                AR.reset()
                actT, kact = AR.bf(NFC, TB)
                fn_x = [AR.f32(D) for _ in range(2)]
                gfin, kgf = AR.f32(D)
                fn_junk = [AR.bf(D) for _ in range(1)]
                xblk = [AR.f32(512) for _ in range(4)]
                mark = AR.off
                hf, khf = AR.bf(16, TB)
                wg = [AR.bf(16, 256) for _ in range(2)]
                wu = [AR.bf(16, 256) for _ in range(2)]
                sgb = [AR.bf(512) for _ in range(2)]
                P.dma("sp", gfin, gfin_d.partition_broadcast(128), writes=[kgf])
                P.dma("sp", hf, HF[:, :, ps_ * TB:(ps_ + 1) * TB].rearrange("c p t -> p c t"), reads=["HF"], writes=[khf])

                def load_gu(fb):
                    load_w(wg[fb % 2][0], wg[fb % 2][1], wview(w_gate, 0, 16, fb * 256, 256))
                    load_w(wu[fb % 2][0], wu[fb % 2][1], wview(w_up, 0, 16, fb * 256, 256))
                load_gu(0)
                pend_norm = list(range((ps_ - 1) * 8, ps_ * 8)) if ps_ > 0 else []
                for fb in range(22):
                    if fb + 1 < 22:
                        load_gu(fb + 1)
                    (wg_, kwg), (wu_, kwu) = wg[fb % 2], wu[fb % 2]
                    for fc in range(2):
                        f = fb * 2 + fc
                        for tk in range(2):
                            it = (f * 2 + tk) % 2
                            bg, bu = 2 * it, 2 * it + 1
                            tsl = slice(tk * 512, (tk + 1) * 512)
                            for c in range(16):
                                P.op("pe", "matmul", reads=[kwg, khf], writes=[bk(bg)], out=bank(bg), lhsT=wg_[:, c, fc * 128:(fc + 1) * 128], rhs=hf[:, c, tsl], start=c == 0, stop=c == 15)
                            for c in range(16):
                                P.op("pe", "matmul", reads=[kwu, khf], writes=[bk(bu)], out=bank(bu), lhsT=wu_[:, c, fc * 128:(fc + 1) * 128], rhs=hf[:, c, tsl], start=c == 0, stop=c == 15)
                            sg_, ksg = sgb[it]
                            P.op("act", "activation", reads=[bk(bg)], writes=[ksg], out=sg_, in_=bank(bg), func=AF.Silu)
                            P.op("dve", "tensor_tensor", reads=[bk(bu), ksg], writes=[f"{kact}_{f}"], out=actT[:, f, tsl], in0=bank(bu), in1=sg_, op=ALU.mult)
                    if pend_norm and fb % 2 == 1:
                        t = pend_norm.pop(0)
                        final_norm(t, fn_x[t % 2][0], fn_x[t % 2][1], fn_junk[0][0], fn_junk[0][1], gfin, kgf)
                P.barrier()
                AR.off = mark
                wd = [AR.bf(11, 512) for _ in range(6)]

                wd_next = [0]

                def load_wd_upto(gmax):
                    while wd_next[0] < min(gmax, 16):
                        gi = wd_next[0]
                        n_, g = gi // 4, gi % 4
                        d_, kd_ = wd[gi % 6]
                        load_w(d_, kd_, wview(w_down, g * 11 * 128, 11, n_ * 512, 512))
                        wd_next[0] += 1
                kact_all = [f"{kact}_{f}" for f in range(NFC)]
                for n in range(4):
                    load_wd_upto(n * 4 + 6)
                    for t8 in range(8):
                        t = ps_ * 8 + t8
                        b_ = t8 % 4
                        xb_, kxb = xblk[t8 % 4]
                        P.dma("sp", xb_, X2[t * 128:(t + 1) * 128, n * 512:(n + 1) * 512], reads=["X2"], writes=[kxb])
                        for f in range(NFC):
                            d_, kd_ = wd[(n * 4 + f // 11) % 6]
                            P.op("pe", "matmul", reads=[kact_all[f], kd_], writes=[bk(b_)], out=bank(b_), lhsT=actT[:, f, t8 * 128:(t8 + 1) * 128], rhs=d_[:, f % 11, :], start=f == 0, stop=f == NFC - 1)
                        P.op("dve", "tensor_tensor", reads=[bk(b_), kxb], writes=[kxb], out=xb_, in0=bank(b_), in1=xb_, op=ALU.add)
                        P.dma("sp", X2[t * 128:(t + 1) * 128, n * 512:(n + 1) * 512], xb_, reads=[kxb], writes=["X2"])
                        if ps_ == 1 and n == 3:
                            final_norm(t, fn_x[t % 2][0], fn_x[t % 2][1], fn_junk[0][0], fn_junk[0][1], gfin, kgf)


        except _Stop:
            pass

        need = P.finalize()
        sems = {"eng": {}, "dma": {}}
        for e in ENGS:
            sems["eng"][e] = [nc.alloc_semaphore(name=f"s_{e}_{i}") for i in range(need[e])]
        for q in ("sp", "pool"):
            sems["dma"][q] = [nc.alloc_semaphore(name=f"d_{q}_{i}") for i in range(NDMASEM)]
        with nc.Block() as block:
            @block.tensor
            def _(e):
                P.run("pe", e, sems)

            @block.scalar
            def _(e):
                P.run("act", e, sems)

            @block.vector
            def _(e):
                P.run("dve", e, sems)

            @block.gpsimd
            def _(e):
                P.run("pool", e, sems)

            @block.sync
            def _(e):
                P.run("sp", e, sems)
    return nc


_NC = [None]


def _host_consts():
    p = np.arange(128)
    cf = np.zeros((128, 4), np.float32)
    cf[:, 0] = (10000.0 ** (-(2.0 * (p % 64)) / 128.0)) / (2 * math.pi)
    cf[:, 1] = (10000.0 ** (-(2.0 * (p % 32)) / 64.0)) / (2 * math.pi)
    cf[:, 2] = np.where(p < 64, -1.0, 1.0)
    cf[:, 3] = np.where((p % 64) < 32, -1.0, 1.0)
    cb = np.zeros((128, 512), np.float32)
    cb[:, 0:128] = np.eye(128)
    cb[:, 128:256] = 1.0
    m = np.arange(128)
    cb[(m + 64) % 128, 256 + m] = 1.0
    src = np.where((m % 64) < 32, m + 32, m - 32)
    cb[src, 384 + m] = 1.0
    return cf, cb


def kernel(x, mem, positions, g_mix, w_in, g_q_lat, w_uq, g_kv_lat, w_ukv,
           lambda_q1, lambda_k1, lambda_q2, lambda_k2, g_diff_sub, w_out,
           g_xattn, g_mem, w_xq, w_xk, w_xv, w_xo,
           g_ffn, w_gate, w_up, w_down, g_final):
    f32 = lambda a: np.ascontiguousarray(np.asarray(a), dtype=np.float32)
    if _NC[0] is None:
        _NC[0] = build_program()
    nc = _NC[0]
    cf, cb = _host_consts()

    def colT(g, n):
        return f32(g).reshape(n, 128).T

    cols = np.concatenate([colT(g_mix[0], 16), colT(g_xattn[0], 16), colT(g_mem[0], 16), colT(g_ffn[0], 16),
                           colT(g_q_lat[0], 4), colT(g_kv_lat[0], 2), colT(g_diff_sub[0], 2)], axis=1)
    cols = np.ascontiguousarray(cols, dtype=np.float32)
    lamv = np.ascontiguousarray(np.stack([f32(lambda_q1[0]), f32(lambda_k1[0]), f32(lambda_q2[0]), f32(lambda_k2[0])]))
    wuq = f32(w_uq[0]).reshape(512, 8, 192)
    wuq_p = np.ascontiguousarray(np.concatenate([wuq[:, :, :128].reshape(512, 1024), wuq[:, :, 128:].reshape(512, 512)], axis=1))
    wukv = f32(w_ukv[0]).reshape(256, 8, 256)
    wukv_p = np.ascontiguousarray(np.concatenate([wukv[:, :, :128].reshape(256, 1024), wukv[:, :, 128:].reshape(256, 1024)], axis=1))
    shared = {
        "w_in": f32(w_in[0]), "w_uq": wuq_p, "w_ukv": wukv_p, "w_out": f32(w_out[0]),
        "w_xq": f32(w_xq[0]), "w_xk": f32(w_xk[0]), "w_xv": f32(w_xv[0]), "w_xo": f32(w_xo[0]),
        "w_gate": f32(w_gate[0]), "w_up": f32(w_up[0]), "w_down": f32(w_down[0]),
        "cols": cols, "lamv": lamv, "gfin": f32(g_final).reshape(1, D), "cf": cf, "cb": cb,
    }
    xs = f32(x)
    ms = f32(mem)
    ps = np.ascontiguousarray(np.asarray(positions), dtype=np.int32)
    in_maps = []
    for b in range(8):
        d = dict(shared)
        d["x"] = xs[b]
        d["mem"] = ms[b]
        d["pos"] = ps[b:b + 1]
        in_maps.append(d)
    res = run_bass_kernel_spmd(nc, in_maps, core_ids=list(range(8)))
    return np.stack([np.asarray(r["y"], dtype=np.float32) for r in res.results], axis=0)
```

```python
import math
import contextlib
import numpy as np
import concourse.bass as bass
import concourse.mybir as mybir
from concourse.bass_utils import run_bass_kernel_spmd

F32 = mybir.dt.float32
BF16 = mybir.dt.bfloat16
I32 = mybir.dt.int32
AF = mybir.ActivationFunctionType
ALU = mybir.AluOpType
AX = mybir.AxisListType

S = 2048
D = 2048
NT = 16
NC16 = 16
FF = 5632
NFC = 44
MEM = 256
IN_W = 3904
EPS = 1e-6

ENGS = ["pe", "act", "dve", "pool", "sp"]
EPOCH = 2000
NDMASEM = 16


class Prog:
    def __init__(self):
        self.ops = {e: [] for e in ENGS}
        self.res = {}
        self.dma_n = {"sp": 0, "pool": 0}

    @staticmethod
    def _joint(r):
        return r is not None and not r[1] and r[0] and all(t[0] == "d" for t in r[0]) and len(r[0]) < 48

    def _deps(self, reads, writes, is_dma=False):
        toks = []
        for k in reads:
            r = self.res.get(k)
            if r:
                toks.extend(r[0])
        for k in writes:
            r = self.res.get(k)
            if r:
                if is_dma and self._joint(r):
                    toks.extend(r[2])
                else:
                    toks.extend(r[0])
                    toks.extend(r[1].values())
        return toks

    def _commit(self, tok, reads, writes):
        for k in reads:
            if k in writes:
                continue
            r = self.res.setdefault(k, [[], {}, []])
            if tok[0] == "e":
                r[1][tok[1]] = tok
            else:
                r[1][(tok[1], tok[2] % NDMASEM)] = tok
        for k in writes:
            r = self.res.get(k)
            if tok[0] == "d" and self._joint(r):
                r[0].append(tok)
            else:
                inherited = (list(r[0]) + list(r[1].values())) if (r and tok[0] == "d") else []
                self.res[k] = [[tok], {}, inherited]

    def op(self, eng, meth, reads=(), writes=(), **kw):
        if eng != "pe":
            psr = [k for k in reads if k.startswith("ps") and k not in writes]
            if psr:
                writes = list(writes) + psr
        toks = self._deps(reads, writes)
        idx = len(self.ops[eng])
        self.ops[eng].append({"waits": toks, "fn": (lambda e, m=meth, kw=kw: getattr(e, m)(**kw)), "sig": False})
        self._commit(("e", eng, idx), reads, writes)

    def dma(self, q, out, in_, reads=(), writes=()):
        toks = self._deps(reads, writes, is_dma=True)
        k = self.dma_n[q]
        self.dma_n[q] += 1
        if k >= NDMASEM:
            toks.append(("d", q, k - NDMASEM))
        self.ops[q].append({"waits": toks, "fn": (lambda e, o=out, i=in_: e.dma_start(out=o, in_=i)),
                            "sig": False, "dma": k})
        self._commit(("d", q, k), reads, writes)

    def barrier(self):
        toks = []
        for e in ENGS:
            for i in range(len(self.ops[e]) - 1, -1, -1):
                o = self.ops[e][i]
                if o["fn"] is not None and "dma" not in o:
                    toks.append(("e", e, i))
                    break
        for q in ("sp", "pool"):
            n = self.dma_n[q]
            for k in range(max(0, n - NDMASEM), n):
                toks.append(("d", q, k))
        for e in ENGS:
            self.ops[e].append({"waits": list(toks), "fn": None, "sig": False})

    def finalize(self):
        for e in ENGS:
            for o in self.ops[e]:
                for t in o["waits"]:
                    if t[0] == "e" and not (t[1] == e and e in ("pe", "sp")):
                        self.ops[t[1]][t[2]]["sig"] = True
        self.cnt = {}
        for e in ENGS:
            c = 0
            for o in self.ops[e]:
                if o["sig"]:
                    c += 1
                    o["cnt"] = c
            self.cnt[e] = c
        return {e: max(1, (self.cnt[e] + EPOCH - 1) // EPOCH) for e in ENGS}

    def run(self, e, handle, sems):
        seen_e = {}
        seen_d = {}
        for o in self.ops[e]:
            for t in o["waits"]:
                if t[0] == "e":
                    _, pe_, idx = t
                    if pe_ == e and e in ("pe", "sp"):
                        continue
                    if seen_e.get(pe_, -1) >= idx:
                        continue
                    seen_e[pe_] = idx
                    c = self.ops[pe_][idx]["cnt"]
                    handle.wait_ge(sems["eng"][pe_][(c - 1) // EPOCH], (c - 1) % EPOCH + 1)
                else:
                    _, q, k = t
                    slot = k % NDMASEM
                    val = 16 * (k // NDMASEM + 1)
                    if seen_d.get((q, slot), 0) >= val:
                        continue
                    seen_d[(q, slot)] = val
                    handle.wait_ge(sems["dma"][q][slot], val)
            if o["fn"] is None:
                continue
            inst = o["fn"](handle)
            if "dma" in o:
                inst.then_inc(sems["dma"][e][o["dma"] % NDMASEM], 16)
            elif o["sig"]:
                c = o["cnt"]
                inst.then_inc(sems["eng"][e][(c - 1) // EPOCH], 1)
        if e == "sp":
            for q in ("sp", "pool"):
                n = self.dma_n[q]
                for k in range(max(0, n - NDMASEM), n):
                    handle.wait_ge(sems["dma"][q][k % NDMASEM], 16 * (k // NDMASEM + 1))


class _Stop(Exception):
    pass


_STOP = [99]


def _phase(n):
    if n > _STOP[0]:
        raise _Stop()


class Arena:
    def __init__(self, ap, size):
        self.ap, self.size, self.off, self.n = ap, size, 0, 0

    def reset(self):
        self.off = 0

    def _take(self, nel):
        self.off = (self.off + 15) // 16 * 16
        v = self.ap[:, self.off:self.off + nel]
        self.off += nel
        assert self.off <= self.size, ("arena overflow", self.off, self.size)
        self.n += 1
        return v

    def bf(self, *free):
        n = int(np.prod(free))
        v = self._take(n)
        if len(free) == 2:
            v = v.rearrange("p (a b) -> p a b", a=free[0])
        return v, f"A{self.n}"

    def f32(self, *free):
        n = int(np.prod(free))
        v = self._take(2 * n).bitcast(F32)
        if len(free) == 2:
            v = v.rearrange("p (a b) -> p a b", a=free[0])
        return v, f"A{self.n}"

    def i32(self, *free):
        n = int(np.prod(free))
        v = self._take(2 * n).bitcast(I32)
        return v, f"A{self.n}"


C_GMIX, C_GX, C_GMEM, C_GFFN, C_GQ, C_GKV, C_GSUB = 0, 16, 32, 48, 64, 68, 70
NCOL = 72
ARENA_EL = 100 * 1024


def build_program():
    nc = bass.Bass("TRN2", target_bir_lowering=False)

    def din(name, shape, dt=F32):
        return nc.dram_tensor(name, list(shape), dt, kind="ExternalInput").ap()

    def dscr(name, shape, dt):
        return nc.dram_tensor(name, list(shape), dt).ap()

    x = din("x", [S, D])
    mem = din("mem", [MEM, D])
    pos = din("pos", [1, S], I32)
    w_in = din("w_in", [D, IN_W])
    w_uq = din("w_uq", [512, 1536])
    w_ukv = din("w_ukv", [256, 2048])
    w_out = din("w_out", [D, D])
    w_xq = din("w_xq", [D, 512])
    w_xk = din("w_xk", [D, 512])
    w_xv = din("w_xv", [D, 512])
    w_xo = din("w_xo", [512, D])
    w_gate = din("w_gate", [D, FF])
    w_up = din("w_up", [D, FF])
    w_down = din("w_down", [FF, D])
    cols_d = din("cols", [128, NCOL])
    lam_d = din("lamv", [4, 128])
    gfin_d = din("gfin", [1, D])
    cf_d = din("cf", [128, 4])
    cb_d = din("cb", [128, 512])
    y = nc.dram_tensor("y", [S, D], F32, kind="ExternalOutput").ap()

    QKd = dscr("s_qkd", [16, 128, S], BF16)
    Vd = dscr("s_vd", [4, 128, NT, 256], BF16)
    CQ = dscr("s_cq", [4, 128, S], BF16)
    CKV = dscr("s_ckv", [2, 128, S], BF16)
    KR = dscr("s_kr", [64, S], BF16)
    QN = dscr("s_qn", [8, 128, S], BF16)
    QR = dscr("s_qr", [4, 128, S], BF16)
    KN = dscr("s_kn", [8, 128, S], BF16)
    Vm = dscr("s_vm", [8, 128, NT, 128], BF16)
    MIX = dscr("s_mix", [16, 128, S], BF16)
    X1 = dscr("s_x1", [S, D], F32)
    X2 = dscr("s_x2", [S, D], F32)
    HF = dscr("s_hf", [16, 128, S], BF16)

    P = Prog()
    es = contextlib.ExitStack()
    with es:
        def sb(name, shape, dt):
            return es.enter_context(nc.sbuf_tensor(name, list(shape), dt))

        arena_t = sb("arena", [128, ARENA_EL], BF16)
        AR = Arena(arena_t, ARENA_EL)
        cb = sb("cb_sb", [128, 512], BF16)
        cols = sb("cols_sb", [128, NCOL], F32)
        cf = sb("cf_sb", [128, 4], F32)
        sm = sb("sm", [128, 64], F32)
        lamb = sb("lamb", [128, 4, 128], F32)
        psA = es.enter_context(nc.psum_tensor("psA", [128, 2048], F32))
        psB = es.enter_context(nc.psum_tensor("psB", [128, 2048], F32))

        ident = cb[:, 0:128]
        ones = cb[:, 128:256]
        swd = cb[:, 256:384]
        swm = cb[:, 384:512]

        def bank(b):
            t = psA if b < 4 else psB
            return t[:, (b % 4) * 512:(b % 4 + 1) * 512]

        def bk(b):
            return f"ps{b}"

        def banks4(g):
            return (psA if g == 0 else psB), [f"ps{4 * g + i}" for i in range(4)]

        def bankpair_bf(b):
            t = psA if b < 4 else psB
            return t[:, (b % 4) * 512:(b % 4 + 2) * 512].bitcast(BF16)

        EPS6, EPS5, NEGLAM, GS0, GS1 = 0, 1, 2, 3, 4
        SS0 = 8
        uid = [0]

        def key(prefix):
            uid[0] += 1
            return f"{prefix}{uid[0]}"

        try:
            P.dma("pool", cb[:], cb_d, writes=["cb"])
            P.dma("sp", cols[:], cols_d, writes=["cols"])
            P.dma("sp", cf[:], cf_d, writes=["cf"])
            for i in range(4):
                P.dma("sp", lamb[:, i, :], lam_d[i:i + 1, :].partition_broadcast(128), writes=[f"lamb{i}"])
            P.op("dve", "memset", writes=["sm_eps6", "sm_eps5", "sm_neglam", "sm_gs", "sm5", "sm6", "sm7"] + [f"sm_ss{i}" for i in range(4)], ap=sm[:], constant=0.0)
            P.op("dve", "memset", writes=["sm_eps6"], ap=sm[:, EPS6:EPS6 + 1], constant=1e-6)
            P.op("dve", "memset", writes=["sm_eps5"], ap=sm[:, EPS5:EPS5 + 1], constant=1e-5)
            P.op("dve", "tensor_tensor", reads=["lamb0", "lamb1"], writes=["lamb0"], out=lamb[:, 0, :], in0=lamb[:, 0, :], in1=lamb[:, 1, :], op=ALU.mult)
            P.op("dve", "tensor_tensor", reads=["lamb2", "lamb3"], writes=["lamb2"], out=lamb[:, 2, :], in0=lamb[:, 2, :], in1=lamb[:, 3, :], op=ALU.mult)
            P.op("dve", "reduce_sum", reads=["lamb0"], writes=["sm5"], out=sm[:, 5:6], in_=lamb[:, 0, :], axis=AX.X)
            P.op("dve", "reduce_sum", reads=["lamb2"], writes=["sm6"], out=sm[:, 6:7], in_=lamb[:, 2, :], axis=AX.X)
            P.op("act", "activation", reads=["sm5"], writes=["sm5"], out=sm[:, 5:6], in_=sm[:, 5:6], func=AF.Exp)
            P.op("act", "activation", reads=["sm6"], writes=["sm6"], out=sm[:, 6:7], in_=sm[:, 6:7], func=AF.Exp)
            P.op("dve", "tensor_tensor", reads=["sm5", "sm6"], writes=["sm7"], out=sm[:, 7:8], in0=sm[:, 6:7], in1=sm[:, 5:6], op=ALU.subtract)
            P.op("dve", "tensor_scalar", reads=["sm7"], writes=["sm_neglam"], out=sm[:, NEGLAM:NEGLAM + 1], in0=sm[:, 7:8], scalar1=-0.2, scalar2=None, op0=ALU.add)
            P.op("dve", "tensor_scalar", reads=["cols"], writes=["sm_gs"], out=sm[:, GS0:GS0 + 2], in0=cols[:, C_GSUB:C_GSUB + 2], scalar1=0.8, scalar2=None, op0=ALU.mult)

            ssn = [0]

            def rstd_cols(src_ap, junk_ap, n_feat, eps_col, rk, wk_junk):
                slot = ssn[0] % 4
                ssn[0] += 1
                c0 = SS0 + 3 * slot
                kss = f"sm_ss{slot}"
                P.op("act", "activation", reads=[kss], writes=[kss], out=sm[:, c0:c0 + 1], in_=sm[:, c0:c0 + 1], func=AF.Copy, scale=0.0)
                P.op("act", "activation", reads=rk, writes=[kss] + wk_junk, out=junk_ap, in_=src_ap, func=AF.Square, accum_out=sm[:, c0:c0 + 1])
                P.op("act", "activation", reads=[kss, "sm_eps6", "sm_eps5"], writes=[kss], out=sm[:, c0 + 1:c0 + 2], in_=sm[:, c0:c0 + 1], func=AF.Ln, bias=sm[:, eps_col:eps_col + 1], scale=1.0 / n_feat)
                P.op("act", "activation", reads=[kss], writes=[kss], out=sm[:, c0 + 2:c0 + 3], in_=sm[:, c0 + 1:c0 + 2], func=AF.Exp, scale=-0.5)
                return sm[:, c0 + 2:c0 + 3], kss

            trn = [0]

            def norm_s1(src, ksrc, xnb, kxnb, eng="dve"):
                r, kr = rstd_cols(src, xnb, D, EPS6, [ksrc], [kxnb])
                P.op(eng, "tensor_scalar", reads=[ksrc, kr], writes=[kxnb], out=xnb, in0=src, scalar1=r, scalar2=None, op0=ALU.mult)

            def norm_s2(xnb, kxnb, gcol0, dst3, kdst, b0=None):
                if b0 is None:
                    b0 = 6 if trn[0] % 2 else 4
                    trn[0] += 1
                pst = bankpair_bf(b0)
                for c in range(16):
                    P.op("pe", "transpose", reads=[kxnb, "cb"], writes=[bk(b0), bk(b0 + 1)], out=pst[:, c * 128:(c + 1) * 128], in_=xnb[:, c * 128:(c + 1) * 128], identity=ident)
                P.op("dve", "tensor_tensor", reads=[bk(b0), bk(b0 + 1), "cols"], writes=[kdst], out=dst3, in0=pst.rearrange("p (c t) -> p c t", c=16), in1=cols[:, gcol0:gcol0 + 16].unsqueeze(2).to_broadcast([128, 16, 128]), op=ALU.mult)

            def norm_to_T(src, ksrc, xnb, kxnb, gcol0, dst3, kdst):
                norm_s1(src, ksrc, xnb, kxnb)
                norm_s2(xnb, kxnb, gcol0, dst3, kdst)

            def dma_split(q, dst, src, reads, writes, grp=2):
                if len(dst.shape) == 3 and dst.shape[1] > grp:
                    for c0 in range(0, dst.shape[1], grp):
                        c1 = min(dst.shape[1], c0 + grp)
                        P.dma(q, dst[:, c0:c1, :], src[:, c0:c1, :], reads=reads, writes=writes)
                else:
                    P.dma(q, dst, src, reads=reads, writes=writes)

            def load_w(dst, kdst, src):
                dma_split("pool", dst, src, [], [kdst])

            def wview(w, r0, nr_chunks, c0, ncols):
                return w[r0:r0 + nr_chunks * 128, c0:c0 + ncols].rearrange("(c p) n -> p c n", p=128)

            AR.reset()
            hT, khT = AR.bf(16, S)
            xnb2 = [AR.bf(D) for _ in range(2)]
            wblk = [AR.bf(16, 512) for _ in range(2)]
            sqb = [AR.bf(512) for _ in range(6)]
            tbf = [AR.bf(512) for _ in range(2)]
            stage = [AR.bf(S) for _ in range(2)]
            vstage = [AR.bf(512) for _ in range(2)]
            xt2 = [AR.f32(D) for _ in range(2)]
            tmpf, ktmpf = AR.f32(6, 512)
            cosd, kcosd = AR.f32(S)
            sind, ksind = AR.f32(S)
            cosm, kcosm = AR.f32(S)
            sinm, ksinm = AR.f32(S)
            rp8, krp8 = AR.f32(8, 512)
            ropeu = [(rp8[:, j, :], f"{krp8}_{j}") for j in range(0, 2)]
            ropew = [(rp8[:, j, :], f"{krp8}_{j}") for j in range(2, 4)]
            rsl = [(rp8[:, j, :], f"{krp8}_{j}") for j in range(4, 6)]
            lnl = [(rp8[:, j, :], f"{krp8}_{j}") for j in range(6, 8)]

            rp8f = rp8.rearrange("p a b -> p (a b)")
            posf, kposf = rp8f[:, 0:S], "tab_posf"
            posi, kposi = rp8f[:, S:2 * S].bitcast(I32), "tab_posi"
            ua, kua = tmpf.rearrange("p a b -> p (a b)")[:, 0:S], ktmpf

            def tab_setup():
                P.dma("sp", posi, pos.partition_broadcast(128), writes=[kposi])
                P.op("dve", "tensor_copy", reads=[kposi], writes=[kposf], out=posf, in_=posi)

            def make_tables_gen(cdst, kc, sdst, ks, fcol, sign_col):
                P.op("dve", "tensor_scalar", reads=[kposf, "cf"], writes=[kua], out=ua, in0=posf, scalar1=cf[:, fcol:fcol + 1], scalar2=None, op0=ALU.mult)
                yield
                P.op("dve", "tensor_copy", reads=[kua], writes=[kposi], out=posi, in_=ua)
                yield
                P.op("dve", "tensor_copy", reads=[kposi], writes=[ks], out=sdst, in_=posi)
                yield
                P.op("dve", "tensor_tensor", reads=[kua, ks], writes=[kua], out=ua, in0=ua, in1=sdst, op=ALU.subtract)
                yield
                P.op("dve", "tensor_scalar", reads=[kua], writes=[ks], out=sdst, in0=ua, scalar1=0.5, scalar2=None, op0=ALU.is_gt)
                yield
                P.op("dve", "tensor_tensor", reads=[kua, ks], writes=[kua], out=ua, in0=ua, in1=sdst, op=ALU.subtract)
                yield
                P.op("dve", "tensor_scalar", reads=[kua], writes=[ks], out=sdst, in0=ua, scalar1=-0.5, scalar2=None, op0=ALU.is_lt)
                yield
                P.op("dve", "tensor_tensor", reads=[kua, ks], writes=[kua], out=ua, in0=ua, in1=sdst, op=ALU.add)
                yield
                P.op("dve", "tensor_scalar", reads=[kua], writes=[kc], out=cdst, in0=ua, scalar1=0.25, scalar2=None, op0=ALU.add)
                yield
                P.op("act", "activation", reads=[kua], writes=[ks], out=sdst, in_=ua, func=AF.Sin, scale=6.28318)
                yield
                P.op("dve", "tensor_scalar", reads=[kc], writes=[kua], out=ua, in0=cdst, scalar1=0.5, scalar2=None, op0=ALU.is_gt)
                yield
                P.op("dve", "tensor_tensor", reads=[kua, kc], writes=[kc], out=cdst, in0=cdst, in1=ua, op=ALU.subtract)
                yield
                P.op("act", "activation", reads=[kc], writes=[kc], out=cdst, in_=cdst, func=AF.Sin, scale=6.28318)
                yield
                P.op("dve", "tensor_scalar", reads=[ks, "cf"], writes=[ks], out=sdst, in0=sdst, scalar1=cf[:, sign_col:sign_col + 1], scalar2=None, op0=ALU.mult)
                yield

            def make_tables(*a):
                for _ in make_tables_gen(*a):
                    pass

            _phase(1)
            def tabgen():
                tab_setup()
                yield
                yield from make_tables_gen(cosd, kcosd, sind, ksind, 0, 2)
                yield from make_tables_gen(cosm, kcosm, sinm, ksinm, 1, 3)
            tg = tabgen()

            blocks = [(0, 512), (512, 320)] + [(832 + 512 * i, 512) for i in range(4)] + [(2880 + 512 * i, 512) for i in range(2)]

            def load_blk(i):
                c0, n = blocks[i]
                dst, kd = wblk[i % 2]
                load_w(dst[:, :, 0:n], kd, wview(w_in, 0, 16, c0, n))
                if i == 1:
                    load_w(dst[:, :, 320:384], kd, wview(w_in, 0, 16, 768, 64))

            load_blk(6)
            load_blk(7)
            def xload(t):
                P.dma("sp", xt2[t % 2][0], x[t * 128:(t + 1) * 128, :], writes=[xt2[t % 2][1]])

            def s1(t):
                norm_s1(xt2[t % 2][0], xt2[t % 2][1], xnb2[t % 2][0], xnb2[t % 2][1])

            def s2(t):
                norm_s2(xnb2[t % 2][0], xnb2[t % 2][1], C_GMIX, hT[:, :, t * 128:(t + 1) * 128], f"{khT}_{t}")

            xload(0)
            xload(1)
            s1(0)
            xload(2)
            s1(1)
            xload(3)
            s2(0)
            for t in range(NT):
                if t + 2 < NT:
                    s1(t + 2)
                    if t + 4 < NT:
                        xload(t + 4)
                if t + 1 < NT:
                    s2(t + 1)
                for bi in (6, 7):
                    wb_, kwb = wblk[bi % 2]
                    b_ = (2 * t + bi) % 3
                    for c in range(16):
                        P.op("pe", "matmul", reads=[kwb, f"{khT}_{t}"], writes=[bk(b_)], out=bank(b_), lhsT=hT[:, c, t * 128:(t + 1) * 128], rhs=wb_[:, c, 0:512], start=c == 0, stop=c == 15)
                    vs_ap, kvs = vstage[bi % 2]
                    P.op("act", "activation", reads=[bk(b_)], writes=[kvs], out=vs_ap, in_=bank(b_), func=AF.Copy)
                    h0 = 2 * (bi - 6)
                    P.dma("sp", Vd[h0:h0 + 2, :, t, :].rearrange("h p d -> p h d"), vs_ap.rearrange("p (h d) -> p h d", h=2),
                          reads=[kvs], writes=["Vd"])
                next(tg, None)
                next(tg, None)
            for _ in tg:
                pass
            P.barrier()

            _phase(2)
            rope_pend = []

            def rope_flush():
                while rope_pend:
                    rope_pend.pop(0)()

            def rope_tile(src_bank, npart, ctab, stab, kct, kst, swap, tb, dst, kdst, slot):
                tb_ap, ktb = tbf[slot]
                u_ap, ku = ropeu[slot]
                w_ap, kw = ropew[slot]
                sl = slice(tb * 512, (tb + 1) * 512)
                src = bank(src_bank)[0:npart, :]
                P.op("act", "activation", reads=[bk(src_bank)], writes=[ktb], out=tb_ap[0:npart, :], in_=src, func=AF.Copy)
                P.op("dve", "tensor_tensor", reads=[bk(src_bank), kct], writes=[ku], out=u_ap[0:npart, :], in0=src, in1=ctab[0:npart, sl], op=ALU.mult)
                rope_flush()

                def part2():
                    P.op("pe", "matmul", reads=[ktb, "cb"], writes=[bk(7)], out=bank(7)[0:npart, :], lhsT=swap[0:npart, 0:npart], rhs=tb_ap[0:npart, :], start=True, stop=True)
                    P.op("dve", "tensor_tensor", reads=[bk(7), kst], writes=[kw], out=w_ap[0:npart, :], in0=bank(7)[0:npart, :], in1=stab[0:npart, sl], op=ALU.mult)
                    P.op("dve", "tensor_tensor", reads=[ku, kw], writes=[kdst], out=dst, in0=u_ap[0:npart, :], in1=w_ap[0:npart, :], op=ALU.add)
                rope_pend.append(part2)

            load_blk(0)
            _phase(2.02)
            load_blk(1)
            _phase(2.05)
            (w0, kw0), (w1, kw1) = wblk[0], wblk[1]
            for tb in range(4):
                tsl = slice(tb * 512, (tb + 1) * 512)
                for ch in range(7):
                    wsrc, kws = (w0, kw0) if ch < 4 else (w1, kw1)
                    cofs = ch * 128 if ch < 4 else (ch - 4) * 128
                    m = 128 if ch < 6 else 64
                    if tb == 0 and ch == 1: _phase(2.06)
                    if tb == 0 and ch == 6: _phase(2.07)
                    for c in range(16):
                        P.op("pe", "matmul", reads=[kws] + [f"{khT}_{tt}" for tt in range(4 * tb, 4 * tb + 4)], writes=[bk(ch)], out=bank(ch), lhsT=wsrc[:, c, cofs:cofs + 128], rhs=hT[:, c, tsl], start=c == 0, stop=c == 15)
                if tb == 0: _phase(2.1)
                st_ap, kst_ = stage[tb % 2]
                rope_tile(6, 128, cosm, sinm, kcosm, ksinm, swm, tb, st_ap[:, 0:512], kst_, tb % 2)
                if tb == 0: _phase(2.2)
                rope_flush()
                P.dma("sp", KR[:, tsl], st_ap[0:64, 0:512], reads=[kst_], writes=["KR"])
                if tb == 0: _phase(2.3)
                for ch in range(6):
                    sq_ap, ksq = sqb[ch]
                    P.op("dve", "tensor_copy", reads=[bk(ch)], writes=[ktmpf], out=tmpf[:, ch, :], in_=bank(ch))
                    P.op("act", "activation", reads=[ktmpf], writes=[ksq], out=sq_ap, in_=tmpf[:, ch, :], func=AF.Square)
                if tb == 0: _phase(2.4)
                for ch in range(4):
                    P.op("pe", "matmul", reads=[sqb[ch][1], "cb"], writes=[bk(6)], out=bank(6), lhsT=ones, rhs=sqb[ch][0], start=ch == 0, stop=ch == 3)
                for ch in range(4, 6):
                    P.op("pe", "matmul", reads=[sqb[ch][1], "cb"], writes=[bk(7)], out=bank(7), lhsT=ones, rhs=sqb[ch][0], start=ch == 4, stop=ch == 5)
                if tb == 0: _phase(2.5)
                for j, (bnk, nf) in enumerate(((6, 512), (7, 256))):
                    ln_ap, kln = lnl[j]
                    rs_ap, krs = rsl[j]
                    P.op("act", "activation", reads=[bk(bnk), "sm_eps6"], writes=[kln], out=ln_ap, in_=bank(bnk), func=AF.Ln, bias=sm[:, EPS6:EPS6 + 1], scale=1.0 / nf)
                    P.op("act", "activation", reads=[kln], writes=[krs], out=rs_ap, in_=ln_ap, func=AF.Exp, scale=-0.5)
                if tb == 0: _phase(2.6)
                st2_ap, kst2 = stage[(tb + 1) % 2]
                st3 = st2_ap.rearrange("p (a b) -> p a b", a=4)
                for ch in range(4):
                    P.op("dve", "scalar_tensor_tensor", reads=[ktmpf, "cols", rsl[0][1]], writes=[kst2], out=st3[:, ch, :], in0=tmpf[:, ch, :], scalar=cols[:, C_GQ + ch:C_GQ + ch + 1], in1=rsl[0][0], op0=ALU.mult, op1=ALU.mult)
                P.dma("sp", CQ[:, :, tsl].rearrange("c p t -> p c t"), st3, reads=[kst2], writes=["CQ"])
                if tb == 0: _phase(2.7)
                vs_ap, kvs = vstage[0]
                vs_b, kvs_b = vstage[1]
                for j, (dst_, kd_) in enumerate(((vs_ap, kvs), (vs_b, kvs_b))):
                    ch = 4 + j
                    P.op("dve", "scalar_tensor_tensor", reads=[ktmpf, "cols", rsl[1][1]], writes=[kd_], out=dst_, in0=tmpf[:, ch, :], scalar=cols[:, C_GKV + j:C_GKV + j + 1], in1=rsl[1][0], op0=ALU.mult, op1=ALU.mult)
                    P.dma("sp", CKV[j, :, tsl], dst_, reads=[kd_], writes=["CKV"])

            _phase(3)
            nblk = len(blocks)
            for bi in range(2, 6):
                if bi == 2:
                    load_blk(2)
                if bi + 1 < 6:
                    load_blk(bi + 1)
                wb_, kwb = wblk[bi % 2]
                for j in range(4):
                    chunk = (bi - 2) * 4 + j
                    st_ap, kst_ = stage[chunk % 2]
                    for tb in range(4):
                        tsl = slice(tb * 512, (tb + 1) * 512)
                        sb_ = (chunk * 4 + tb) % 3
                        for c in range(16):
                            P.op("pe", "matmul", reads=[kwb] + [f"{khT}_{tt}" for tt in range(4 * tb, 4 * tb + 4)], writes=[bk(sb_)], out=bank(sb_), lhsT=wb_[:, c, j * 128:(j + 1) * 128], rhs=hT[:, c, tsl], start=c == 0, stop=c == 15)
                        rope_tile(sb_, 128, cosd, sind, kcosd, ksind, swd, tb, st_ap[:, tsl], kst_, (chunk * 4 + tb) % 2)
                    rope_flush()
                    P.dma("sp", QKd[chunk], st_ap, reads=[kst_], writes=["QKd"])
            _phase(5)
            P.barrier()
            AR.reset()
            cq, kcq = AR.bf(4, S)
            ckv, kckv = AR.bf(2, S)
            wuq, kwuq = AR.bf(4, 1536)
            wukv, kwukv = AR.bf(2, 2048)
            tbf = [AR.bf(512) for _ in range(2)]
            stage = [AR.bf(S) for _ in range(2)]
            vstage = [AR.bf(512) for _ in range(2)]
            cosm2, kcosm2 = AR.f32(S)
            sinm2, ksinm2 = AR.f32(S)
            ropeu = [AR.f32(512) for _ in range(2)]
            ropew = [AR.f32(512) for _ in range(2)]
            xt2 = [AR.f32(D) for _ in range(2)]
            tmpf, ktmpf = AR.f32(6, 512)
            posi = xt2[0][0].bitcast(I32)
            kposi = xt2[0][1]
            posf, kposf = xt2[1]
            ua, kua = tmpf.rearrange("p a b -> p (a b)")[:, 0:S], ktmpf
            P.dma("sp", posi, pos.partition_broadcast(128), writes=[kposi])
            P.op("dve", "tensor_copy", reads=[kposi], writes=[kposf], out=posf, in_=posi)
            make_tables(cosm2, kcosm2, sinm2, ksinm2, 1, 3)

            P.dma("sp", cq, CQ.rearrange("c p t -> p c t"), reads=["CQ"], writes=[kcq])
            P.dma("sp", ckv, CKV.rearrange("c p t -> p c t"), reads=["CKV"], writes=[kckv])
            load_w(wuq, kwuq, wview(w_uq, 0, 4, 0, 1536))
            load_w(wukv, kwukv, wview(w_ukv, 0, 2, 0, 2048))
            cnt = [0]

            def fm_proj(wt, kwt, ncc, col0, src, ksrc, tb, b_):
                tsl = slice(tb * 512, (tb + 1) * 512)
                for c in range(ncc):
                    P.op("pe", "matmul", reads=[kwt, ksrc], writes=[bk(b_)], out=bank(b_), lhsT=wt[:, c, col0:col0 + 128], rhs=src[:, c, tsl], start=c == 0, stop=c == ncc - 1)

            evn = [0]

            def evac(b_, dst, kdst):
                evn[0] += 1
                if evn[0] % 2:
                    P.op("act", "activation", reads=[bk(b_)], writes=[kdst], out=dst, in_=bank(b_), func=AF.Copy)
                else:
                    P.op("dve", "tensor_copy", reads=[bk(b_)], writes=[kdst], out=dst, in_=bank(b_))

            for h in range(8):
                st_ap, kst_ = stage[cnt[0] % 2]
                for tb in range(4):
                    b_ = (cnt[0] * 4 + tb) % 3
                    fm_proj(wuq, kwuq, 4, h * 128, cq, kcq, tb, b_)
                    evac(b_, st_ap[:, tb * 512:(tb + 1) * 512], kst_)
                P.dma("sp", QN[h], st_ap, reads=[kst_], writes=["QN"])
                cnt[0] += 1
            for i in range(4):
                st_ap, kst_ = stage[cnt[0] % 2]
                for tb in range(4):
                    b_ = (cnt[0] * 4 + tb) % 3
                    fm_proj(wuq, kwuq, 4, 1024 + i * 128, cq, kcq, tb, b_)
                    rope_tile(b_, 128, cosm2, sinm2, kcosm2, ksinm2, swm, tb, st_ap[:, tb * 512:(tb + 1) * 512], kst_, tb % 2)
                rope_flush()
                P.dma("sp", QR[i], st_ap, reads=[kst_], writes=["QR"])
                cnt[0] += 1
            for h in range(8):
                st_ap, kst_ = stage[cnt[0] % 2]
                for tb in range(4):
                    b_ = (cnt[0] * 4 + tb) % 3
                    fm_proj(wukv, kwukv, 2, h * 128, ckv, kckv, tb, b_)
                    evac(b_, st_ap[:, tb * 512:(tb + 1) * 512], kst_)
                P.dma("sp", KN[h], st_ap, reads=[kst_], writes=["KN"])
                cnt[0] += 1
            for t in range(NT):
                for half in range(2):
                    b_ = (t * 2 + half) % 3
                    for c in range(2):
                        P.op("pe", "matmul", reads=[kwukv, kckv], writes=[bk(b_)], out=bank(b_), lhsT=ckv[:, c, t * 128:(t + 1) * 128], rhs=wukv[:, c, 1024 + half * 512:1024 + (half + 1) * 512], start=c == 0, stop=c == 1)
                    vs_ap, kvs = vstage[(t * 2 + half) % 2]
                    evac(b_, vs_ap, kvs)
                    P.dma("sp", Vm[half * 4:half * 4 + 4, :, t, :].rearrange("h p d -> p h d"), vs_ap.rearrange("p (h d) -> p h d", h=4),
                          reads=[kvs], writes=["Vm"])

            attn_pend = []

            def attn_flush(kp=99):
                keep = []
                for st, fn in list(attn_pend):
                    if st <= kp:
                        fn()
                    else:
                        keep.append((st, fn))
                attn_pend[:] = keep

            def attention(qv, kq, kv_, kk, nkt, vfun, kvv, ndc, softmaxes, scale, q0, tail, PT):
                nkp = nkt // 2
                for si, chunks in enumerate(softmaxes):
                    def S_(kp):
                        for kt in (2 * kp, 2 * kp + 1):
                            b_ = (kp % 2) * 2 + (kt % 2)
                            for ci, (ch, K) in enumerate(chunks):
                                P.op("pe", "matmul", reads=[kk, kq], writes=[bk(b_)], out=bank(b_), lhsT=kv_[0:K, ch, kt * 128:(kt + 1) * 128], rhs=qv[0:K, ch, q0:q0 + 512], start=ci == 0, stop=ci == len(chunks) - 1)

                    def E_(kp):
                        pt_ap, kpt = PT[kp % 3]
                        b0 = (kp % 2) * 2
                        src = psA[:, b0 * 512:(b0 + 2) * 512]
                        P.op("act", "activation", reads=[bk(b0), bk(b0 + 1)], writes=[kpt], out=pt_ap, in_=src, func=AF.Exp, scale=scale)

                    def PV_(kp):
                        pt_ap, kpt = PT[kp % 3]
                        for kt in (2 * kp, 2 * kp + 1):
                            rhs = pt_ap[:, (kt % 2) * 512:(kt % 2 + 1) * 512]
                            for dc in range(ndc):
                                P.op("pe", "matmul", reads=[kvv, kpt], writes=[bk(4 + dc)], out=bank(4 + dc), lhsT=vfun(kt, dc), rhs=rhs, start=kt == 0, stop=kt == nkt - 1)
                            P.op("pe", "matmul", reads=["cb", kpt], writes=[bk(6)], out=bank(6), lhsT=ones, rhs=rhs, start=kt == 0, stop=kt == nkt - 1)

                    S_(0)
                    E_(0)
                    for kp in range(1, nkp):
                        S_(kp)
                        E_(kp)
                        PV_(kp - 1)
                        attn_flush(kp)
                    PV_(nkp - 1)
                    tail(si)

            def run_softmax_pipeline(descs, PT):
                def S_(d, kp):
                    for kt in (2 * kp, 2 * kp + 1):
                        b_ = (kp % 2) * 2 + (kt % 2)
                        ch_ = d["chunks"]
                        for ci, (ch, K) in enumerate(ch_):
                            P.op("pe", "matmul", reads=[d["kk"], d["kq"]], writes=[bk(b_)], out=bank(b_), lhsT=d["kv"][0:K, ch, kt * 128:(kt + 1) * 128], rhs=d["qv"][0:K, ch, d["q0"]:d["q0"] + 512], start=ci == 0, stop=ci == len(ch_) - 1)

                def E_(d, kp):
                    pt_ap, kpt = PT[kp % 3]
                    b0 = (kp % 2) * 2
                    P.op("act", "activation", reads=[bk(b0), bk(b0 + 1)], writes=[kpt], out=pt_ap, in_=psA[:, b0 * 512:(b0 + 2) * 512], func=AF.Exp, scale=d["scale"])

                def PV_(d, kp):
                    pt_ap, kpt = PT[kp % 3]
                    nkt = 16
                    for kt in (2 * kp, 2 * kp + 1):
                        rhs = pt_ap[:, (kt % 2) * 512:(kt % 2 + 1) * 512]
                        for dc in range(d["ndc"]):
                            P.op("pe", "matmul", reads=[d["kvv"], kpt], writes=[bk(4 + dc)], out=bank(4 + dc), lhsT=d["vfun"](kt, dc), rhs=rhs, start=kt == 0, stop=kt == nkt - 1)
                        P.op("pe", "matmul", reads=["cb", kpt], writes=[bk(6)], out=bank(6), lhsT=ones, rhs=rhs, start=kt == 0, stop=kt == nkt - 1)

                for i, d in enumerate(descs):
                    if d["pre"] is not None:
                        d["pre"]()
                    if i == 0:
                        S_(d, 0)
                        E_(d, 0)
                    for kp in range(1, 8):
                        S_(d, kp)
                        E_(d, kp)
                        PV_(d, kp - 1)
                        attn_flush(kp)
                    if i + 1 < len(descs):
                        S_(descs[i + 1], 0)
                        E_(descs[i + 1], 0)
                    PV_(d, 7)
                    d["tail"]()

            _phase(6)
            P.barrier()
            AR.reset()
            qb2 = [AR.bf(2, S) for _ in range(2)]
            kb2 = [AR.bf(2, S) for _ in range(2)]
            vb2 = [AR.bf(16, 256) for _ in range(2)]
            PT = [AR.bf(1024) for _ in range(3)]
            ostg = [AR.bf(2, 512) for _ in range(2)]
            sqd = [AR.bf(2, 512) for _ in range(2)]
            rden = [AR.f32(512) for _ in range(2)]
            o1n = [AR.f32(2, 512) for _ in range(2)]
            ot = [AR.f32(2, 512) for _ in range(2)]
            lnd = [AR.f32(512) for _ in range(2)]
            rsd = [AR.f32(512) for _ in range(2)]
            wo, kwo = AR.bf(16, D)
            for g in range(4):
                load_w(wo[:, 4 * g:4 * g + 4, :], f"{kwo}_{g}", wview(w_out, 4 * g * 128, 4, 0, D))
            kwo_all = [f"{kwo}_{g}" for g in range(4)]

            jobs = [("d", h) for h in range(4)] + [("m", h) for h in range(8)]

            def load_job(ji):
                kind, h = jobs[ji]
                (q_, kq_), (k_, kk_), (v_, kv2) = qb2[ji % 2], kb2[ji % 2], vb2[ji % 2]
                if kind == "d":
                    for j in range(2):
                        P.dma("sp", q_[:, j, :], QKd[2 * h + j], reads=["QKd"], writes=[kq_])
                        P.dma("sp", k_[:, j, :], QKd[8 + 2 * h + j], reads=["QKd"], writes=[kk_])
                    P.dma("sp", v_, Vd[h], reads=["Vd"], writes=[kv2])
                else:
                    P.dma("sp", q_[:, 0, :], QN[h], reads=["QN"], writes=[kq_])
                    P.dma("sp", q_[0:64, 1, :], QR[h // 2, (h % 2) * 64:(h % 2) * 64 + 64, :], reads=["QR"], writes=[kq_])
                    P.dma("sp", k_[:, 0, :], KN[h], reads=["KN"], writes=[kk_])
                    P.dma("sp", k_[0:64, 1, :], KR, reads=["KR"], writes=[kk_])
                    P.dma("sp", v_[:, :, 0:128], Vm[h], reads=["Vm"], writes=[kv2])

            tcount = [0]
            descs = []
            load_job(0)
            for ji, (kind, h) in enumerate(jobs):
                (q_, kq_), (k_, kk_), (v_, kv2) = qb2[ji % 2], kb2[ji % 2], vb2[ji % 2]
                for qi in range(4):
                    q0 = qi * 512
                    pre = (lambda ji=ji: load_job(ji + 1)) if (qi == 0 and ji + 1 < len(jobs)) else None
                    if kind == "m":
                        def tail_m(si, h=h, q0=q0):
                            s_ = tcount[0] % 2
                            tcount[0] += 1
                            rd, krd = rden[s_]
                            o2, ko2 = ot[s_]
                            og, kog = ostg[s_]
                            P.op("act", "activation", reads=[bk(6)], writes=[krd], out=rd, in_=bank(6), func=AF.Copy)
                            P.op("dve", "tensor_copy", reads=[bk(4)], writes=[ko2], out=o2[:, 0, :], in_=bank(4))

                            def later():
                                P.op("dve", "reciprocal", reads=[krd], writes=[krd], out=rd, in_=rd)
                                P.op("dve", "tensor_tensor", reads=[ko2, krd], writes=[kog], out=og[:, 0, :], in0=o2[:, 0, :], in1=rd, op=ALU.mult)
                                P.dma("sp", MIX[h, :, q0:q0 + 512], og[:, 0, :], reads=[kog], writes=["MIX"])
                            attn_pend.append((2, later))
                        descs.append(dict(pre=pre, qv=q_, kq=kq_, kv=k_, kk=kk_, vfun=(lambda kt, dc, v_=v_: v_[:, kt, 0:128]), kvv=kv2, ndc=1,
                                          chunks=[(0, 128), (1, 64)], scale=192.0 ** -0.5, q0=q0, tail=(lambda tail_m=tail_m: tail_m(0))))
                    else:
                        s_ = tcount[0] % 2
                        tcount[0] += 1

                        def tail_d(si, h=h, q0=q0, s_=s_):
                            rd, krd = rden[si]
                            o1, ko1 = o1n[s_]
                            o2, ko2 = ot[s_]
                            dst, kdst = (o1, ko1) if si == 0 else (o2, ko2)
                            P.op("act", "activation", reads=[bk(6)], writes=[krd], out=rd, in_=bank(6), func=AF.Copy)
                            P.op("dve", "tensor_copy", reads=[bk(4)], writes=[kdst], out=dst[:, 0, :], in_=bank(4))
                            P.op("dve", "tensor_copy", reads=[bk(5)], writes=[kdst], out=dst[:, 1, :], in_=bank(5))

                            def norm_part():
                                P.op("dve", "reciprocal", reads=[krd], writes=[krd], out=rd, in_=rd)
                                for dc in range(2):
                                    P.op("dve", "tensor_tensor", reads=[kdst, krd], writes=[kdst], out=dst[:, dc, :], in0=dst[:, dc, :], in1=rd, op=ALU.mult)
                                if si == 1:
                                    for dc in range(2):
                                        P.op("dve", "scalar_tensor_tensor", reads=[ko2, ko1, "sm_neglam"], writes=[ko2], out=o2[:, dc, :], in0=o2[:, dc, :], scalar=sm[:, NEGLAM:NEGLAM + 1], in1=o1[:, dc, :], op0=ALU.mult, op1=ALU.add)
                            attn_pend.append((2, norm_part))
                            if si == 0:
                                return
                            sq_, ksq_ = sqd[s_]
                            og, kog = ostg[s_]
                            ln_, kln_ = lnd[s_]
                            rs_, krs_ = rsd[s_]

                            def sq_part():
                                for dc in range(2):
                                    P.op("act", "activation", reads=[ko2], writes=[ksq_], out=sq_[:, dc, :], in_=o2[:, dc, :], func=AF.Square)
                                for dc in range(2):
                                    P.op("pe", "matmul", reads=[ksq_, "cb"], writes=[bk(7)], out=bank(7), lhsT=ones, rhs=sq_[:, dc, :], start=dc == 0, stop=dc == 1)

                            def fin_part():
                                P.op("act", "activation", reads=[bk(7), "sm_eps5"], writes=[kln_], out=ln_, in_=bank(7), func=AF.Ln, bias=sm[:, EPS5:EPS5 + 1], scale=1.0 / 256)
                                P.op("act", "activation", reads=[kln_], writes=[krs_], out=rs_, in_=ln_, func=AF.Exp, scale=-0.5)
                                for dc in range(2):
                                    P.op("dve", "scalar_tensor_tensor", reads=[ko2, krs_, "sm_gs"], writes=[kog], out=og[:, dc, :], in0=o2[:, dc, :], scalar=sm[:, GS0 + dc:GS0 + dc + 1], in1=rs_, op0=ALU.mult, op1=ALU.mult)
                                P.dma("sp", MIX[8 + 2 * h:8 + 2 * h + 2, :, q0:q0 + 512].rearrange("c p t -> p c t"), og, reads=[kog], writes=["MIX"])
                            attn_pend.append((4, sq_part))
                            attn_pend.append((6, fin_part))
                        for si in range(2):
                            descs.append(dict(pre=(pre if si == 0 else None), qv=q_, kq=kq_, kv=k_, kk=kk_,
                                              vfun=(lambda kt, dc, v_=v_: v_[:, kt, dc * 128:(dc + 1) * 128]), kvv=kv2, ndc=2,
                                              chunks=[(si, 128)], scale=128.0 ** -0.5, q0=q0, tail=(lambda tail_d=tail_d, si=si: tail_d(si))))

            run_softmax_pipeline(descs, PT)
            attn_flush()
            _phase(7)
            P.barrier()
            AR.reset()
            mixT, kmix = AR.bf(16, S)
            xt2 = [AR.f32(D) for _ in range(2)]
            for g in range(4):
                P.dma("sp", mixT[:, 4 * g:4 * g + 4, :], MIX[4 * g:4 * g + 4].rearrange("c p t -> p c t"), reads=["MIX"], writes=[f"{kmix}_{g}"])
            kmix_all = [f"{kmix}_{g}" for g in range(4)]
            P.dma("sp", xt2[0][0], x[0:128, :], writes=[xt2[0][1]])
            for t in range(NT):
                if t + 1 < NT:
                    P.dma("sp", xt2[(t + 1) % 2][0], x[(t + 1) * 128:(t + 2) * 128, :], writes=[xt2[(t + 1) % 2][1]])
                pst, pk = banks4(t % 2)
                for n in range(4):
                    for c in range(16):
                        P.op("pe", "matmul", reads=[kmix_all[c // 4], kwo_all[c // 4]], writes=[pk[n]], out=pst[:, n * 512:(n + 1) * 512], lhsT=mixT[:, c, t * 128:(t + 1) * 128], rhs=wo[:, c, n * 512:(n + 1) * 512], start=c == 0, stop=c == 15)
                xs, kxs = xt2[t % 2]
                P.op("dve", "tensor_tensor", reads=pk + [kxs], writes=[kxs], out=xs, in0=pst[:, :], in1=xs, op=ALU.add)
                P.dma("sp", X1[t * 128:(t + 1) * 128, :], xs, reads=[kxs], writes=["X1"])

            _phase(8)
            P.barrier()
            AR.reset()
            wxa, kwxa = AR.bf(16, 512)
            wxo, kwxo = AR.bf(4, D)
            xkT, kxk = AR.bf(4, MEM)
            xv, kxv = AR.bf(2, 512)
            hxT, khx = AR.bf(16, 512)
            xqT, kxq = AR.bf(4, 512)
            xoT, kxo = AR.bf(4, 512)
            hfs, khfs = AR.bf(16, 512)
            _pt = AR.bf(1024)
            PT = [_pt, _pt, _pt]
            xnb4 = [AR.bf(D) for _ in range(4)]
            xnb2 = xnb4[0:2]
            x1b2 = [AR.f32(4, D) for _ in range(2)]
            rden4 = [AR.f32(512) for _ in range(4)]
            xof = [AR.f32(512) for _ in range(4)]
            xt2 = [(x1b2[1][0][:, 0, :], x1b2[1][1] + "_0")]
            mark_p2 = AR.off
            hmT, khm = AR.bf(16, MEM)
            wxb, kwxb = AR.bf(16, 512)

            load_w(wxa, kwxa, wview(w_xk, 0, 16, 0, 512))
            load_w(wxb, kwxb, wview(w_xv, 0, 16, 0, 512))
            load_w(wxo, kwxo, wview(w_xo, 0, 4, 0, D))
            for mt in range(2):
                xs, kxs = xt2[0]
                P.dma("sp", xs, mem[mt * 128:(mt + 1) * 128, :], writes=[kxs])
                xn, kxn = xnb2[mt % 2]
                norm_to_T(xs, kxs, xn, kxn, C_GMEM, hmT[:, :, mt * 128:(mt + 1) * 128], khm)
            for h in range(4):
                for c in range(16):
                    P.op("pe", "matmul", reads=[kwxa, khm], writes=[bk(h % 2)], out=bank(h % 2)[:, 0:MEM], lhsT=wxa[:, c, h * 128:(h + 1) * 128], rhs=hmT[:, c, :], start=c == 0, stop=c == 15)
                P.op("act", "activation", reads=[bk(h % 2)], writes=[kxk], out=xkT[:, h, :], in_=bank(h % 2)[:, 0:MEM], func=AF.Copy)
            for mt in range(2):
                for c in range(16):
                    P.op("pe", "matmul", reads=[kwxb, khm], writes=[bk(2 + mt)], out=bank(2 + mt), lhsT=hmT[:, c, mt * 128:(mt + 1) * 128], rhs=wxb[:, c, :], start=c == 0, stop=c == 15)
                P.op("act", "activation", reads=[bk(2 + mt)], writes=[kxv], out=xv[:, mt, :], in_=bank(2 + mt), func=AF.Copy)
            load_w(wxa, kwxa, wview(w_xq, 0, 16, 0, 512))

            P.barrier()
            AR.off = mark_p2
            xnbh = [AR.bf(D) for _ in range(3)]
            XN_ENG = "dve"

            def A1(tb):
                xb, kxb = x1b2[tb % 2]
                for i in range(4):
                    t = tb * 4 + i
                    P.dma("sp", xb[:, i, :], X1[t * 128:(t + 1) * 128, :], reads=["X1"], writes=[f"{kxb}_{i}"])
                for i in range(4):
                    norm_s1(xb[:, i, :], f"{kxb}_{i}", xnb4[i][0], xnb4[i][1])

            def A2(tb, i, b0=None):
                norm_s2(xnb4[i][0], xnb4[i][1], C_GX, hxT[:, :, i * 128:(i + 1) * 128], khx, b0=b0)

            def XQ(h):
                for c in range(16):
                    P.op("pe", "matmul", reads=[kwxa, khx], writes=[bk(h % 2)], out=bank(h % 2), lhsT=wxa[:, c, h * 128:(h + 1) * 128], rhs=hxT[:, c, :], start=c == 0, stop=c == 15)
                P.op("act", "activation", reads=[bk(h % 2)], writes=[kxq], out=xqT[:, h, :], in_=bank(h % 2), func=AF.Copy)

            A1(0)
            for i in range(4):
                A2(0, i)
            for h in range(4):
                XQ(h)
            for tb in range(4):
                xb, kxb = x1b2[tb % 2]
                for h in range(4):
                    def tail_x(si, h=h):
                        rd, krd = rden4[h]
                        xo_, kxo_ = xof[h]
                        P.op("act", "activation", reads=[bk(6)], writes=[krd], out=rd, in_=bank(6), func=AF.Copy)
                        P.op("dve", "tensor_copy", reads=[bk(4)], writes=[kxo_], out=xo_, in_=bank(4))

                        def later(h=h, rd=rd, krd=krd, xo_=xo_, kxo_=kxo_):
                            P.op("act", "activation", reads=[krd], writes=[krd], out=rd, in_=rd, func=AF.Ln)
                            P.op("act", "activation", reads=[krd], writes=[krd], out=rd, in_=rd, func=AF.Exp, scale=-1.0)
                            P.op("dve", "tensor_tensor", reads=[kxo_, krd], writes=[f"{kxo}_{h}"], out=xoT[:, h, :], in0=xo_, in1=rd, op=ALU.mult)
                        attn_pend.append((99, later))
                    attention(xqT, kxq, xkT, kxk, 2, lambda kt, dc, h=h: xv[:, kt, h * 128:(h + 1) * 128], kxv, 1,
                              [[(h, 128)]], 128.0 ** -0.5, 0, tail_x, PT)
                attn_flush()
                if tb + 1 < 4:
                    A1(tb + 1)
                for i in range(4):
                    t = tb * 4 + i
                    pst, pk = banks4(i % 2)
                    for n in range(4):
                        for h in range(4):
                            P.op("pe", "matmul", reads=[f"{kxo}_{h}", kwxo], writes=[pk[n]], out=pst[:, n * 512:(n + 1) * 512], lhsT=xoT[:, h, i * 128:(i + 1) * 128], rhs=wxo[:, h, n * 512:(n + 1) * 512], start=h == 0, stop=h == 3)
                    if tb + 1 < 4:
                        A2(tb + 1, i, b0=4 * ((i + 1) % 2))
                    xs = xb[:, i, :]
                    kxs = f"{kxb}_{i}"
                    P.op("dve", "tensor_tensor", reads=pk + [kxs], writes=[kxs], out=xs, in0=pst[:, :], in1=xs, op=ALU.add)
                    P.dma("sp", X2[t * 128:(t + 1) * 128, :], xs, reads=[kxs], writes=["X2"])
                    norm_s1(xs, kxs, xnbh[i % 3][0], xnbh[i % 3][1], eng=XN_ENG)
                    if i >= 2:
                        j = i - 2
                        norm_s2(xnbh[j % 3][0], xnbh[j % 3][1], C_GFFN, hfs[:, :, j * 128:(j + 1) * 128], khfs, b0=4 * (i % 2) + 2)
                if tb + 1 < 4:
                    XQ(0)
                    XQ(1)
                norm_s2(xnbh[2][0], xnbh[2][1], C_GFFN, hfs[:, :, 2 * 128:3 * 128], khfs)
                if tb + 1 < 4:
                    XQ(2)
                norm_s2(xnbh[0][0], xnbh[0][1], C_GFFN, hfs[:, :, 3 * 128:4 * 128], khfs)
                if tb + 1 < 4:
                    XQ(3)
                P.dma("sp", HF[:, :, tb * 512:(tb + 1) * 512].rearrange("c p t -> p c t"), hfs, reads=[khfs], writes=["HF"])

            _phase(9)
            def final_norm(t, xs, kxs, junk, kjunk, gfin, kgf):
                P.dma("sp", xs, X2[t * 128:(t + 1) * 128, :], reads=["X2"], writes=[kxs])
                r, kr = rstd_cols(xs, junk, D, EPS6, [kxs], [kjunk])
                P.op("dve", "scalar_tensor_tensor", reads=[kxs, kr, kgf], writes=[kxs], out=xs, in0=xs, scalar=r, in1=gfin, op0=ALU.mult, op1=ALU.mult)
                P.dma("sp", y[t * 128:(t + 1) * 128, :], xs, reads=[kxs], writes=["y"])

            TB = 1024
            for ps_ in range(2):
                P.barrier()
                AR.reset()
                actT, kact = AR.bf(NFC, TB)
                fn_x = [AR.f32(D) for _ in range(2)]
                gfin, kgf = AR.f32(D)
                fn_junk = [AR.bf(D) for _ in range(1)]
                xblk = [AR.f32(512) for _ in range(4)]
                mark = AR.off
                hf, khf = AR.bf(16, TB)
                wg = [AR.bf(16, 256) for _ in range(2)]
                wu = [AR.bf(16, 256) for _ in range(2)]
                sgb = [AR.bf(512) for _ in range(2)]
                P.dma("sp", gfin, gfin_d.partition_broadcast(128), writes=[kgf])
                P.dma("sp", hf, HF[:, :, ps_ * TB:(ps_ + 1) * TB].rearrange("c p t -> p c t"), reads=["HF"], writes=[khf])

                def load_gu(fb):
                    load_w(wg[fb % 2][0], wg[fb % 2][1], wview(w_gate, 0, 16, fb * 256, 256))
                    load_w(wu[fb % 2][0], wu[fb % 2][1], wview(w_up, 0, 16, fb * 256, 256))
                load_gu(0)
                pend_norm = list(range((ps_ - 1) * 8, ps_ * 8)) if ps_ > 0 else []
                for fb in range(22):
                    if fb + 1 < 22:
                        load_gu(fb + 1)
                    (wg_, kwg), (wu_, kwu) = wg[fb % 2], wu[fb % 2]
                    for fc in range(2):
                        f = fb * 2 + fc
                        for tk in range(2):
                            it = (f * 2 + tk) % 2
                            bg, bu = 2 * it, 2 * it + 1
                            tsl = slice(tk * 512, (tk + 1) * 512)
                            for c in range(16):
                                P.op("pe", "matmul", reads=[kwg, khf], writes=[bk(bg)], out=bank(bg), lhsT=wg_[:, c, fc * 128:(fc + 1) * 128], rhs=hf[:, c, tsl], start=c == 0, stop=c == 15)
                            for c in range(16):
                                P.op("pe", "matmul", reads=[kwu, khf], writes=[bk(bu)], out=bank(bu), lhsT=wu_[:, c, fc * 128:(fc + 1) * 128], rhs=hf[:, c, tsl], start=c == 0, stop=c == 15)
                            sg_, ksg = sgb[it]
                            P.op("act", "activation", reads=[bk(bg)], writes=[ksg], out=sg_, in_=bank(bg), func=AF.Silu)
                            P.op("dve", "tensor_tensor", reads=[bk(bu), ksg], writes=[f"{kact}_{f}"], out=actT[:, f, tsl], in0=bank(bu), in1=sg_, op=ALU.mult)
                    if pend_norm and fb % 2 == 1:
                        t = pend_norm.pop(0)
                        final_norm(t, fn_x[t % 2][0], fn_x[t % 2][1], fn_junk[0][0], fn_junk[0][1], gfin, kgf)
                P.barrier()
                AR.off = mark
                wd = [AR.bf(11, 512) for _ in range(6)]

                wd_next = [0]

                def load_wd_upto(gmax):
                    while wd_next[0] < min(gmax, 16):
                        gi = wd_next[0]
                        n_, g = gi // 4, gi % 4
                        d_, kd_ = wd[gi % 6]
                        load_w(d_, kd_, wview(w_down, g * 11 * 128, 11, n_ * 512, 512))
                        wd_next[0] += 1
                kact_all = [f"{kact}_{f}" for f in range(NFC)]
                for n in range(4):
                    load_wd_upto(n * 4 + 6)
                    for t8 in range(8):
                        t = ps_ * 8 + t8
                        b_ = t8 % 4
                        xb_, kxb = xblk[t8 % 4]
                        P.dma("sp", xb_, X2[t * 128:(t + 1) * 128, n * 512:(n + 1) * 512], reads=["X2"], writes=[kxb])
                        for f in range(NFC):
                            d_, kd_ = wd[(n * 4 + f // 11) % 6]
                            P.op("pe", "matmul", reads=[kact_all[f], kd_], writes=[bk(b_)], out=bank(b_), lhsT=actT[:, f, t8 * 128:(t8 + 1) * 128], rhs=d_[:, f % 11, :], start=f == 0, stop=f == NFC - 1)
                        P.op("dve", "tensor_tensor", reads=[bk(b_), kxb], writes=[kxb], out=xb_, in0=bank(b_), in1=xb_, op=ALU.add)
                        P.dma("sp", X2[t * 128:(t + 1) * 128, n * 512:(n + 1) * 512], xb_, reads=[kxb], writes=["X2"])
                        if ps_ == 1 and n == 3:
                            final_norm(t, fn_x[t % 2][0], fn_x[t % 2][1], fn_junk[0][0], fn_junk[0][1], gfin, kgf)


        except _Stop:
            pass

        need = P.finalize()
        sems = {"eng": {}, "dma": {}}
        for e in ENGS:
            sems["eng"][e] = [nc.alloc_semaphore(name=f"s_{e}_{i}") for i in range(need[e])]
        for q in ("sp", "pool"):
            sems["dma"][q] = [nc.alloc_semaphore(name=f"d_{q}_{i}") for i in range(NDMASEM)]
        with nc.Block() as block:
            @block.tensor
            def _(e):
                P.run("pe", e, sems)

            @block.scalar
            def _(e):
                P.run("act", e, sems)

            @block.vector
            def _(e):
                P.run("dve", e, sems)

            @block.gpsimd
            def _(e):
                P.run("pool", e, sems)

            @block.sync
            def _(e):
                P.run("sp", e, sems)
    return nc


_NC = [None]


def _host_consts():
    p = np.arange(128)
    cf = np.zeros((128, 4), np.float32)
    cf[:, 0] = (10000.0 ** (-(2.0 * (p % 64)) / 128.0)) / (2 * math.pi)
    cf[:, 1] = (10000.0 ** (-(2.0 * (p % 32)) / 64.0)) / (2 * math.pi)
    cf[:, 2] = np.where(p < 64, -1.0, 1.0)
    cf[:, 3] = np.where((p % 64) < 32, -1.0, 1.0)
    cb = np.zeros((128, 512), np.float32)
    cb[:, 0:128] = np.eye(128)
    cb[:, 128:256] = 1.0
    m = np.arange(128)
    cb[(m + 64) % 128, 256 + m] = 1.0
    src = np.where((m % 64) < 32, m + 32, m - 32)
    cb[src, 384 + m] = 1.0
    return cf, cb


def kernel(x, mem, positions, g_mix, w_in, g_q_lat, w_uq, g_kv_lat, w_ukv,
           lambda_q1, lambda_k1, lambda_q2, lambda_k2, g_diff_sub, w_out,
           g_xattn, g_mem, w_xq, w_xk, w_xv, w_xo,
           g_ffn, w_gate, w_up, w_down, g_final):
    f32 = lambda a: np.ascontiguousarray(np.asarray(a), dtype=np.float32)
    if _NC[0] is None:
        _NC[0] = build_program()
    nc = _NC[0]
    cf, cb = _host_consts()

    def colT(g, n):
        return f32(g).reshape(n, 128).T

    cols = np.concatenate([colT(g_mix[0], 16), colT(g_xattn[0], 16), colT(g_mem[0], 16), colT(g_ffn[0], 16),
                           colT(g_q_lat[0], 4), colT(g_kv_lat[0], 2), colT(g_diff_sub[0], 2)], axis=1)
    cols = np.ascontiguousarray(cols, dtype=np.float32)
    lamv = np.ascontiguousarray(np.stack([f32(lambda_q1[0]), f32(lambda_k1[0]), f32(lambda_q2[0]), f32(lambda_k2[0])]))
    wuq = f32(w_uq[0]).reshape(512, 8, 192)
    wuq_p = np.ascontiguousarray(np.concatenate([wuq[:, :, :128].reshape(512, 1024), wuq[:, :, 128:].reshape(512, 512)], axis=1))
    wukv = f32(w_ukv[0]).reshape(256, 8, 256)
    wukv_p = np.ascontiguousarray(np.concatenate([wukv[:, :, :128].reshape(256, 1024), wukv[:, :, 128:].reshape(256, 1024)], axis=1))
    shared = {
        "w_in": f32(w_in[0]), "w_uq": wuq_p, "w_ukv": wukv_p, "w_out": f32(w_out[0]),
        "w_xq": f32(w_xq[0]), "w_xk": f32(w_xk[0]), "w_xv": f32(w_xv[0]), "w_xo": f32(w_xo[0]),
        "w_gate": f32(w_gate[0]), "w_up": f32(w_up[0]), "w_down": f32(w_down[0]),
        "cols": cols, "lamv": lamv, "gfin": f32(g_final).reshape(1, D), "cf": cf, "cb": cb,
    }
    xs = f32(x)
    ms = f32(mem)
    ps = np.ascontiguousarray(np.asarray(positions), dtype=np.int32)
    in_maps = []
    for b in range(8):
        d = dict(shared)
        d["x"] = xs[b]
        d["mem"] = ms[b]
        d["pos"] = ps[b:b + 1]
        in_maps.append(d)
    res = run_bass_kernel_spmd(nc, in_maps, core_ids=list(range(8)))
    return np.stack([np.asarray(r["y"], dtype=np.float32) for r in res.results], axis=0)
```

```python
import math
import contextlib
import numpy as np
import concourse.bass as bass
import concourse.mybir as mybir
from concourse.bass_utils import run_bass_kernel_spmd

F32 = mybir.dt.float32
BF16 = mybir.dt.bfloat16
I32 = mybir.dt.int32
AF = mybir.ActivationFunctionType
ALU = mybir.AluOpType
AX = mybir.AxisListType

S = 2048
D = 2048
NT = 16
NC16 = 16
FF = 5632
NFC = 44
MEM = 256
IN_W = 3904
EPS = 1e-6

ENGS = ["pe", "act", "dve", "pool", "sp"]
EPOCH = 2000
NDMASEM = 16


class Prog:
    def __init__(self):
        self.ops = {e: [] for e in ENGS}
        self.res = {}
        self.dma_n = {"sp": 0, "pool": 0}

    @staticmethod
    def _joint(r):
        return r is not None and not r[1] and r[0] and all(t[0] == "d" for t in r[0]) and len(r[0]) < 48

    def _deps(self, reads, writes, is_dma=False):
        toks = []
        for k in reads:
            r = self.res.get(k)
            if r:
                toks.extend(r[0])
        for k in writes:
            r = self.res.get(k)
            if r:
                if is_dma and self._joint(r):
                    toks.extend(r[2])
                else:
                    toks.extend(r[0])
                    toks.extend(r[1].values())
        return toks

    def _commit(self, tok, reads, writes):
        for k in reads:
            if k in writes:
                continue
            r = self.res.setdefault(k, [[], {}, []])
            if tok[0] == "e":
                r[1][tok[1]] = tok
            else:
                r[1][(tok[1], tok[2] % NDMASEM)] = tok
        for k in writes:
            r = self.res.get(k)
            if tok[0] == "d" and self._joint(r):
                r[0].append(tok)
            else:
                inherited = (list(r[0]) + list(r[1].values())) if (r and tok[0] == "d") else []
                self.res[k] = [[tok], {}, inherited]

    def op(self, eng, meth, reads=(), writes=(), **kw):
        if eng != "pe":
            psr = [k for k in reads if k.startswith("ps") and k not in writes]
            if psr:
                writes = list(writes) + psr
        toks = self._deps(reads, writes)
        idx = len(self.ops[eng])
        self.ops[eng].append({"waits": toks, "fn": (lambda e, m=meth, kw=kw: getattr(e, m)(**kw)), "sig": False})
        self._commit(("e", eng, idx), reads, writes)

    def dma(self, q, out, in_, reads=(), writes=()):
        toks = self._deps(reads, writes, is_dma=True)
        k = self.dma_n[q]
        self.dma_n[q] += 1
        if k >= NDMASEM:
            toks.append(("d", q, k - NDMASEM))
        self.ops[q].append({"waits": toks, "fn": (lambda e, o=out, i=in_: e.dma_start(out=o, in_=i)),
                            "sig": False, "dma": k})
        self._commit(("d", q, k), reads, writes)

    def barrier(self):
        toks = []
        for e in ENGS:
            for i in range(len(self.ops[e]) - 1, -1, -1):
                o = self.ops[e][i]
                if o["fn"] is not None and "dma" not in o:
                    toks.append(("e", e, i))
                    break
        for q in ("sp", "pool"):
            n = self.dma_n[q]
            for k in range(max(0, n - NDMASEM), n):
                toks.append(("d", q, k))
        for e in ENGS:
            self.ops[e].append({"waits": list(toks), "fn": None, "sig": False})

    def finalize(self):
        for e in ENGS:
            for o in self.ops[e]:
                for t in o["waits"]:
                    if t[0] == "e" and not (t[1] == e and e in ("pe", "sp")):
                        self.ops[t[1]][t[2]]["sig"] = True
        self.cnt = {}
        for e in ENGS:
            c = 0
            for o in self.ops[e]:
                if o["sig"]:
                    c += 1
                    o["cnt"] = c
            self.cnt[e] = c
        return {e: max(1, (self.cnt[e] + EPOCH - 1) // EPOCH) for e in ENGS}

    def run(self, e, handle, sems):
        seen_e = {}
        seen_d = {}
        for o in self.ops[e]:
            for t in o["waits"]:
                if t[0] == "e":
                    _, pe_, idx = t
                    if pe_ == e and e in ("pe", "sp"):
                        continue
                    if seen_e.get(pe_, -1) >= idx:
                        continue
                    seen_e[pe_] = idx
                    c = self.ops[pe_][idx]["cnt"]
                    handle.wait_ge(sems["eng"][pe_][(c - 1) // EPOCH], (c - 1) % EPOCH + 1)
                else:
                    _, q, k = t
                    slot = k % NDMASEM
                    val = 16 * (k // NDMASEM + 1)
                    if seen_d.get((q, slot), 0) >= val:
                        continue
                    seen_d[(q, slot)] = val
                    handle.wait_ge(sems["dma"][q][slot], val)
            if o["fn"] is None:
                continue
            inst = o["fn"](handle)
            if "dma" in o:
                inst.then_inc(sems["dma"][e][o["dma"] % NDMASEM], 16)
            elif o["sig"]:
                c = o["cnt"]
                inst.then_inc(sems["eng"][e][(c - 1) // EPOCH], 1)
        if e == "sp":
            for q in ("sp", "pool"):
                n = self.dma_n[q]
                for k in range(max(0, n - NDMASEM), n):
                    handle.wait_ge(sems["dma"][q][k % NDMASEM], 16 * (k // NDMASEM + 1))


class _Stop(Exception):
    pass


_STOP = [99]


def _phase(n):
    if n > _STOP[0]:
        raise _Stop()


class Arena:
    def __init__(self, ap, size):
        self.ap, self.size, self.off, self.n = ap, size, 0, 0

    def reset(self):
        self.off = 0

    def _take(self, nel):
        self.off = (self.off + 15) // 16 * 16
        v = self.ap[:, self.off:self.off + nel]
        self.off += nel
        assert self.off <= self.size, ("arena overflow", self.off, self.size)
        self.n += 1
        return v

    def bf(self, *free):
        n = int(np.prod(free))
        v = self._take(n)
        if len(free) == 2:
            v = v.rearrange("p (a b) -> p a b", a=free[0])
        return v, f"A{self.n}"

    def f32(self, *free):
        n = int(np.prod(free))
        v = self._take(2 * n).bitcast(F32)
        if len(free) == 2:
            v = v.rearrange("p (a b) -> p a b", a=free[0])
        return v, f"A{self.n}"

    def i32(self, *free):
        n = int(np.prod(free))
        v = self._take(2 * n).bitcast(I32)
        return v, f"A{self.n}"


C_GMIX, C_GX, C_GMEM, C_GFFN, C_GQ, C_GKV, C_GSUB = 0, 16, 32, 48, 64, 68, 70
NCOL = 72
ARENA_EL = 100 * 1024


def build_program():
    nc = bass.Bass("TRN2", target_bir_lowering=False)

    def din(name, shape, dt=F32):
        return nc.dram_tensor(name, list(shape), dt, kind="ExternalInput").ap()

    def dscr(name, shape, dt):
        return nc.dram_tensor(name, list(shape), dt).ap()

    x = din("x", [S, D])
    mem = din("mem", [MEM, D])
    pos = din("pos", [1, S], I32)
    w_in = din("w_in", [D, IN_W])
    w_uq = din("w_uq", [512, 1536])
    w_ukv = din("w_ukv", [256, 2048])
    w_out = din("w_out", [D, D])
    w_xq = din("w_xq", [D, 512])
    w_xk = din("w_xk", [D, 512])
    w_xv = din("w_xv", [D, 512])
    w_xo = din("w_xo", [512, D])
    w_gate = din("w_gate", [D, FF])
    w_up = din("w_up", [D, FF])
    w_down = din("w_down", [FF, D])
    cols_d = din("cols", [128, NCOL])
    lam_d = din("lamv", [4, 128])
    gfin_d = din("gfin", [1, D])
    cf_d = din("cf", [128, 4])
    cb_d = din("cb", [128, 512])
    y = nc.dram_tensor("y", [S, D], F32, kind="ExternalOutput").ap()

    QKd = dscr("s_qkd", [16, 128, S], BF16)
    Vd = dscr("s_vd", [4, 128, NT, 256], BF16)
    CQ = dscr("s_cq", [4, 128, S], BF16)
    CKV = dscr("s_ckv", [2, 128, S], BF16)
    KR = dscr("s_kr", [64, S], BF16)
    QN = dscr("s_qn", [8, 128, S], BF16)
    QR = dscr("s_qr", [4, 128, S], BF16)
    KN = dscr("s_kn", [8, 128, S], BF16)
    Vm = dscr("s_vm", [8, 128, NT, 128], BF16)
    MIX = dscr("s_mix", [16, 128, S], BF16)
    X1 = dscr("s_x1", [S, D], F32)
    X2 = dscr("s_x2", [S, D], F32)
    HF = dscr("s_hf", [16, 128, S], BF16)

    P = Prog()
    es = contextlib.ExitStack()
    with es:
        def sb(name, shape, dt):
            return es.enter_context(nc.sbuf_tensor(name, list(shape), dt))

        arena_t = sb("arena", [128, ARENA_EL], BF16)
        AR = Arena(arena_t, ARENA_EL)
        cb = sb("cb_sb", [128, 512], BF16)
        cols = sb("cols_sb", [128, NCOL], F32)
        cf = sb("cf_sb", [128, 4], F32)
        sm = sb("sm", [128, 64], F32)
        lamb = sb("lamb", [128, 4, 128], F32)
        psA = es.enter_context(nc.psum_tensor("psA", [128, 2048], F32))
        psB = es.enter_context(nc.psum_tensor("psB", [128, 2048], F32))

        ident = cb[:, 0:128]
        ones = cb[:, 128:256]
        swd = cb[:, 256:384]
        swm = cb[:, 384:512]

        def bank(b):
            t = psA if b < 4 else psB
            return t[:, (b % 4) * 512:(b % 4 + 1) * 512]

        def bk(b):
            return f"ps{b}"

        def banks4(g):
            return (psA if g == 0 else psB), [f"ps{4 * g + i}" for i in range(4)]

        def bankpair_bf(b):
            t = psA if b < 4 else psB
            return t[:, (b % 4) * 512:(b % 4 + 2) * 512].bitcast(BF16)

        EPS6, EPS5, NEGLAM, GS0, GS1 = 0, 1, 2, 3, 4
        SS0 = 8
        uid = [0]

        def key(prefix):
            uid[0] += 1
            return f"{prefix}{uid[0]}"

        try:
            P.dma("pool", cb[:], cb_d, writes=["cb"])
            P.dma("sp", cols[:], cols_d, writes=["cols"])
            P.dma("sp", cf[:], cf_d, writes=["cf"])
            for i in range(4):
                P.dma("sp", lamb[:, i, :], lam_d[i:i + 1, :].partition_broadcast(128), writes=[f"lamb{i}"])
            P.op("dve", "memset", writes=["sm_eps6", "sm_eps5", "sm_neglam", "sm_gs", "sm5", "sm6", "sm7"] + [f"sm_ss{i}" for i in range(4)], ap=sm[:], constant=0.0)
            P.op("dve", "memset", writes=["sm_eps6"], ap=sm[:, EPS6:EPS6 + 1], constant=1e-6)
            P.op("dve", "memset", writes=["sm_eps5"], ap=sm[:, EPS5:EPS5 + 1], constant=1e-5)
            P.op("dve", "tensor_tensor", reads=["lamb0", "lamb1"], writes=["lamb0"], out=lamb[:, 0, :], in0=lamb[:, 0, :], in1=lamb[:, 1, :], op=ALU.mult)
            P.op("dve", "tensor_tensor", reads=["lamb2", "lamb3"], writes=["lamb2"], out=lamb[:, 2, :], in0=lamb[:, 2, :], in1=lamb[:, 3, :], op=ALU.mult)
            P.op("dve", "reduce_sum", reads=["lamb0"], writes=["sm5"], out=sm[:, 5:6], in_=lamb[:, 0, :], axis=AX.X)
            P.op("dve", "reduce_sum", reads=["lamb2"], writes=["sm6"], out=sm[:, 6:7], in_=lamb[:, 2, :], axis=AX.X)
            P.op("act", "activation", reads=["sm5"], writes=["sm5"], out=sm[:, 5:6], in_=sm[:, 5:6], func=AF.Exp)
            P.op("act", "activation", reads=["sm6"], writes=["sm6"], out=sm[:, 6:7], in_=sm[:, 6:7], func=AF.Exp)
            P.op("dve", "tensor_tensor", reads=["sm5", "sm6"], writes=["sm7"], out=sm[:, 7:8], in0=sm[:, 6:7], in1=sm[:, 5:6], op=ALU.subtract)
            P.op("dve", "tensor_scalar", reads=["sm7"], writes=["sm_neglam"], out=sm[:, NEGLAM:NEGLAM + 1], in0=sm[:, 7:8], scalar1=-0.2, scalar2=None, op0=ALU.add)
            P.op("dve", "tensor_scalar", reads=["cols"], writes=["sm_gs"], out=sm[:, GS0:GS0 + 2], in0=cols[:, C_GSUB:C_GSUB + 2], scalar1=0.8, scalar2=None, op0=ALU.mult)

            ssn = [0]

            def rstd_cols(src_ap, junk_ap, n_feat, eps_col, rk, wk_junk):
                slot = ssn[0] % 4
                ssn[0] += 1
                c0 = SS0 + 3 * slot
                kss = f"sm_ss{slot}"
                P.op("act", "activation", reads=[kss], writes=[kss], out=sm[:, c0:c0 + 1], in_=sm[:, c0:c0 + 1], func=AF.Copy, scale=0.0)
                P.op("act", "activation", reads=rk, writes=[kss] + wk_junk, out=junk_ap, in_=src_ap, func=AF.Square, accum_out=sm[:, c0:c0 + 1])
                P.op("act", "activation", reads=[kss, "sm_eps6", "sm_eps5"], writes=[kss], out=sm[:, c0 + 1:c0 + 2], in_=sm[:, c0:c0 + 1], func=AF.Ln, bias=sm[:, eps_col:eps_col + 1], scale=1.0 / n_feat)
                P.op("act", "activation", reads=[kss], writes=[kss], out=sm[:, c0 + 2:c0 + 3], in_=sm[:, c0 + 1:c0 + 2], func=AF.Exp, scale=-0.5)
                return sm[:, c0 + 2:c0 + 3], kss

            trn = [0]

            def norm_s1(src, ksrc, xnb, kxnb, eng="dve"):
                r, kr = rstd_cols(src, xnb, D, EPS6, [ksrc], [kxnb])
                P.op(eng, "tensor_scalar", reads=[ksrc, kr], writes=[kxnb], out=xnb, in0=src, scalar1=r, scalar2=None, op0=ALU.mult)

            def norm_s2(xnb, kxnb, gcol0, dst3, kdst, b0=None):
                if b0 is None:
                    b0 = 6 if trn[0] % 2 else 4
                    trn[0] += 1
                pst = bankpair_bf(b0)
                for c in range(16):
                    P.op("pe", "transpose", reads=[kxnb, "cb"], writes=[bk(b0), bk(b0 + 1)], out=pst[:, c * 128:(c + 1) * 128], in_=xnb[:, c * 128:(c + 1) * 128], identity=ident)
                P.op("dve", "tensor_tensor", reads=[bk(b0), bk(b0 + 1), "cols"], writes=[kdst], out=dst3, in0=pst.rearrange("p (c t) -> p c t", c=16), in1=cols[:, gcol0:gcol0 + 16].unsqueeze(2).to_broadcast([128, 16, 128]), op=ALU.mult)

            def norm_to_T(src, ksrc, xnb, kxnb, gcol0, dst3, kdst):
                norm_s1(src, ksrc, xnb, kxnb)
                norm_s2(xnb, kxnb, gcol0, dst3, kdst)

            def dma_split(q, dst, src, reads, writes, grp=2):
                if len(dst.shape) == 3 and dst.shape[1] > grp:
                    for c0 in range(0, dst.shape[1], grp):
                        c1 = min(dst.shape[1], c0 + grp)
                        P.dma(q, dst[:, c0:c1, :], src[:, c0:c1, :], reads=reads, writes=writes)
                else:
                    P.dma(q, dst, src, reads=reads, writes=writes)

            def load_w(dst, kdst, src):
                dma_split("pool", dst, src, [], [kdst])

            def wview(w, r0, nr_chunks, c0, ncols):
                return w[r0:r0 + nr_chunks * 128, c0:c0 + ncols].rearrange("(c p) n -> p c n", p=128)

            AR.reset()
            hT, khT = AR.bf(16, S)
            xnb2 = [AR.bf(D) for _ in range(2)]
            wblk = [AR.bf(16, 512) for _ in range(2)]
            sqb = [AR.bf(512) for _ in range(6)]
            tbf = [AR.bf(512) for _ in range(2)]
            stage = [AR.bf(S) for _ in range(2)]
            vstage = [AR.bf(512) for _ in range(2)]
            xt2 = [AR.f32(D) for _ in range(2)]
            tmpf, ktmpf = AR.f32(6, 512)
            cosd, kcosd = AR.f32(S)
            sind, ksind = AR.f32(S)
            cosm, kcosm = AR.f32(S)
            sinm, ksinm = AR.f32(S)
            rp8, krp8 = AR.f32(8, 512)
            ropeu = [(rp8[:, j, :], f"{krp8}_{j}") for j in range(0, 2)]
            ropew = [(rp8[:, j, :], f"{krp8}_{j}") for j in range(2, 4)]
            rsl = [(rp8[:, j, :], f"{krp8}_{j}") for j in range(4, 6)]
            lnl = [(rp8[:, j, :], f"{krp8}_{j}") for j in range(6, 8)]

            rp8f = rp8.rearrange("p a b -> p (a b)")
            posf, kposf = rp8f[:, 0:S], "tab_posf"
            posi, kposi = rp8f[:, S:2 * S].bitcast(I32), "tab_posi"
            ua, kua = tmpf.rearrange("p a b -> p (a b)")[:, 0:S], ktmpf

            def tab_setup():
                P.dma("sp", posi, pos.partition_broadcast(128), writes=[kposi])
                P.op("dve", "tensor_copy", reads=[kposi], writes=[kposf], out=posf, in_=posi)

            def make_tables_gen(cdst, kc, sdst, ks, fcol, sign_col):
                P.op("dve", "tensor_scalar", reads=[kposf, "cf"], writes=[kua], out=ua, in0=posf, scalar1=cf[:, fcol:fcol + 1], scalar2=None, op0=ALU.mult)
                yield
                P.op("dve", "tensor_copy", reads=[kua], writes=[kposi], out=posi, in_=ua)
                yield
                P.op("dve", "tensor_copy", reads=[kposi], writes=[ks], out=sdst, in_=posi)
                yield
                P.op("dve", "tensor_tensor", reads=[kua, ks], writes=[kua], out=ua, in0=ua, in1=sdst, op=ALU.subtract)
                yield
                P.op("dve", "tensor_scalar", reads=[kua], writes=[ks], out=sdst, in0=ua, scalar1=0.5, scalar2=None, op0=ALU.is_gt)
                yield
                P.op("dve", "tensor_tensor", reads=[kua, ks], writes=[kua], out=ua, in0=ua, in1=sdst, op=ALU.subtract)
                yield
                P.op("dve", "tensor_scalar", reads=[kua], writes=[ks], out=sdst, in0=ua, scalar1=-0.5, scalar2=None, op0=ALU.is_lt)
                yield
                P.op("dve", "tensor_tensor", reads=[kua, ks], writes=[kua], out=ua, in0=ua, in1=sdst, op=ALU.add)
                yield
                P.op("dve", "tensor_scalar", reads=[kua], writes=[kc], out=cdst, in0=ua, scalar1=0.25, scalar2=None, op0=ALU.add)
                yield
                P.op("act", "activation", reads=[kua], writes=[ks], out=sdst, in_=ua, func=AF.Sin, scale=6.28318)
                yield
                P.op("dve", "tensor_scalar", reads=[kc], writes=[kua], out=ua, in0=cdst, scalar1=0.5, scalar2=None, op0=ALU.is_gt)
                yield
                P.op("dve", "tensor_tensor", reads=[kua, kc], writes=[kc], out=cdst, in0=cdst, in1=ua, op=ALU.subtract)
                yield
                P.op("act", "activation", reads=[kc], writes=[kc], out=cdst, in_=cdst, func=AF.Sin, scale=6.28318)
                yield
                P.op("dve", "tensor_scalar", reads=[ks, "cf"], writes=[ks], out=sdst, in0=sdst, scalar1=cf[:, sign_col:sign_col + 1], scalar2=None, op0=ALU.mult)
                yield

            def make_tables(*a):
                for _ in make_tables_gen(*a):
                    pass

            _phase(1)
            def tabgen():
                tab_setup()
                yield
                yield from make_tables_gen(cosd, kcosd, sind, ksind, 0, 2)
                yield from make_tables_gen(cosm, kcosm, sinm, ksinm, 1, 3)
            tg = tabgen()

            blocks = [(0, 512), (512, 320)] + [(832 + 512 * i, 512) for i in range(4)] + [(2880 + 512 * i, 512) for i in range(2)]

            def load_blk(i):
                c0, n = blocks[i]
                dst, kd = wblk[i % 2]
                load_w(dst[:, :, 0:n], kd, wview(w_in, 0, 16, c0, n))
                if i == 1:
                    load_w(dst[:, :, 320:384], kd, wview(w_in, 0, 16, 768, 64))

            load_blk(6)
            load_blk(7)
            def xload(t):
                P.dma("sp", xt2[t % 2][0], x[t * 128:(t + 1) * 128, :], writes=[xt2[t % 2][1]])

            def s1(t):
                norm_s1(xt2[t % 2][0], xt2[t % 2][1], xnb2[t % 2][0], xnb2[t % 2][1])

            def s2(t):
                norm_s2(xnb2[t % 2][0], xnb2[t % 2][1], C_GMIX, hT[:, :, t * 128:(t + 1) * 128], f"{khT}_{t}")

            xload(0)
            xload(1)
            s1(0)
            xload(2)
            s1(1)
            xload(3)
            s2(0)
            for t in range(NT):
                if t + 2 < NT:
                    s1(t + 2)
                    if t + 4 < NT:
                        xload(t + 4)
                if t + 1 < NT:
                    s2(t + 1)
                for bi in (6, 7):
                    wb_, kwb = wblk[bi % 2]
                    b_ = (2 * t + bi) % 3
                    for c in range(16):
                        P.op("pe", "matmul", reads=[kwb, f"{khT}_{t}"], writes=[bk(b_)], out=bank(b_), lhsT=hT[:, c, t * 128:(t + 1) * 128], rhs=wb_[:, c, 0:512], start=c == 0, stop=c == 15)
                    vs_ap, kvs = vstage[bi % 2]
                    P.op("act", "activation", reads=[bk(b_)], writes=[kvs], out=vs_ap, in_=bank(b_), func=AF.Copy)
                    h0 = 2 * (bi - 6)
                    P.dma("sp", Vd[h0:h0 + 2, :, t, :].rearrange("h p d -> p h d"), vs_ap.rearrange("p (h d) -> p h d", h=2),
                          reads=[kvs], writes=["Vd"])
                next(tg, None)
                next(tg, None)
            for _ in tg:
                pass
            P.barrier()

            _phase(2)
            rope_pend = []

            def rope_flush():
                while rope_pend:
                    rope_pend.pop(0)()

            def rope_tile(src_bank, npart, ctab, stab, kct, kst, swap, tb, dst, kdst, slot):
                tb_ap, ktb = tbf[slot]
                u_ap, ku = ropeu[slot]
                w_ap, kw = ropew[slot]
                sl = slice(tb * 512, (tb + 1) * 512)
                src = bank(src_bank)[0:npart, :]
                P.op("act", "activation", reads=[bk(src_bank)], writes=[ktb], out=tb_ap[0:npart, :], in_=src, func=AF.Copy)
                P.op("dve", "tensor_tensor", reads=[bk(src_bank), kct], writes=[ku], out=u_ap[0:npart, :], in0=src, in1=ctab[0:npart, sl], op=ALU.mult)
                rope_flush()

                def part2():
                    P.op("pe", "matmul", reads=[ktb, "cb"], writes=[bk(7)], out=bank(7)[0:npart, :], lhsT=swap[0:npart, 0:npart], rhs=tb_ap[0:npart, :], start=True, stop=True)
                    P.op("dve", "tensor_tensor", reads=[bk(7), kst], writes=[kw], out=w_ap[0:npart, :], in0=bank(7)[0:npart, :], in1=stab[0:npart, sl], op=ALU.mult)
                    P.op("dve", "tensor_tensor", reads=[ku, kw], writes=[kdst], out=dst, in0=u_ap[0:npart, :], in1=w_ap[0:npart, :], op=ALU.add)
                rope_pend.append(part2)

            load_blk(0)
            _phase(2.02)
            load_blk(1)
            _phase(2.05)
            (w0, kw0), (w1, kw1) = wblk[0], wblk[1]
            for tb in range(4):
                tsl = slice(tb * 512, (tb + 1) * 512)
                for ch in range(7):
                    wsrc, kws = (w0, kw0) if ch < 4 else (w1, kw1)
                    cofs = ch * 128 if ch < 4 else (ch - 4) * 128
                    m = 128 if ch < 6 else 64
                    if tb == 0 and ch == 1: _phase(2.06)
                    if tb == 0 and ch == 6: _phase(2.07)
                    for c in range(16):
                        P.op("pe", "matmul", reads=[kws] + [f"{khT}_{tt}" for tt in range(4 * tb, 4 * tb + 4)], writes=[bk(ch)], out=bank(ch), lhsT=wsrc[:, c, cofs:cofs + 128], rhs=hT[:, c, tsl], start=c == 0, stop=c == 15)
                if tb == 0: _phase(2.1)
                st_ap, kst_ = stage[tb % 2]
                rope_tile(6, 128, cosm, sinm, kcosm, ksinm, swm, tb, st_ap[:, 0:512], kst_, tb % 2)
                if tb == 0: _phase(2.2)
                rope_flush()
                P.dma("sp", KR[:, tsl], st_ap[0:64, 0:512], reads=[kst_], writes=["KR"])
                if tb == 0: _phase(2.3)
                for ch in range(6):
                    sq_ap, ksq = sqb[ch]
                    P.op("dve", "tensor_copy", reads=[bk(ch)], writes=[ktmpf], out=tmpf[:, ch, :], in_=bank(ch))
                    P.op("act", "activation", reads=[ktmpf], writes=[ksq], out=sq_ap, in_=tmpf[:, ch, :], func=AF.Square)
                if tb == 0: _phase(2.4)
                for ch in range(4):
                    P.op("pe", "matmul", reads=[sqb[ch][1], "cb"], writes=[bk(6)], out=bank(6), lhsT=ones, rhs=sqb[ch][0], start=ch == 0, stop=ch == 3)
                for ch in range(4, 6):
                    P.op("pe", "matmul", reads=[sqb[ch][1], "cb"], writes=[bk(7)], out=bank(7), lhsT=ones, rhs=sqb[ch][0], start=ch == 4, stop=ch == 5)
                if tb == 0: _phase(2.5)
                for j, (bnk, nf) in enumerate(((6, 512), (7, 256))):
                    ln_ap, kln = lnl[j]
                    rs_ap, krs = rsl[j]
                    P.op("act", "activation", reads=[bk(bnk), "sm_eps6"], writes=[kln], out=ln_ap, in_=bank(bnk), func=AF.Ln, bias=sm[:, EPS6:EPS6 + 1], scale=1.0 / nf)
                    P.op("act", "activation", reads=[kln], writes=[krs], out=rs_ap, in_=ln_ap, func=AF.Exp, scale=-0.5)
                if tb == 0: _phase(2.6)
                st2_ap, kst2 = stage[(tb + 1) % 2]
                st3 = st2_ap.rearrange("p (a b) -> p a b", a=4)
                for ch in range(4):
                    P.op("dve", "scalar_tensor_tensor", reads=[ktmpf, "cols", rsl[0][1]], writes=[kst2], out=st3[:, ch, :], in0=tmpf[:, ch, :], scalar=cols[:, C_GQ + ch:C_GQ + ch + 1], in1=rsl[0][0], op0=ALU.mult, op1=ALU.mult)
                P.dma("sp", CQ[:, :, tsl].rearrange("c p t -> p c t"), st3, reads=[kst2], writes=["CQ"])
                if tb == 0: _phase(2.7)
                vs_ap, kvs = vstage[0]
                vs_b, kvs_b = vstage[1]
                for j, (dst_, kd_) in enumerate(((vs_ap, kvs), (vs_b, kvs_b))):
                    ch = 4 + j
                    P.op("dve", "scalar_tensor_tensor", reads=[ktmpf, "cols", rsl[1][1]], writes=[kd_], out=dst_, in0=tmpf[:, ch, :], scalar=cols[:, C_GKV + j:C_GKV + j + 1], in1=rsl[1][0], op0=ALU.mult, op1=ALU.mult)
                    P.dma("sp", CKV[j, :, tsl], dst_, reads=[kd_], writes=["CKV"])

            _phase(3)
            nblk = len(blocks)
            for bi in range(2, 6):
                if bi == 2:
                    load_blk(2)
                if bi + 1 < 6:
                    load_blk(bi + 1)
                wb_, kwb = wblk[bi % 2]
                for j in range(4):
                    chunk = (bi - 2) * 4 + j
                    st_ap, kst_ = stage[chunk % 2]
                    for tb in range(4):
                        tsl = slice(tb * 512, (tb + 1) * 512)
                        sb_ = (chunk * 4 + tb) % 3
                        for c in range(16):
                            P.op("pe", "matmul", reads=[kwb] + [f"{khT}_{tt}" for tt in range(4 * tb, 4 * tb + 4)], writes=[bk(sb_)], out=bank(sb_), lhsT=wb_[:, c, j * 128:(j + 1) * 128], rhs=hT[:, c, tsl], start=c == 0, stop=c == 15)
                        rope_tile(sb_, 128, cosd, sind, kcosd, ksind, swd, tb, st_ap[:, tsl], kst_, (chunk * 4 + tb) % 2)
                    rope_flush()
                    P.dma("sp", QKd[chunk], st_ap, reads=[kst_], writes=["QKd"])
            _phase(5)
            P.barrier()
            AR.reset()
            cq, kcq = AR.bf(4, S)
            ckv, kckv = AR.bf(2, S)
            wuq, kwuq = AR.bf(4, 1536)
            wukv, kwukv = AR.bf(2, 2048)
            tbf = [AR.bf(512) for _ in range(2)]
            stage = [AR.bf(S) for _ in range(2)]
            vstage = [AR.bf(512) for _ in range(2)]
            cosm2, kcosm2 = AR.f32(S)
            sinm2, ksinm2 = AR.f32(S)
            ropeu = [AR.f32(512) for _ in range(2)]
            ropew = [AR.f32(512) for _ in range(2)]
            xt2 = [AR.f32(D) for _ in range(2)]
            tmpf, ktmpf = AR.f32(6, 512)
            posi = xt2[0][0].bitcast(I32)
            kposi = xt2[0][1]
            posf, kposf = xt2[1]
            ua, kua = tmpf.rearrange("p a b -> p (a b)")[:, 0:S], ktmpf
            P.dma("sp", posi, pos.partition_broadcast(128), writes=[kposi])
            P.op("dve", "tensor_copy", reads=[kposi], writes=[kposf], out=posf, in_=posi)
            make_tables(cosm2, kcosm2, sinm2, ksinm2, 1, 3)

            P.dma("sp", cq, CQ.rearrange("c p t -> p c t"), reads=["CQ"], writes=[kcq])
            P.dma("sp", ckv, CKV.rearrange("c p t -> p c t"), reads=["CKV"], writes=[kckv])
            load_w(wuq, kwuq, wview(w_uq, 0, 4, 0, 1536))
            load_w(wukv, kwukv, wview(w_ukv, 0, 2, 0, 2048))
            cnt = [0]

            def fm_proj(wt, kwt, ncc, col0, src, ksrc, tb, b_):
                tsl = slice(tb * 512, (tb + 1) * 512)
                for c in range(ncc):
                    P.op("pe", "matmul", reads=[kwt, ksrc], writes=[bk(b_)], out=bank(b_), lhsT=wt[:, c, col0:col0 + 128], rhs=src[:, c, tsl], start=c == 0, stop=c == ncc - 1)

            evn = [0]

            def evac(b_, dst, kdst):
                evn[0] += 1
                if evn[0] % 2:
                    P.op("act", "activation", reads=[bk(b_)], writes=[kdst], out=dst, in_=bank(b_), func=AF.Copy)
                else:
                    P.op("dve", "tensor_copy", reads=[bk(b_)], writes=[kdst], out=dst, in_=bank(b_))

            for h in range(8):
                st_ap, kst_ = stage[cnt[0] % 2]
                for tb in range(4):
                    b_ = (cnt[0] * 4 + tb) % 3
                    fm_proj(wuq, kwuq, 4, h * 128, cq, kcq, tb, b_)
                    evac(b_, st_ap[:, tb * 512:(tb + 1) * 512], kst_)
                P.dma("sp", QN[h], st_ap, reads=[kst_], writes=["QN"])
                cnt[0] += 1
            for i in range(4):
                st_ap, kst_ = stage[cnt[0] % 2]
                for tb in range(4):
                    b_ = (cnt[0] * 4 + tb) % 3
                    fm_proj(wuq, kwuq, 4, 1024 + i * 128, cq, kcq, tb, b_)
                    rope_tile(b_, 128, cosm2, sinm2, kcosm2, ksinm2, swm, tb, st_ap[:, tb * 512:(tb + 1) * 512], kst_, tb % 2)
                rope_flush()
                P.dma("sp", QR[i], st_ap, reads=[kst_], writes=["QR"])
                cnt[0] += 1
            for h in range(8):
                st_ap, kst_ = stage[cnt[0] % 2]
                for tb in range(4):
                    b_ = (cnt[0] * 4 + tb) % 3
                    fm_proj(wukv, kwukv, 2, h * 128, ckv, kckv, tb, b_)
                    evac(b_, st_ap[:, tb * 512:(tb + 1) * 512], kst_)
                P.dma("sp", KN[h], st_ap, reads=[kst_], writes=["KN"])
                cnt[0] += 1
            for t in range(NT):
                for half in range(2):
                    b_ = (t * 2 + half) % 3
                    for c in range(2):
                        P.op("pe", "matmul", reads=[kwukv, kckv], writes=[bk(b_)], out=bank(b_), lhsT=ckv[:, c, t * 128:(t + 1) * 128], rhs=wukv[:, c, 1024 + half * 512:1024 + (half + 1) * 512], start=c == 0, stop=c == 1)
                    vs_ap, kvs = vstage[(t * 2 + half) % 2]
                    evac(b_, vs_ap, kvs)
                    P.dma("sp", Vm[half * 4:half * 4 + 4, :, t, :].rearrange("h p d -> p h d"), vs_ap.rearrange("p (h d) -> p h d", h=4),
                          reads=[kvs], writes=["Vm"])

            attn_pend = []

            def attn_flush(kp=99):
                keep = []
                for st, fn in list(attn_pend):
                    if st <= kp:
                        fn()
                    else:
                        keep.append((st, fn))
                attn_pend[:] = keep

            def attention(qv, kq, kv_, kk, nkt, vfun, kvv, ndc, softmaxes, scale, q0, tail, PT):
                nkp = nkt // 2
                for si, chunks in enumerate(softmaxes):
                    def S_(kp):
                        for kt in (2 * kp, 2 * kp + 1):
                            b_ = (kp % 2) * 2 + (kt % 2)
                            for ci, (ch, K) in enumerate(chunks):
                                P.op("pe", "matmul", reads=[kk, kq], writes=[bk(b_)], out=bank(b_), lhsT=kv_[0:K, ch, kt * 128:(kt + 1) * 128], rhs=qv[0:K, ch, q0:q0 + 512], start=ci == 0, stop=ci == len(chunks) - 1)

                    def E_(kp):
                        pt_ap, kpt = PT[kp % 3]
                        b0 = (kp % 2) * 2
                        src = psA[:, b0 * 512:(b0 + 2) * 512]
                        P.op("act", "activation", reads=[bk(b0), bk(b0 + 1)], writes=[kpt], out=pt_ap, in_=src, func=AF.Exp, scale=scale)

                    def PV_(kp):
                        pt_ap, kpt = PT[kp % 3]
                        for kt in (2 * kp, 2 * kp + 1):
                            rhs = pt_ap[:, (kt % 2) * 512:(kt % 2 + 1) * 512]
                            for dc in range(ndc):
                                P.op("pe", "matmul", reads=[kvv, kpt], writes=[bk(4 + dc)], out=bank(4 + dc), lhsT=vfun(kt, dc), rhs=rhs, start=kt == 0, stop=kt == nkt - 1)
                            P.op("pe", "matmul", reads=["cb", kpt], writes=[bk(6)], out=bank(6), lhsT=ones, rhs=rhs, start=kt == 0, stop=kt == nkt - 1)

                    S_(0)
                    E_(0)
                    for kp in range(1, nkp):
                        S_(kp)
                        E_(kp)
                        PV_(kp - 1)
                        attn_flush(kp)
                    PV_(nkp - 1)
                    tail(si)

            def run_softmax_pipeline(descs, PT):
                def S_(d, kp):
                    for kt in (2 * kp, 2 * kp + 1):
                        b_ = (kp % 2) * 2 + (kt % 2)
                        ch_ = d["chunks"]
                        for ci, (ch, K) in enumerate(ch_):
                            P.op("pe", "matmul", reads=[d["kk"], d["kq"]], writes=[bk(b_)], out=bank(b_), lhsT=d["kv"][0:K, ch, kt * 128:(kt + 1) * 128], rhs=d["qv"][0:K, ch, d["q0"]:d["q0"] + 512], start=ci == 0, stop=ci == len(ch_) - 1)

                def E_(d, kp):
                    pt_ap, kpt = PT[kp % 3]
                    b0 = (kp % 2) * 2
                    P.op("act", "activation", reads=[bk(b0), bk(b0 + 1)], writes=[kpt], out=pt_ap, in_=psA[:, b0 * 512:(b0 + 2) * 512], func=AF.Exp, scale=d["scale"])

                def PV_(d, kp):
                    pt_ap, kpt = PT[kp % 3]
                    nkt = 16
                    for kt in (2 * kp, 2 * kp + 1):
                        rhs = pt_ap[:, (kt % 2) * 512:(kt % 2 + 1) * 512]
                        for dc in range(d["ndc"]):
                            P.op("pe", "matmul", reads=[d["kvv"], kpt], writes=[bk(4 + dc)], out=bank(4 + dc), lhsT=d["vfun"](kt, dc), rhs=rhs, start=kt == 0, stop=kt == nkt - 1)
                        P.op("pe", "matmul", reads=["cb", kpt], writes=[bk(6)], out=bank(6), lhsT=ones, rhs=rhs, start=kt == 0, stop=kt == nkt - 1)

                for i, d in enumerate(descs):
                    if d["pre"] is not None:
                        d["pre"]()
                    if i == 0:
                        S_(d, 0)
                        E_(d, 0)
                    for kp in range(1, 8):
                        S_(d, kp)
                        E_(d, kp)
                        PV_(d, kp - 1)
                        attn_flush(kp)
                    if i + 1 < len(descs):
                        S_(descs[i + 1], 0)
                        E_(descs[i + 1], 0)
                    PV_(d, 7)
                    d["tail"]()

            _phase(6)
            P.barrier()
            AR.reset()
            qb2 = [AR.bf(2, S) for _ in range(2)]
            kb2 = [AR.bf(2, S) for _ in range(2)]
            vb2 = [AR.bf(16, 256) for _ in range(2)]
            PT = [AR.bf(1024) for _ in range(3)]
            ostg = [AR.bf(2, 512) for _ in range(2)]
            sqd = [AR.bf(2, 512) for _ in range(2)]
            rden = [AR.f32(512) for _ in range(2)]
            o1n = [AR.f32(2, 512) for _ in range(2)]
            ot = [AR.f32(2, 512) for _ in range(2)]
            lnd = [AR.f32(512) for _ in range(2)]
            rsd = [AR.f32(512) for _ in range(2)]
            wo, kwo = AR.bf(16, D)
            for g in range(4):
                load_w(wo[:, 4 * g:4 * g + 4, :], f"{kwo}_{g}", wview(w_out, 4 * g * 128, 4, 0, D))
            kwo_all = [f"{kwo}_{g}" for g in range(4)]

            jobs = [("d", h) for h in range(4)] + [("m", h) for h in range(8)]

            def load_job(ji):
                kind, h = jobs[ji]
                (q_, kq_), (k_, kk_), (v_, kv2) = qb2[ji % 2], kb2[ji % 2], vb2[ji % 2]
                if kind == "d":
                    for j in range(2):
                        P.dma("sp", q_[:, j, :], QKd[2 * h + j], reads=["QKd"], writes=[kq_])
                        P.dma("sp", k_[:, j, :], QKd[8 + 2 * h + j], reads=["QKd"], writes=[kk_])
                    P.dma("sp", v_, Vd[h], reads=["Vd"], writes=[kv2])
                else:
                    P.dma("sp", q_[:, 0, :], QN[h], reads=["QN"], writes=[kq_])
                    P.dma("sp", q_[0:64, 1, :], QR[h // 2, (h % 2) * 64:(h % 2) * 64 + 64, :], reads=["QR"], writes=[kq_])
                    P.dma("sp", k_[:, 0, :], KN[h], reads=["KN"], writes=[kk_])
                    P.dma("sp", k_[0:64, 1, :], KR, reads=["KR"], writes=[kk_])
                    P.dma("sp", v_[:, :, 0:128], Vm[h], reads=["Vm"], writes=[kv2])

            tcount = [0]
            descs = []
            load_job(0)
            for ji, (kind, h) in enumerate(jobs):
                (q_, kq_), (k_, kk_), (v_, kv2) = qb2[ji % 2], kb2[ji % 2], vb2[ji % 2]
                for qi in range(4):
                    q0 = qi * 512
                    pre = (lambda ji=ji: load_job(ji + 1)) if (qi == 0 and ji + 1 < len(jobs)) else None
                    if kind == "m":
                        def tail_m(si, h=h, q0=q0):
                            s_ = tcount[0] % 2
                            tcount[0] += 1
                            rd, krd = rden[s_]
                            o2, ko2 = ot[s_]
                            og, kog = ostg[s_]
                            P.op("act", "activation", reads=[bk(6)], writes=[krd], out=rd, in_=bank(6), func=AF.Copy)
                            P.op("dve", "tensor_copy", reads=[bk(4)], writes=[ko2], out=o2[:, 0, :], in_=bank(4))

                            def later():
                                P.op("dve", "reciprocal", reads=[krd], writes=[krd], out=rd, in_=rd)
                                P.op("dve", "tensor_tensor", reads=[ko2, krd], writes=[kog], out=og[:, 0, :], in0=o2[:, 0, :], in1=rd, op=ALU.mult)
                                P.dma("sp", MIX[h, :, q0:q0 + 512], og[:, 0, :], reads=[kog], writes=["MIX"])
                            attn_pend.append((2, later))
                        descs.append(dict(pre=pre, qv=q_, kq=kq_, kv=k_, kk=kk_, vfun=(lambda kt, dc, v_=v_: v_[:, kt, 0:128]), kvv=kv2, ndc=1,
                                          chunks=[(0, 128), (1, 64)], scale=192.0 ** -0.5, q0=q0, tail=(lambda tail_m=tail_m: tail_m(0))))
                    else:
                        s_ = tcount[0] % 2
                        tcount[0] += 1

                        def tail_d(si, h=h, q0=q0, s_=s_):
                            rd, krd = rden[si]
                            o1, ko1 = o1n[s_]
                            o2, ko2 = ot[s_]
                            dst, kdst = (o1, ko1) if si == 0 else (o2, ko2)
                            P.op("act", "activation", reads=[bk(6)], writes=[krd], out=rd, in_=bank(6), func=AF.Copy)
                            P.op("dve", "tensor_copy", reads=[bk(4)], writes=[kdst], out=dst[:, 0, :], in_=bank(4))
                            P.op("dve", "tensor_copy", reads=[bk(5)], writes=[kdst], out=dst[:, 1, :], in_=bank(5))

                            def norm_part():
                                P.op("dve", "reciprocal", reads=[krd], writes=[krd], out=rd, in_=rd)
                                for dc in range(2):
                                    P.op("dve", "tensor_tensor", reads=[kdst, krd], writes=[kdst], out=dst[:, dc, :], in0=dst[:, dc, :], in1=rd, op=ALU.mult)
                                if si == 1:
                                    for dc in range(2):
                                        P.op("dve", "scalar_tensor_tensor", reads=[ko2, ko1, "sm_neglam"], writes=[ko2], out=o2[:, dc, :], in0=o2[:, dc, :], scalar=sm[:, NEGLAM:NEGLAM + 1], in1=o1[:, dc, :], op0=ALU.mult, op1=ALU.add)
                            attn_pend.append((2, norm_part))
                            if si == 0:
                                return
                            sq_, ksq_ = sqd[s_]
                            og, kog = ostg[s_]
                            ln_, kln_ = lnd[s_]
                            rs_, krs_ = rsd[s_]

                            def sq_part():
                                for dc in range(2):
                                    P.op("act", "activation", reads=[ko2], writes=[ksq_], out=sq_[:, dc, :], in_=o2[:, dc, :], func=AF.Square)
                                for dc in range(2):
                                    P.op("pe", "matmul", reads=[ksq_, "cb"], writes=[bk(7)], out=bank(7), lhsT=ones, rhs=sq_[:, dc, :], start=dc == 0, stop=dc == 1)

                            def fin_part():
                                P.op("act", "activation", reads=[bk(7), "sm_eps5"], writes=[kln_], out=ln_, in_=bank(7), func=AF.Ln, bias=sm[:, EPS5:EPS5 + 1], scale=1.0 / 256)
                                P.op("act", "activation", reads=[kln_], writes=[krs_], out=rs_, in_=ln_, func=AF.Exp, scale=-0.5)
                                for dc in range(2):
                                    P.op("dve", "scalar_tensor_tensor", reads=[ko2, krs_, "sm_gs"], writes=[kog], out=og[:, dc, :], in0=o2[:, dc, :], scalar=sm[:, GS0 + dc:GS0 + dc + 1], in1=rs_, op0=ALU.mult, op1=ALU.mult)
                                P.dma("sp", MIX[8 + 2 * h:8 + 2 * h + 2, :, q0:q0 + 512].rearrange("c p t -> p c t"), og, reads=[kog], writes=["MIX"])
                            attn_pend.append((4, sq_part))
                            attn_pend.append((6, fin_part))
                        for si in range(2):
                            descs.append(dict(pre=(pre if si == 0 else None), qv=q_, kq=kq_, kv=k_, kk=kk_,
                                              vfun=(lambda kt, dc, v_=v_: v_[:, kt, dc * 128:(dc + 1) * 128]), kvv=kv2, ndc=2,
                                              chunks=[(si, 128)], scale=128.0 ** -0.5, q0=q0, tail=(lambda tail_d=tail_d, si=si: tail_d(si))))

            run_softmax_pipeline(descs, PT)
            attn_flush()
            _phase(7)
            P.barrier()
            AR.reset()
            mixT, kmix = AR.bf(16, S)
            xt2 = [AR.f32(D) for _ in range(2)]
            for g in range(4):
                P.dma("sp", mixT[:, 4 * g:4 * g + 4, :], MIX[4 * g:4 * g + 4].rearrange("c p t -> p c t"), reads=["MIX"], writes=[f"{kmix}_{g}"])
            kmix_all = [f"{kmix}_{g}" for g in range(4)]
            P.dma("sp", xt2[0][0], x[0:128, :], writes=[xt2[0][1]])
            for t in range(NT):
                if t + 1 < NT:
                    P.dma("sp", xt2[(t + 1) % 2][0], x[(t + 1) * 128:(t + 2) * 128, :], writes=[xt2[(t + 1) % 2][1]])
                pst, pk = banks4(t % 2)
                for n in range(4):
                    for c in range(16):
                        P.op("pe", "matmul", reads=[kmix_all[c // 4], kwo_all[c // 4]], writes=[pk[n]], out=pst[:, n * 512:(n + 1) * 512], lhsT=mixT[:, c, t * 128:(t + 1) * 128], rhs=wo[:, c, n * 512:(n + 1) * 512], start=c == 0, stop=c == 15)
                xs, kxs = xt2[t % 2]
                P.op("dve", "tensor_tensor", reads=pk + [kxs], writes=[kxs], out=xs, in0=pst[:, :], in1=xs, op=ALU.add)
                P.dma("sp", X1[t * 128:(t + 1) * 128, :], xs, reads=[kxs], writes=["X1"])

            _phase(8)
            P.barrier()
            AR.reset()
            wxa, kwxa = AR.bf(16, 512)
            wxo, kwxo = AR.bf(4, D)
            xkT, kxk = AR.bf(4, MEM)
            xv, kxv = AR.bf(2, 512)
            hxT, khx = AR.bf(16, 512)
            xqT, kxq = AR.bf(4, 512)
            xoT, kxo = AR.bf(4, 512)
            hfs, khfs = AR.bf(16, 512)
            _pt = AR.bf(1024)
            PT = [_pt, _pt, _pt]
            xnb4 = [AR.bf(D) for _ in range(4)]
            xnb2 = xnb4[0:2]
            x1b2 = [AR.f32(4, D) for _ in range(2)]
            rden4 = [AR.f32(512) for _ in range(4)]
            xof = [AR.f32(512) for _ in range(4)]
            xt2 = [(x1b2[1][0][:, 0, :], x1b2[1][1] + "_0")]
            mark_p2 = AR.off
            hmT, khm = AR.bf(16, MEM)
            wxb, kwxb = AR.bf(16, 512)

            load_w(wxa, kwxa, wview(w_xk, 0, 16, 0, 512))
            load_w(wxb, kwxb, wview(w_xv, 0, 16, 0, 512))
            load_w(wxo, kwxo, wview(w_xo, 0, 4, 0, D))
            for mt in range(2):
                xs, kxs = xt2[0]
                P.dma("sp", xs, mem[mt * 128:(mt + 1) * 128, :], writes=[kxs])
                xn, kxn = xnb2[mt % 2]
                norm_to_T(xs, kxs, xn, kxn, C_GMEM, hmT[:, :, mt * 128:(mt + 1) * 128], khm)
            for h in range(4):
                for c in range(16):
                    P.op("pe", "matmul", reads=[kwxa, khm], writes=[bk(h % 2)], out=bank(h % 2)[:, 0:MEM], lhsT=wxa[:, c, h * 128:(h + 1) * 128], rhs=hmT[:, c, :], start=c == 0, stop=c == 15)
                P.op("act", "activation", reads=[bk(h % 2)], writes=[kxk], out=xkT[:, h, :], in_=bank(h % 2)[:, 0:MEM], func=AF.Copy)
            for mt in range(2):
                for c in range(16):
                    P.op("pe", "matmul", reads=[kwxb, khm], writes=[bk(2 + mt)], out=bank(2 + mt), lhsT=hmT[:, c, mt * 128:(mt + 1) * 128], rhs=wxb[:, c, :], start=c == 0, stop=c == 15)
                P.op("act", "activation", reads=[bk(2 + mt)], writes=[kxv], out=xv[:, mt, :], in_=bank(2 + mt), func=AF.Copy)
            load_w(wxa, kwxa, wview(w_xq, 0, 16, 0, 512))

            P.barrier()
            AR.off = mark_p2
            xnbh = [AR.bf(D) for _ in range(3)]
            XN_ENG = "dve"

            def A1(tb):
                xb, kxb = x1b2[tb % 2]
                for i in range(4):
                    t = tb * 4 + i
                    P.dma("sp", xb[:, i, :], X1[t * 128:(t + 1) * 128, :], reads=["X1"], writes=[f"{kxb}_{i}"])
                for i in range(4):
                    norm_s1(xb[:, i, :], f"{kxb}_{i}", xnb4[i][0], xnb4[i][1])

            def A2(tb, i, b0=None):
                norm_s2(xnb4[i][0], xnb4[i][1], C_GX, hxT[:, :, i * 128:(i + 1) * 128], khx, b0=b0)

            def XQ(h):
                for c in range(16):
                    P.op("pe", "matmul", reads=[kwxa, khx], writes=[bk(h % 2)], out=bank(h % 2), lhsT=wxa[:, c, h * 128:(h + 1) * 128], rhs=hxT[:, c, :], start=c == 0, stop=c == 15)
                P.op("act", "activation", reads=[bk(h % 2)], writes=[kxq], out=xqT[:, h, :], in_=bank(h % 2), func=AF.Copy)

            A1(0)
            for i in range(4):
                A2(0, i)
            for h in range(4):
                XQ(h)
            for tb in range(4):
                xb, kxb = x1b2[tb % 2]
                for h in range(4):
                    def tail_x(si, h=h):
                        rd, krd = rden4[h]
                        xo_, kxo_ = xof[h]
                        P.op("act", "activation", reads=[bk(6)], writes=[krd], out=rd, in_=bank(6), func=AF.Copy)
                        P.op("dve", "tensor_copy", reads=[bk(4)], writes=[kxo_], out=xo_, in_=bank(4))

                        def later(h=h, rd=rd, krd=krd, xo_=xo_, kxo_=kxo_):
                            P.op("act", "activation", reads=[krd], writes=[krd], out=rd, in_=rd, func=AF.Ln)
                            P.op("act", "activation", reads=[krd], writes=[krd], out=rd, in_=rd, func=AF.Exp, scale=-1.0)
                            P.op("dve", "tensor_tensor", reads=[kxo_, krd], writes=[f"{kxo}_{h}"], out=xoT[:, h, :], in0=xo_, in1=rd, op=ALU.mult)
                        attn_pend.append((99, later))
                    attention(xqT, kxq, xkT, kxk, 2, lambda kt, dc, h=h: xv[:, kt, h * 128:(h + 1) * 128], kxv, 1,
                              [[(h, 128)]], 128.0 ** -0.5, 0, tail_x, PT)
                attn_flush()
                if tb + 1 < 4:
                    A1(tb + 1)
                for i in range(4):
                    t = tb * 4 + i
                    pst, pk = banks4(i % 2)
                    for n in range(4):
                        for h in range(4):
                            P.op("pe", "matmul", reads=[f"{kxo}_{h}", kwxo], writes=[pk[n]], out=pst[:, n * 512:(n + 1) * 512], lhsT=xoT[:, h, i * 128:(i + 1) * 128], rhs=wxo[:, h, n * 512:(n + 1) * 512], start=h == 0, stop=h == 3)
                    if tb + 1 < 4:
                        A2(tb + 1, i, b0=4 * ((i + 1) % 2))
                    xs = xb[:, i, :]
                    kxs = f"{kxb}_{i}"
                    P.op("dve", "tensor_tensor", reads=pk + [kxs], writes=[kxs], out=xs, in0=pst[:, :], in1=xs, op=ALU.add)
                    P.dma("sp", X2[t * 128:(t + 1) * 128, :], xs, reads=[kxs], writes=["X2"])
                    norm_s1(xs, kxs, xnbh[i % 3][0], xnbh[i % 3][1], eng=XN_ENG)
                    if i >= 2:
                        j = i - 2
                        norm_s2(xnbh[j % 3][0], xnbh[j % 3][1], C_GFFN, hfs[:, :, j * 128:(j + 1) * 128], khfs, b0=4 * (i % 2) + 2)
                if tb + 1 < 4:
                    XQ(0)
                    XQ(1)
                norm_s2(xnbh[2][0], xnbh[2][1], C_GFFN, hfs[:, :, 2 * 128:3 * 128], khfs)
                if tb + 1 < 4:
                    XQ(2)
                norm_s2(xnbh[0][0], xnbh[0][1], C_GFFN, hfs[:, :, 3 * 128:4 * 128], khfs)
                if tb + 1 < 4:
                    XQ(3)
                P.dma("sp", HF[:, :, tb * 512:(tb + 1) * 512].rearrange("c p t -> p c t"), hfs, reads=[khfs], writes=["HF"])

            _phase(9)
            def final_norm(t, xs, kxs, junk, kjunk, gfin, kgf):
                P.dma("sp", xs, X2[t * 128:(t + 1) * 128, :], reads=["X2"], writes=[kxs])
                r, kr = rstd_cols(xs, junk, D, EPS6, [kxs], [kjunk])
                P.op("dve", "scalar_tensor_tensor", reads=[kxs, kr, kgf], writes=[kxs], out=xs, in0=xs, scalar=r, in1=gfin, op0=ALU.mult, op1=ALU.mult)
                P.dma("sp", y[t * 128:(t + 1) * 128, :], xs, reads=[kxs], writes=["y"])

            TB = 1024
            for ps_ in range(2):
                P.barrier()
                AR.reset()
                actT, kact = AR.bf(NFC, TB)
                fn_x = [AR.f32(D) for _ in range(2)]
                gfin, kgf = AR.f32(D)
                fn_junk = [AR.bf(D) for _ in range(1)]
                xblk = [AR.f32(512) for _ in range(8)]
                mark = AR.off
                hf, khf = AR.bf(16, TB)
                wg = [AR.bf(16, 256) for _ in range(2)]
                wu = [AR.bf(16, 256) for _ in range(2)]
                sgb = [AR.bf(512) for _ in range(2)]
                P.dma("sp", gfin, gfin_d.partition_broadcast(128), writes=[kgf])
                P.dma("sp", hf, HF[:, :, ps_ * TB:(ps_ + 1) * TB].rearrange("c p t -> p c t"), reads=["HF"], writes=[khf])

                def load_gu(fb):
                    load_w(wg[fb % 2][0], wg[fb % 2][1], wview(w_gate, 0, 16, fb * 256, 256))
                    load_w(wu[fb % 2][0], wu[fb % 2][1], wview(w_up, 0, 16, fb * 256, 256))
                load_gu(0)
                pend_norm = list(range((ps_ - 1) * 8, ps_ * 8)) if ps_ > 0 else []
                for fb in range(22):
                    if fb + 1 < 22:
                        load_gu(fb + 1)
                    (wg_, kwg), (wu_, kwu) = wg[fb % 2], wu[fb % 2]
                    for fc in range(2):
                        f = fb * 2 + fc
                        for tk in range(2):
                            it = (f * 2 + tk) % 2
                            bg, bu = 2 * it, 2 * it + 1
                            tsl = slice(tk * 512, (tk + 1) * 512)
                            for c in range(16):
                                P.op("pe", "matmul", reads=[kwg, khf], writes=[bk(bg)], out=bank(bg), lhsT=wg_[:, c, fc * 128:(fc + 1) * 128], rhs=hf[:, c, tsl], start=c == 0, stop=c == 15)
                            for c in range(16):
                                P.op("pe", "matmul", reads=[kwu, khf], writes=[bk(bu)], out=bank(bu), lhsT=wu_[:, c, fc * 128:(fc + 1) * 128], rhs=hf[:, c, tsl], start=c == 0, stop=c == 15)
                            sg_, ksg = sgb[it]
                            P.op("act", "activation", reads=[bk(bg)], writes=[ksg], out=sg_, in_=bank(bg), func=AF.Silu)
                            P.op("dve", "tensor_tensor", reads=[bk(bu), ksg], writes=[f"{kact}_{f}"], out=actT[:, f, tsl], in0=bank(bu), in1=sg_, op=ALU.mult)
                    if pend_norm and fb % 2 == 1:
                        t = pend_norm.pop(0)
                        final_norm(t, fn_x[t % 2][0], fn_x[t % 2][1], fn_junk[0][0], fn_junk[0][1], gfin, kgf)
                P.barrier()
                AR.off = mark
                wd = [AR.bf(11, 512) for _ in range(6)]

                wd_next = [0]

                def load_wd_upto(gmax):
                    while wd_next[0] < min(gmax, 16):
                        gi = wd_next[0]
                        n_, g = gi // 4, gi % 4
                        d_, kd_ = wd[gi % 6]
                        load_w(d_, kd_, wview(w_down, g * 11 * 128, 11, n_ * 512, 512))
                        wd_next[0] += 1
                kact_all = [f"{kact}_{f}" for f in range(NFC)]
                for n in range(4):
                    for g in range(4):
                        gi = n * 4 + g
                        load_wd_upto(gi + 6)
                        d_, kd_ = wd[gi % 6]
                        for t8 in range(8):
                            t = ps_ * 8 + t8
                            xb_, kxb = xblk[t8]
                            if g == 0:
                                P.dma("sp", xb_, X2[t * 128:(t + 1) * 128, n * 512:(n + 1) * 512], reads=["X2"], writes=[kxb])
                            for fi in range(11):
                                f = g * 11 + fi
                                P.op("pe", "matmul", reads=[kact_all[f], kd_], writes=[bk(t8)], out=bank(t8), lhsT=actT[:, f, t8 * 128:(t8 + 1) * 128], rhs=d_[:, fi, :], start=f == 0, stop=f == NFC - 1)
                    for t8 in range(8):
                        t = ps_ * 8 + t8
                        xb_, kxb = xblk[t8]
                        P.op("dve", "tensor_tensor", reads=[bk(t8), kxb], writes=[kxb], out=xb_, in0=bank(t8), in1=xb_, op=ALU.add)
                        P.dma("sp", X2[t * 128:(t + 1) * 128, n * 512:(n + 1) * 512], xb_, reads=[kxb], writes=["X2"])
                        if ps_ == 1 and n == 3:
                            final_norm(t, fn_x[t % 2][0], fn_x[t % 2][1], fn_junk[0][0], fn_junk[0][1], gfin, kgf)


        except _Stop:
            pass

        need = P.finalize()
        sems = {"eng": {}, "dma": {}}
        for e in ENGS:
            sems["eng"][e] = [nc.alloc_semaphore(name=f"s_{e}_{i}") for i in range(need[e])]
        for q in ("sp", "pool"):
            sems["dma"][q] = [nc.alloc_semaphore(name=f"d_{q}_{i}") for i in range(NDMASEM)]
        with nc.Block() as block:
            @block.tensor
            def _(e):
                P.run("pe", e, sems)

            @block.scalar
            def _(e):
                P.run("act", e, sems)

            @block.vector
            def _(e):
                P.run("dve", e, sems)

            @block.gpsimd
            def _(e):
                P.run("pool", e, sems)

            @block.sync
            def _(e):
                P.run("sp", e, sems)
    return nc


_NC = [None]


def _host_consts():
    p = np.arange(128)
    cf = np.zeros((128, 4), np.float32)
    cf[:, 0] = (10000.0 ** (-(2.0 * (p % 64)) / 128.0)) / (2 * math.pi)
    cf[:, 1] = (10000.0 ** (-(2.0 * (p % 32)) / 64.0)) / (2 * math.pi)
    cf[:, 2] = np.where(p < 64, -1.0, 1.0)
    cf[:, 3] = np.where((p % 64) < 32, -1.0, 1.0)
    cb = np.zeros((128, 512), np.float32)
    cb[:, 0:128] = np.eye(128)
    cb[:, 128:256] = 1.0
    m = np.arange(128)
    cb[(m + 64) % 128, 256 + m] = 1.0
    src = np.where((m % 64) < 32, m + 32, m - 32)
    cb[src, 384 + m] = 1.0
    return cf, cb


def kernel(x, mem, positions, g_mix, w_in, g_q_lat, w_uq, g_kv_lat, w_ukv,
           lambda_q1, lambda_k1, lambda_q2, lambda_k2, g_diff_sub, w_out,
           g_xattn, g_mem, w_xq, w_xk, w_xv, w_xo,
           g_ffn, w_gate, w_up, w_down, g_final):
    f32 = lambda a: np.ascontiguousarray(np.asarray(a), dtype=np.float32)
    if _NC[0] is None:
        _NC[0] = build_program()
    nc = _NC[0]
    cf, cb = _host_consts()

    def colT(g, n):
        return f32(g).reshape(n, 128).T

    cols = np.concatenate([colT(g_mix[0], 16), colT(g_xattn[0], 16), colT(g_mem[0], 16), colT(g_ffn[0], 16),
                           colT(g_q_lat[0], 4), colT(g_kv_lat[0], 2), colT(g_diff_sub[0], 2)], axis=1)
    cols = np.ascontiguousarray(cols, dtype=np.float32)
    lamv = np.ascontiguousarray(np.stack([f32(lambda_q1[0]), f32(lambda_k1[0]), f32(lambda_q2[0]), f32(lambda_k2[0])]))
    wuq = f32(w_uq[0]).reshape(512, 8, 192)
    wuq_p = np.ascontiguousarray(np.concatenate([wuq[:, :, :128].reshape(512, 1024), wuq[:, :, 128:].reshape(512, 512)], axis=1))
    wukv = f32(w_ukv[0]).reshape(256, 8, 256)
    wukv_p = np.ascontiguousarray(np.concatenate([wukv[:, :, :128].reshape(256, 1024), wukv[:, :, 128:].reshape(256, 1024)], axis=1))
    shared = {
        "w_in": f32(w_in[0]), "w_uq": wuq_p, "w_ukv": wukv_p, "w_out": f32(w_out[0]),
        "w_xq": f32(w_xq[0]), "w_xk": f32(w_xk[0]), "w_xv": f32(w_xv[0]), "w_xo": f32(w_xo[0]),
        "w_gate": f32(w_gate[0]), "w_up": f32(w_up[0]), "w_down": f32(w_down[0]),
        "cols": cols, "lamv": lamv, "gfin": f32(g_final).reshape(1, D), "cf": cf, "cb": cb,
    }
    xs = f32(x)
    ms = f32(mem)
    ps = np.ascontiguousarray(np.asarray(positions), dtype=np.int32)
    in_maps = []
    for b in range(8):
        d = dict(shared)
        d["x"] = xs[b]
        d["mem"] = ms[b]
        d["pos"] = ps[b:b + 1]
        in_maps.append(d)
    res = run_bass_kernel_spmd(nc, in_maps, core_ids=list(range(8)))
    return np.stack([np.asarray(r["y"], dtype=np.float32) for r in res.results], axis=0)
```

```python
import math
import contextlib
import numpy as np
import concourse.bass as bass
import concourse.mybir as mybir
from concourse.bass_utils import run_bass_kernel_spmd

F32 = mybir.dt.float32
BF16 = mybir.dt.bfloat16
I32 = mybir.dt.int32
AF = mybir.ActivationFunctionType
ALU = mybir.AluOpType
AX = mybir.AxisListType

S = 2048
D = 2048
NT = 16
NC16 = 16
FF = 5632
NFC = 44
MEM = 256
IN_W = 3904
EPS = 1e-6

ENGS = ["pe", "act", "dve", "pool", "sp"]
EPOCH = 2000
NDMASEM = 16


class Prog:
    def __init__(self):
        self.ops = {e: [] for e in ENGS}
        self.res = {}
        self.dma_n = {"sp": 0, "pool": 0}

    @staticmethod
    def _joint(r):
        return r is not None and not r[1] and r[0] and all(t[0] == "d" for t in r[0]) and len(r[0]) < 48

    def _deps(self, reads, writes, is_dma=False):
        toks = []
        for k in reads:
            r = self.res.get(k)
            if r:
                toks.extend(r[0])
        for k in writes:
            r = self.res.get(k)
            if r:
                if is_dma and self._joint(r):
                    toks.extend(r[2])
                else:
                    toks.extend(r[0])
                    toks.extend(r[1].values())
        return toks

    def _commit(self, tok, reads, writes):
        for k in reads:
            if k in writes:
                continue
            r = self.res.setdefault(k, [[], {}, []])
            if tok[0] == "e":
                r[1][tok[1]] = tok
            else:
                r[1][(tok[1], tok[2] % NDMASEM)] = tok
        for k in writes:
            r = self.res.get(k)
            if tok[0] == "d" and self._joint(r):
                r[0].append(tok)
            else:
                inherited = (list(r[0]) + list(r[1].values())) if (r and tok[0] == "d") else []
                self.res[k] = [[tok], {}, inherited]

    def op(self, eng, meth, reads=(), writes=(), **kw):
        if eng != "pe":
            psr = [k for k in reads if k.startswith("ps") and k not in writes]
            if psr:
                writes = list(writes) + psr
        toks = self._deps(reads, writes)
        idx = len(self.ops[eng])
        self.ops[eng].append({"waits": toks, "fn": (lambda e, m=meth, kw=kw: getattr(e, m)(**kw)), "sig": False})
        self._commit(("e", eng, idx), reads, writes)

    def dma(self, q, out, in_, reads=(), writes=()):
        toks = self._deps(reads, writes, is_dma=True)
        k = self.dma_n[q]
        self.dma_n[q] += 1
        if k >= NDMASEM:
            toks.append(("d", q, k - NDMASEM))
        self.ops[q].append({"waits": toks, "fn": (lambda e, o=out, i=in_: e.dma_start(out=o, in_=i)),
                            "sig": False, "dma": k})
        self._commit(("d", q, k), reads, writes)

    def barrier(self):
        toks = []
        for e in ENGS:
            for i in range(len(self.ops[e]) - 1, -1, -1):
                o = self.ops[e][i]
                if o["fn"] is not None and "dma" not in o:
                    toks.append(("e", e, i))
                    break
        for q in ("sp", "pool"):
            n = self.dma_n[q]
            for k in range(max(0, n - NDMASEM), n):
                toks.append(("d", q, k))
        for e in ENGS:
            self.ops[e].append({"waits": list(toks), "fn": None, "sig": False})

    def finalize(self):
        for e in ENGS:
            for o in self.ops[e]:
                for t in o["waits"]:
                    if t[0] == "e" and not (t[1] == e and e in ("pe", "sp")):
                        self.ops[t[1]][t[2]]["sig"] = True
        self.cnt = {}
        for e in ENGS:
            c = 0
            for o in self.ops[e]:
                if o["sig"]:
                    c += 1
                    o["cnt"] = c
            self.cnt[e] = c
        return {e: max(1, (self.cnt[e] + EPOCH - 1) // EPOCH) for e in ENGS}

    def run(self, e, handle, sems):
        seen_e = {}
        seen_d = {}
        for o in self.ops[e]:
            for t in o["waits"]:
                if t[0] == "e":
                    _, pe_, idx = t
                    if pe_ == e and e in ("pe", "sp"):
                        continue
                    if seen_e.get(pe_, -1) >= idx:
                        continue
                    seen_e[pe_] = idx
                    c = self.ops[pe_][idx]["cnt"]
                    handle.wait_ge(sems["eng"][pe_][(c - 1) // EPOCH], (c - 1) % EPOCH + 1)
                else:
                    _, q, k = t
                    slot = k % NDMASEM
                    val = 16 * (k // NDMASEM + 1)
                    if seen_d.get((q, slot), 0) >= val:
                        continue
                    seen_d[(q, slot)] = val
                    handle.wait_ge(sems["dma"][q][slot], val)
            if o["fn"] is None:
                continue
            inst = o["fn"](handle)
            if "dma" in o:
                inst.then_inc(sems["dma"][e][o["dma"] % NDMASEM], 16)
            elif o["sig"]:
                c = o["cnt"]
                inst.then_inc(sems["eng"][e][(c - 1) // EPOCH], 1)
        if e == "sp":
            for q in ("sp", "pool"):
                n = self.dma_n[q]
                for k in range(max(0, n - NDMASEM), n):
                    handle.wait_ge(sems["dma"][q][k % NDMASEM], 16 * (k // NDMASEM + 1))


class _Stop(Exception):
    pass


_STOP = [99]


def _phase(n):
    if n > _STOP[0]:
        raise _Stop()


class Arena:
    def __init__(self, ap, size):
        self.ap, self.size, self.off, self.n = ap, size, 0, 0

    def reset(self):
        self.off = 0

    def _take(self, nel):
        self.off = (self.off + 15) // 16 * 16
        v = self.ap[:, self.off:self.off + nel]
        self.off += nel
        assert self.off <= self.size, ("arena overflow", self.off, self.size)
        self.n += 1
        return v

    def bf(self, *free):
        n = int(np.prod(free))
        v = self._take(n)
        if len(free) == 2:
            v = v.rearrange("p (a b) -> p a b", a=free[0])
        return v, f"A{self.n}"

    def f32(self, *free):
        n = int(np.prod(free))
        v = self._take(2 * n).bitcast(F32)
        if len(free) == 2:
            v = v.rearrange("p (a b) -> p a b", a=free[0])
        return v, f"A{self.n}"

    def i32(self, *free):
        n = int(np.prod(free))
        v = self._take(2 * n).bitcast(I32)
        return v, f"A{self.n}"


C_GMIX, C_GX, C_GMEM, C_GFFN, C_GQ, C_GKV, C_GSUB = 0, 16, 32, 48, 64, 68, 70
NCOL = 72
ARENA_EL = 100 * 1024


def build_program():
    nc = bass.Bass("TRN2", target_bir_lowering=False)

    def din(name, shape, dt=F32):
        return nc.dram_tensor(name, list(shape), dt, kind="ExternalInput").ap()

    def dscr(name, shape, dt):
        return nc.dram_tensor(name, list(shape), dt).ap()

    x = din("x", [S, D])
    mem = din("mem", [MEM, D])
    pos = din("pos", [1, S], I32)
    w_in = din("w_in", [D, IN_W])
    w_uq = din("w_uq", [512, 1536])
    w_ukv = din("w_ukv", [256, 2048])
    w_out = din("w_out", [D, D])
    w_xq = din("w_xq", [D, 512])
    w_xk = din("w_xk", [D, 512])
    w_xv = din("w_xv", [D, 512])
    w_xo = din("w_xo", [512, D])
    w_gate = din("w_gate", [D, FF])
    w_up = din("w_up", [D, FF])
    w_down = din("w_down", [FF, D])
    cols_d = din("cols", [128, NCOL])
    lam_d = din("lamv", [4, 128])
    gfin_d = din("gfin", [1, D])
    cf_d = din("cf", [128, 4])
    cb_d = din("cb", [128, 512])
    y = nc.dram_tensor("y", [S, D], F32, kind="ExternalOutput").ap()

    QKd = dscr("s_qkd", [16, 128, S], BF16)
    Vd = dscr("s_vd", [4, 128, NT, 256], BF16)
    CQ = dscr("s_cq", [4, 128, S], BF16)
    CKV = dscr("s_ckv", [2, 128, S], BF16)
    KR = dscr("s_kr", [64, S], BF16)
    QN = dscr("s_qn", [8, 128, S], BF16)
    QR = dscr("s_qr", [4, 128, S], BF16)
    KN = dscr("s_kn", [8, 128, S], BF16)
    Vm = dscr("s_vm", [8, 128, NT, 128], BF16)
    MIX = dscr("s_mix", [16, 128, S], BF16)
    X1 = dscr("s_x1", [S, D], F32)
    X2 = dscr("s_x2", [S, D], F32)
    HF = dscr("s_hf", [16, 128, S], BF16)

    P = Prog()
    es = contextlib.ExitStack()
    with es:
        def sb(name, shape, dt):
            return es.enter_context(nc.sbuf_tensor(name, list(shape), dt))

        arena_t = sb("arena", [128, ARENA_EL], BF16)
        AR = Arena(arena_t, ARENA_EL)
        cb = sb("cb_sb", [128, 512], BF16)
        cols = sb("cols_sb", [128, NCOL], F32)
        cf = sb("cf_sb", [128, 4], F32)
        sm = sb("sm", [128, 64], F32)
        lamb = sb("lamb", [128, 4, 128], F32)
        psA = es.enter_context(nc.psum_tensor("psA", [128, 2048], F32))
        psB = es.enter_context(nc.psum_tensor("psB", [128, 2048], F32))

        ident = cb[:, 0:128]
        ones = cb[:, 128:256]
        swd = cb[:, 256:384]
        swm = cb[:, 384:512]

        def bank(b):
            t = psA if b < 4 else psB
            return t[:, (b % 4) * 512:(b % 4 + 1) * 512]

        def bk(b):
            return f"ps{b}"

        def banks4(g):
            return (psA if g == 0 else psB), [f"ps{4 * g + i}" for i in range(4)]

        def bankpair_bf(b):
            t = psA if b < 4 else psB
            return t[:, (b % 4) * 512:(b % 4 + 2) * 512].bitcast(BF16)

        EPS6, EPS5, NEGLAM, GS0, GS1 = 0, 1, 2, 3, 4
        SS0 = 8
        uid = [0]

        def key(prefix):
            uid[0] += 1
            return f"{prefix}{uid[0]}"

        try:
            P.dma("pool", cb[:], cb_d, writes=["cb"])
            P.dma("sp", cols[:], cols_d, writes=["cols"])
            P.dma("sp", cf[:], cf_d, writes=["cf"])
            for i in range(4):
                P.dma("sp", lamb[:, i, :], lam_d[i:i + 1, :].partition_broadcast(128), writes=[f"lamb{i}"])
            P.op("dve", "memset", writes=["sm_eps6", "sm_eps5", "sm_neglam", "sm_gs", "sm5", "sm6", "sm7"] + [f"sm_ss{i}" for i in range(4)], ap=sm[:], constant=0.0)
            P.op("dve", "memset", writes=["sm_eps6"], ap=sm[:, EPS6:EPS6 + 1], constant=1e-6)
            P.op("dve", "memset", writes=["sm_eps5"], ap=sm[:, EPS5:EPS5 + 1], constant=1e-5)
            P.op("dve", "tensor_tensor", reads=["lamb0", "lamb1"], writes=["lamb0"], out=lamb[:, 0, :], in0=lamb[:, 0, :], in1=lamb[:, 1, :], op=ALU.mult)
            P.op("dve", "tensor_tensor", reads=["lamb2", "lamb3"], writes=["lamb2"], out=lamb[:, 2, :], in0=lamb[:, 2, :], in1=lamb[:, 3, :], op=ALU.mult)
            P.op("dve", "reduce_sum", reads=["lamb0"], writes=["sm5"], out=sm[:, 5:6], in_=lamb[:, 0, :], axis=AX.X)
            P.op("dve", "reduce_sum", reads=["lamb2"], writes=["sm6"], out=sm[:, 6:7], in_=lamb[:, 2, :], axis=AX.X)
            P.op("act", "activation", reads=["sm5"], writes=["sm5"], out=sm[:, 5:6], in_=sm[:, 5:6], func=AF.Exp)
            P.op("act", "activation", reads=["sm6"], writes=["sm6"], out=sm[:, 6:7], in_=sm[:, 6:7], func=AF.Exp)
            P.op("dve", "tensor_tensor", reads=["sm5", "sm6"], writes=["sm7"], out=sm[:, 7:8], in0=sm[:, 6:7], in1=sm[:, 5:6], op=ALU.subtract)
            P.op("dve", "tensor_scalar", reads=["sm7"], writes=["sm_neglam"], out=sm[:, NEGLAM:NEGLAM + 1], in0=sm[:, 7:8], scalar1=-0.2, scalar2=None, op0=ALU.add)
            P.op("dve", "tensor_scalar", reads=["cols"], writes=["sm_gs"], out=sm[:, GS0:GS0 + 2], in0=cols[:, C_GSUB:C_GSUB + 2], scalar1=0.8, scalar2=None, op0=ALU.mult)

            ssn = [0]

            def rstd_cols(src_ap, junk_ap, n_feat, eps_col, rk, wk_junk):
                slot = ssn[0] % 4
                ssn[0] += 1
                c0 = SS0 + 3 * slot
                kss = f"sm_ss{slot}"
                P.op("act", "activation", reads=[kss], writes=[kss], out=sm[:, c0:c0 + 1], in_=sm[:, c0:c0 + 1], func=AF.Copy, scale=0.0)
                P.op("act", "activation", reads=rk, writes=[kss] + wk_junk, out=junk_ap, in_=src_ap, func=AF.Square, accum_out=sm[:, c0:c0 + 1])
                P.op("act", "activation", reads=[kss, "sm_eps6", "sm_eps5"], writes=[kss], out=sm[:, c0 + 1:c0 + 2], in_=sm[:, c0:c0 + 1], func=AF.Ln, bias=sm[:, eps_col:eps_col + 1], scale=1.0 / n_feat)
                P.op("act", "activation", reads=[kss], writes=[kss], out=sm[:, c0 + 2:c0 + 3], in_=sm[:, c0 + 1:c0 + 2], func=AF.Exp, scale=-0.5)
                return sm[:, c0 + 2:c0 + 3], kss

            trn = [0]

            def norm_s1(src, ksrc, xnb, kxnb, eng="dve"):
                r, kr = rstd_cols(src, xnb, D, EPS6, [ksrc], [kxnb])
                P.op(eng, "tensor_scalar", reads=[ksrc, kr], writes=[kxnb], out=xnb, in0=src, scalar1=r, scalar2=None, op0=ALU.mult)

            def norm_s2(xnb, kxnb, gcol0, dst3, kdst, b0=None):
                if b0 is None:
                    b0 = 6 if trn[0] % 2 else 4
                    trn[0] += 1
                pst = bankpair_bf(b0)
                for c in range(16):
                    P.op("pe", "transpose", reads=[kxnb, "cb"], writes=[bk(b0), bk(b0 + 1)], out=pst[:, c * 128:(c + 1) * 128], in_=xnb[:, c * 128:(c + 1) * 128], identity=ident)
                P.op("dve", "tensor_tensor", reads=[bk(b0), bk(b0 + 1), "cols"], writes=[kdst], out=dst3, in0=pst.rearrange("p (c t) -> p c t", c=16), in1=cols[:, gcol0:gcol0 + 16].unsqueeze(2).to_broadcast([128, 16, 128]), op=ALU.mult)

            def norm_to_T(src, ksrc, xnb, kxnb, gcol0, dst3, kdst):
                norm_s1(src, ksrc, xnb, kxnb)
                norm_s2(xnb, kxnb, gcol0, dst3, kdst)

            def dma_split(q, dst, src, reads, writes, grp=2):
                if len(dst.shape) == 3 and dst.shape[1] > grp:
                    for c0 in range(0, dst.shape[1], grp):
                        c1 = min(dst.shape[1], c0 + grp)
                        P.dma(q, dst[:, c0:c1, :], src[:, c0:c1, :], reads=reads, writes=writes)
                else:
                    P.dma(q, dst, src, reads=reads, writes=writes)

            def load_w(dst, kdst, src):
                dma_split("pool", dst, src, [], [kdst])

            def wview(w, r0, nr_chunks, c0, ncols):
                return w[r0:r0 + nr_chunks * 128, c0:c0 + ncols].rearrange("(c p) n -> p c n", p=128)

            AR.reset()
            hT, khT = AR.bf(16, S)
            xnb2 = [AR.bf(D) for _ in range(2)]
            wblk = [AR.bf(16, 512) for _ in range(2)]
            sqb = [AR.bf(512) for _ in range(6)]
            tbf = [AR.bf(512) for _ in range(2)]
            stage = [AR.bf(S) for _ in range(2)]
            vstage = [AR.bf(512) for _ in range(2)]
            xt2 = [AR.f32(D) for _ in range(2)]
            tmpf, ktmpf = AR.f32(6, 512)
            cosd, kcosd = AR.f32(S)
            sind, ksind = AR.f32(S)
            cosm, kcosm = AR.f32(S)
            sinm, ksinm = AR.f32(S)
            rp8, krp8 = AR.f32(8, 512)
            ropeu = [(rp8[:, j, :], f"{krp8}_{j}") for j in range(0, 2)]
            ropew = [(rp8[:, j, :], f"{krp8}_{j}") for j in range(2, 4)]
            rsl = [(rp8[:, j, :], f"{krp8}_{j}") for j in range(4, 6)]
            lnl = [(rp8[:, j, :], f"{krp8}_{j}") for j in range(6, 8)]

            rp8f = rp8.rearrange("p a b -> p (a b)")
            posf, kposf = rp8f[:, 0:S], "tab_posf"
            posi, kposi = rp8f[:, S:2 * S].bitcast(I32), "tab_posi"
            ua, kua = tmpf.rearrange("p a b -> p (a b)")[:, 0:S], ktmpf

            def tab_setup():
                P.dma("sp", posi, pos.partition_broadcast(128), writes=[kposi])
                P.op("dve", "tensor_copy", reads=[kposi], writes=[kposf], out=posf, in_=posi)

            def make_tables_gen(cdst, kc, sdst, ks, fcol, sign_col):
                P.op("dve", "tensor_scalar", reads=[kposf, "cf"], writes=[kua], out=ua, in0=posf, scalar1=cf[:, fcol:fcol + 1], scalar2=None, op0=ALU.mult)
                yield
                P.op("dve", "tensor_copy", reads=[kua], writes=[kposi], out=posi, in_=ua)
                yield
                P.op("dve", "tensor_copy", reads=[kposi], writes=[ks], out=sdst, in_=posi)
                yield
                P.op("dve", "tensor_tensor", reads=[kua, ks], writes=[kua], out=ua, in0=ua, in1=sdst, op=ALU.subtract)
                yield
                P.op("dve", "tensor_scalar", reads=[kua], writes=[ks], out=sdst, in0=ua, scalar1=0.5, scalar2=None, op0=ALU.is_gt)
                yield
                P.op("dve", "tensor_tensor", reads=[kua, ks], writes=[kua], out=ua, in0=ua, in1=sdst, op=ALU.subtract)
                yield
                P.op("dve", "tensor_scalar", reads=[kua], writes=[ks], out=sdst, in0=ua, scalar1=-0.5, scalar2=None, op0=ALU.is_lt)
                yield
                P.op("dve", "tensor_tensor", reads=[kua, ks], writes=[kua], out=ua, in0=ua, in1=sdst, op=ALU.add)
                yield
                P.op("dve", "tensor_scalar", reads=[kua], writes=[kc], out=cdst, in0=ua, scalar1=0.25, scalar2=None, op0=ALU.add)
                yield
                P.op("act", "activation", reads=[kua], writes=[ks], out=sdst, in_=ua, func=AF.Sin, scale=6.28318)
                yield
                P.op("dve", "tensor_scalar", reads=[kc], writes=[kua], out=ua, in0=cdst, scalar1=0.5, scalar2=None, op0=ALU.is_gt)
                yield
                P.op("dve", "tensor_tensor", reads=[kua, kc], writes=[kc], out=cdst, in0=cdst, in1=ua, op=ALU.subtract)
                yield
                P.op("act", "activation", reads=[kc], writes=[kc], out=cdst, in_=cdst, func=AF.Sin, scale=6.28318)
                yield
                P.op("dve", "tensor_scalar", reads=[ks, "cf"], writes=[ks], out=sdst, in0=sdst, scalar1=cf[:, sign_col:sign_col + 1], scalar2=None, op0=ALU.mult)
                yield

            def make_tables(*a):
                for _ in make_tables_gen(*a):
                    pass

            _phase(1)
            def tabgen():
                tab_setup()
                yield
                yield from make_tables_gen(cosd, kcosd, sind, ksind, 0, 2)
                yield from make_tables_gen(cosm, kcosm, sinm, ksinm, 1, 3)
            tg = tabgen()

            blocks = [(0, 512), (512, 320)] + [(832 + 512 * i, 512) for i in range(4)] + [(2880 + 512 * i, 512) for i in range(2)]

            def load_blk(i):
                c0, n = blocks[i]
                dst, kd = wblk[i % 2]
                load_w(dst[:, :, 0:n], kd, wview(w_in, 0, 16, c0, n))
                if i == 1:
                    load_w(dst[:, :, 320:384], kd, wview(w_in, 0, 16, 768, 64))

            load_blk(6)
            load_blk(7)
            def xload(t):
                P.dma("sp", xt2[t % 2][0], x[t * 128:(t + 1) * 128, :], writes=[xt2[t % 2][1]])

            def s1(t):
                norm_s1(xt2[t % 2][0], xt2[t % 2][1], xnb2[t % 2][0], xnb2[t % 2][1])

            def s2(t):
                norm_s2(xnb2[t % 2][0], xnb2[t % 2][1], C_GMIX, hT[:, :, t * 128:(t + 1) * 128], f"{khT}_{t}")

            xload(0)
            xload(1)
            s1(0)
            xload(2)
            s1(1)
            xload(3)
            s2(0)
            for t in range(NT):
                if t + 2 < NT:
                    s1(t + 2)
                    if t + 4 < NT:
                        xload(t + 4)
                if t + 1 < NT:
                    s2(t + 1)
                for bi in (6, 7):
                    wb_, kwb = wblk[bi % 2]
                    b_ = (2 * t + bi) % 3
                    for c in range(16):
                        P.op("pe", "matmul", reads=[kwb, f"{khT}_{t}"], writes=[bk(b_)], out=bank(b_), lhsT=hT[:, c, t * 128:(t + 1) * 128], rhs=wb_[:, c, 0:512], start=c == 0, stop=c == 15)
                    vs_ap, kvs = vstage[bi % 2]
                    P.op("act", "activation", reads=[bk(b_)], writes=[kvs], out=vs_ap, in_=bank(b_), func=AF.Copy)
                    h0 = 2 * (bi - 6)
                    P.dma("sp", Vd[h0:h0 + 2, :, t, :].rearrange("h p d -> p h d"), vs_ap.rearrange("p (h d) -> p h d", h=2),
                          reads=[kvs], writes=["Vd"])
                next(tg, None)
                next(tg, None)
            for _ in tg:
                pass
            P.barrier()

            _phase(2)
            rope_pend = []

            def rope_flush():
                while rope_pend:
                    rope_pend.pop(0)()

            def rope_tile(src_bank, npart, ctab, stab, kct, kst, swap, tb, dst, kdst, slot):
                tb_ap, ktb = tbf[slot]
                u_ap, ku = ropeu[slot]
                w_ap, kw = ropew[slot]
                sl = slice(tb * 512, (tb + 1) * 512)
                src = bank(src_bank)[0:npart, :]
                P.op("act", "activation", reads=[bk(src_bank)], writes=[ktb], out=tb_ap[0:npart, :], in_=src, func=AF.Copy)
                P.op("dve", "tensor_tensor", reads=[bk(src_bank), kct], writes=[ku], out=u_ap[0:npart, :], in0=src, in1=ctab[0:npart, sl], op=ALU.mult)
                rope_flush()

                def part2():
                    P.op("pe", "matmul", reads=[ktb, "cb"], writes=[bk(7)], out=bank(7)[0:npart, :], lhsT=swap[0:npart, 0:npart], rhs=tb_ap[0:npart, :], start=True, stop=True)
                    P.op("dve", "tensor_tensor", reads=[bk(7), kst], writes=[kw], out=w_ap[0:npart, :], in0=bank(7)[0:npart, :], in1=stab[0:npart, sl], op=ALU.mult)
                    P.op("dve", "tensor_tensor", reads=[ku, kw], writes=[kdst], out=dst, in0=u_ap[0:npart, :], in1=w_ap[0:npart, :], op=ALU.add)
                rope_pend.append(part2)

            load_blk(0)
            _phase(2.02)
            load_blk(1)
            _phase(2.05)
            (w0, kw0), (w1, kw1) = wblk[0], wblk[1]
            for tb in range(4):
                tsl = slice(tb * 512, (tb + 1) * 512)
                for ch in range(7):
                    wsrc, kws = (w0, kw0) if ch < 4 else (w1, kw1)
                    cofs = ch * 128 if ch < 4 else (ch - 4) * 128
                    m = 128 if ch < 6 else 64
                    if tb == 0 and ch == 1: _phase(2.06)
                    if tb == 0 and ch == 6: _phase(2.07)
                    for c in range(16):
                        P.op("pe", "matmul", reads=[kws] + [f"{khT}_{tt}" for tt in range(4 * tb, 4 * tb + 4)], writes=[bk(ch)], out=bank(ch), lhsT=wsrc[:, c, cofs:cofs + 128], rhs=hT[:, c, tsl], start=c == 0, stop=c == 15)
                if tb == 0: _phase(2.1)
                st_ap, kst_ = stage[tb % 2]
                rope_tile(6, 128, cosm, sinm, kcosm, ksinm, swm, tb, st_ap[:, 0:512], kst_, tb % 2)
                if tb == 0: _phase(2.2)
                rope_flush()
                P.dma("sp", KR[:, tsl], st_ap[0:64, 0:512], reads=[kst_], writes=["KR"])
                if tb == 0: _phase(2.3)
                for ch in range(6):
                    sq_ap, ksq = sqb[ch]
                    P.op("dve", "tensor_copy", reads=[bk(ch)], writes=[ktmpf], out=tmpf[:, ch, :], in_=bank(ch))
                    P.op("act", "activation", reads=[ktmpf], writes=[ksq], out=sq_ap, in_=tmpf[:, ch, :], func=AF.Square)
                if tb == 0: _phase(2.4)
                for ch in range(4):
                    P.op("pe", "matmul", reads=[sqb[ch][1], "cb"], writes=[bk(6)], out=bank(6), lhsT=ones, rhs=sqb[ch][0], start=ch == 0, stop=ch == 3)
                for ch in range(4, 6):
                    P.op("pe", "matmul", reads=[sqb[ch][1], "cb"], writes=[bk(7)], out=bank(7), lhsT=ones, rhs=sqb[ch][0], start=ch == 4, stop=ch == 5)
                if tb == 0: _phase(2.5)
                for j, (bnk, nf) in enumerate(((6, 512), (7, 256))):
                    ln_ap, kln = lnl[j]
                    rs_ap, krs = rsl[j]
                    P.op("act", "activation", reads=[bk(bnk), "sm_eps6"], writes=[kln], out=ln_ap, in_=bank(bnk), func=AF.Ln, bias=sm[:, EPS6:EPS6 + 1], scale=1.0 / nf)
                    P.op("act", "activation", reads=[kln], writes=[krs], out=rs_ap, in_=ln_ap, func=AF.Exp, scale=-0.5)
                if tb == 0: _phase(2.6)
                st2_ap, kst2 = stage[(tb + 1) % 2]
                st3 = st2_ap.rearrange("p (a b) -> p a b", a=4)
                for ch in range(4):
                    P.op("dve", "scalar_tensor_tensor", reads=[ktmpf, "cols", rsl[0][1]], writes=[kst2], out=st3[:, ch, :], in0=tmpf[:, ch, :], scalar=cols[:, C_GQ + ch:C_GQ + ch + 1], in1=rsl[0][0], op0=ALU.mult, op1=ALU.mult)
                P.dma("sp", CQ[:, :, tsl].rearrange("c p t -> p c t"), st3, reads=[kst2], writes=["CQ"])
                if tb == 0: _phase(2.7)
                vs_ap, kvs = vstage[0]
                vs_b, kvs_b = vstage[1]
                for j, (dst_, kd_) in enumerate(((vs_ap, kvs), (vs_b, kvs_b))):
                    ch = 4 + j
                    P.op("dve", "scalar_tensor_tensor", reads=[ktmpf, "cols", rsl[1][1]], writes=[kd_], out=dst_, in0=tmpf[:, ch, :], scalar=cols[:, C_GKV + j:C_GKV + j + 1], in1=rsl[1][0], op0=ALU.mult, op1=ALU.mult)
                    P.dma("sp", CKV[j, :, tsl], dst_, reads=[kd_], writes=["CKV"])

            _phase(3)
            nblk = len(blocks)
            for bi in range(2, 6):
                if bi == 2:
                    load_blk(2)
                if bi + 1 < 6:
                    load_blk(bi + 1)
                wb_, kwb = wblk[bi % 2]
                for j in range(4):
                    chunk = (bi - 2) * 4 + j
                    st_ap, kst_ = stage[chunk % 2]
                    for tb in range(4):
                        tsl = slice(tb * 512, (tb + 1) * 512)
                        sb_ = (chunk * 4 + tb) % 3
                        for c in range(16):
                            P.op("pe", "matmul", reads=[kwb] + [f"{khT}_{tt}" for tt in range(4 * tb, 4 * tb + 4)], writes=[bk(sb_)], out=bank(sb_), lhsT=wb_[:, c, j * 128:(j + 1) * 128], rhs=hT[:, c, tsl], start=c == 0, stop=c == 15)
                        rope_tile(sb_, 128, cosd, sind, kcosd, ksind, swd, tb, st_ap[:, tsl], kst_, (chunk * 4 + tb) % 2)
                    rope_flush()
                    P.dma("sp", QKd[chunk], st_ap, reads=[kst_], writes=["QKd"])
            _phase(5)
            P.barrier()
            AR.reset()
            cq, kcq = AR.bf(4, S)
            ckv, kckv = AR.bf(2, S)
            wuq, kwuq = AR.bf(4, 1536)
            wukv, kwukv = AR.bf(2, 2048)
            tbf = [AR.bf(512) for _ in range(2)]
            stage = [AR.bf(S) for _ in range(2)]
            vstage = [AR.bf(512) for _ in range(2)]
            cosm2, kcosm2 = AR.f32(S)
            sinm2, ksinm2 = AR.f32(S)
            ropeu = [AR.f32(512) for _ in range(2)]
            ropew = [AR.f32(512) for _ in range(2)]
            xt2 = [AR.f32(D) for _ in range(2)]
            tmpf, ktmpf = AR.f32(6, 512)
            posi = xt2[0][0].bitcast(I32)
            kposi = xt2[0][1]
            posf, kposf = xt2[1]
            ua, kua = tmpf.rearrange("p a b -> p (a b)")[:, 0:S], ktmpf
            P.dma("sp", posi, pos.partition_broadcast(128), writes=[kposi])
            P.op("dve", "tensor_copy", reads=[kposi], writes=[kposf], out=posf, in_=posi)
            make_tables(cosm2, kcosm2, sinm2, ksinm2, 1, 3)

            P.dma("sp", cq, CQ.rearrange("c p t -> p c t"), reads=["CQ"], writes=[kcq])
            P.dma("sp", ckv, CKV.rearrange("c p t -> p c t"), reads=["CKV"], writes=[kckv])
            load_w(wuq, kwuq, wview(w_uq, 0, 4, 0, 1536))
            load_w(wukv, kwukv, wview(w_ukv, 0, 2, 0, 2048))
            cnt = [0]

            def fm_proj(wt, kwt, ncc, col0, src, ksrc, tb, b_):
                tsl = slice(tb * 512, (tb + 1) * 512)
                for c in range(ncc):
                    P.op("pe", "matmul", reads=[kwt, ksrc], writes=[bk(b_)], out=bank(b_), lhsT=wt[:, c, col0:col0 + 128], rhs=src[:, c, tsl], start=c == 0, stop=c == ncc - 1)

            evn = [0]

            def evac(b_, dst, kdst):
                evn[0] += 1
                if evn[0] % 2:
                    P.op("act", "activation", reads=[bk(b_)], writes=[kdst], out=dst, in_=bank(b_), func=AF.Copy)
                else:
                    P.op("dve", "tensor_copy", reads=[bk(b_)], writes=[kdst], out=dst, in_=bank(b_))

            for h in range(8):
                st_ap, kst_ = stage[cnt[0] % 2]
                for tb in range(4):
                    b_ = (cnt[0] * 4 + tb) % 3
                    fm_proj(wuq, kwuq, 4, h * 128, cq, kcq, tb, b_)
                    evac(b_, st_ap[:, tb * 512:(tb + 1) * 512], kst_)
                P.dma("sp", QN[h], st_ap, reads=[kst_], writes=["QN"])
                cnt[0] += 1
            for i in range(4):
                st_ap, kst_ = stage[cnt[0] % 2]
                for tb in range(4):
                    b_ = (cnt[0] * 4 + tb) % 3
                    fm_proj(wuq, kwuq, 4, 1024 + i * 128, cq, kcq, tb, b_)
                    rope_tile(b_, 128, cosm2, sinm2, kcosm2, ksinm2, swm, tb, st_ap[:, tb * 512:(tb + 1) * 512], kst_, tb % 2)
                rope_flush()
                P.dma("sp", QR[i], st_ap, reads=[kst_], writes=["QR"])
                cnt[0] += 1
            for h in range(8):
                st_ap, kst_ = stage[cnt[0] % 2]
                for tb in range(4):
                    b_ = (cnt[0] * 4 + tb) % 3
                    fm_proj(wukv, kwukv, 2, h * 128, ckv, kckv, tb, b_)
                    evac(b_, st_ap[:, tb * 512:(tb + 1) * 512], kst_)
                P.dma("sp", KN[h], st_ap, reads=[kst_], writes=["KN"])
                cnt[0] += 1
            for t in range(NT):
                for half in range(2):
                    b_ = (t * 2 + half) % 3
                    for c in range(2):
                        P.op("pe", "matmul", reads=[kwukv, kckv], writes=[bk(b_)], out=bank(b_), lhsT=ckv[:, c, t * 128:(t + 1) * 128], rhs=wukv[:, c, 1024 + half * 512:1024 + (half + 1) * 512], start=c == 0, stop=c == 1)
                    vs_ap, kvs = vstage[(t * 2 + half) % 2]
                    evac(b_, vs_ap, kvs)
                    P.dma("sp", Vm[half * 4:half * 4 + 4, :, t, :].rearrange("h p d -> p h d"), vs_ap.rearrange("p (h d) -> p h d", h=4),
                          reads=[kvs], writes=["Vm"])

            attn_pend = []

            def attn_flush(kp=99):
                keep = []
                for st, fn in list(attn_pend):
                    if st <= kp:
                        fn()
                    else:
                        keep.append((st, fn))
                attn_pend[:] = keep

            def attention(qv, kq, kv_, kk, nkt, vfun, kvv, ndc, softmaxes, scale, q0, tail, PT):
                nkp = nkt // 2
                for si, chunks in enumerate(softmaxes):
                    def S_(kp):
                        for kt in (2 * kp, 2 * kp + 1):
                            b_ = (kp % 2) * 2 + (kt % 2)
                            for ci, (ch, K) in enumerate(chunks):
                                P.op("pe", "matmul", reads=[kk, kq], writes=[bk(b_)], out=bank(b_), lhsT=kv_[0:K, ch, kt * 128:(kt + 1) * 128], rhs=qv[0:K, ch, q0:q0 + 512], start=ci == 0, stop=ci == len(chunks) - 1)

                    def E_(kp):
                        pt_ap, kpt = PT[kp % 3]
                        b0 = (kp % 2) * 2
                        src = psA[:, b0 * 512:(b0 + 2) * 512]
                        P.op("act", "activation", reads=[bk(b0), bk(b0 + 1)], writes=[kpt], out=pt_ap, in_=src, func=AF.Exp, scale=scale)

                    def PV_(kp):
                        pt_ap, kpt = PT[kp % 3]
                        for kt in (2 * kp, 2 * kp + 1):
                            rhs = pt_ap[:, (kt % 2) * 512:(kt % 2 + 1) * 512]
                            for dc in range(ndc):
                                P.op("pe", "matmul", reads=[kvv, kpt], writes=[bk(4 + dc)], out=bank(4 + dc), lhsT=vfun(kt, dc), rhs=rhs, start=kt == 0, stop=kt == nkt - 1)
                            P.op("pe", "matmul", reads=["cb", kpt], writes=[bk(6)], out=bank(6), lhsT=ones, rhs=rhs, start=kt == 0, stop=kt == nkt - 1)

                    S_(0)
                    E_(0)
                    for kp in range(1, nkp):
                        S_(kp)
                        E_(kp)
                        PV_(kp - 1)
                        attn_flush(kp)
                    PV_(nkp - 1)
                    tail(si)

            def run_softmax_pipeline(descs, PT):
                def S_(d, kp):
                    for kt in (2 * kp, 2 * kp + 1):
                        b_ = (kp % 2) * 2 + (kt % 2)
                        ch_ = d["chunks"]
                        for ci, (ch, K) in enumerate(ch_):
                            P.op("pe", "matmul", reads=[d["kk"], d["kq"]], writes=[bk(b_)], out=bank(b_), lhsT=d["kv"][0:K, ch, kt * 128:(kt + 1) * 128], rhs=d["qv"][0:K, ch, d["q0"]:d["q0"] + 512], start=ci == 0, stop=ci == len(ch_) - 1)

                def E_(d, kp):
                    pt_ap, kpt = PT[kp % 3]
                    b0 = (kp % 2) * 2
                    P.op("act", "activation", reads=[bk(b0), bk(b0 + 1)], writes=[kpt], out=pt_ap, in_=psA[:, b0 * 512:(b0 + 2) * 512], func=AF.Exp, scale=d["scale"])

                def PV_(d, kp):
                    pt_ap, kpt = PT[kp % 3]
                    nkt = 16
                    for kt in (2 * kp, 2 * kp + 1):
                        rhs = pt_ap[:, (kt % 2) * 512:(kt % 2 + 1) * 512]
                        for dc in range(d["ndc"]):
                            P.op("pe", "matmul", reads=[d["kvv"], kpt], writes=[bk(4 + dc)], out=bank(4 + dc), lhsT=d["vfun"](kt, dc), rhs=rhs, start=kt == 0, stop=kt == nkt - 1)
                        P.op("pe", "matmul", reads=["cb", kpt], writes=[bk(6)], out=bank(6), lhsT=ones, rhs=rhs, start=kt == 0, stop=kt == nkt - 1)

                for i, d in enumerate(descs):
                    if d["pre"] is not None:
                        d["pre"]()
                    if i == 0:
                        S_(d, 0)
                        E_(d, 0)
                    for kp in range(1, 8):
                        S_(d, kp)
                        E_(d, kp)
                        PV_(d, kp - 1)
                        attn_flush(kp)
                    if i + 1 < len(descs):
                        S_(descs[i + 1], 0)
                        E_(descs[i + 1], 0)
                    PV_(d, 7)
                    d["tail"]()

            _phase(6)
            P.barrier()
            AR.reset()
            qb2 = [AR.bf(2, S) for _ in range(2)]
            kb2 = [AR.bf(2, S) for _ in range(2)]
            vb2 = [AR.bf(16, 256) for _ in range(2)]
            PT = [AR.bf(1024) for _ in range(3)]
            ostg = [AR.bf(2, 512) for _ in range(2)]
            sqd = [AR.bf(2, 512) for _ in range(2)]
            rden = [AR.f32(512) for _ in range(2)]
            o1n = [AR.f32(2, 512) for _ in range(2)]
            ot = [AR.f32(2, 512) for _ in range(2)]
            lnd = [AR.f32(512) for _ in range(2)]
            rsd = [AR.f32(512) for _ in range(2)]
            wo, kwo = AR.bf(16, D)
            kwo_all = [f"{kwo}_{g}" for g in range(4)]

            def wo_prefetch(g):
                load_w(wo[:, 4 * g:4 * g + 4, :], f"{kwo}_{g}", wview(w_out, 4 * g * 128, 4, 0, D))

            jobs = [("d", h) for h in range(4)] + [("m", h) for h in range(8)]

            def load_job(ji):
                kind, h = jobs[ji]
                (q_, kq_), (k_, kk_), (v_, kv2) = qb2[ji % 2], kb2[ji % 2], vb2[ji % 2]
                if kind == "d":
                    for j in range(2):
                        P.dma("sp", q_[:, j, :], QKd[2 * h + j], reads=["QKd"], writes=[kq_])
                        P.dma("sp", k_[:, j, :], QKd[8 + 2 * h + j], reads=["QKd"], writes=[kk_])
                    P.dma("sp", v_, Vd[h], reads=["Vd"], writes=[kv2])
                else:
                    P.dma("sp", q_[:, 0, :], QN[h], reads=["QN"], writes=[kq_])
                    P.dma("sp", q_[0:64, 1, :], QR[h // 2, (h % 2) * 64:(h % 2) * 64 + 64, :], reads=["QR"], writes=[kq_])
                    P.dma("sp", k_[:, 0, :], KN[h], reads=["KN"], writes=[kk_])
                    P.dma("sp", k_[0:64, 1, :], KR, reads=["KR"], writes=[kk_])
                    P.dma("sp", v_[:, :, 0:128], Vm[h], reads=["Vm"], writes=[kv2])

            tcount = [0]
            descs = []
            load_job(0)
            for ji, (kind, h) in enumerate(jobs):
                (q_, kq_), (k_, kk_), (v_, kv2) = qb2[ji % 2], kb2[ji % 2], vb2[ji % 2]
                for qi in range(4):
                    q0 = qi * 512
                    pre = (lambda ji=ji: load_job(ji + 1)) if (qi == 0 and ji + 1 < len(jobs)) else None
                    if kind == "m":
                        def tail_m(si, h=h, q0=q0):
                            s_ = tcount[0] % 2
                            tcount[0] += 1
                            rd, krd = rden[s_]
                            o2, ko2 = ot[s_]
                            og, kog = ostg[s_]
                            P.op("act", "activation", reads=[bk(6)], writes=[krd], out=rd, in_=bank(6), func=AF.Copy)
                            P.op("dve", "tensor_copy", reads=[bk(4)], writes=[ko2], out=o2[:, 0, :], in_=bank(4))

                            def later():
                                P.op("dve", "reciprocal", reads=[krd], writes=[krd], out=rd, in_=rd)
                                P.op("dve", "tensor_tensor", reads=[ko2, krd], writes=[kog], out=og[:, 0, :], in0=o2[:, 0, :], in1=rd, op=ALU.mult)
                                P.dma("sp", MIX[h, :, q0:q0 + 512], og[:, 0, :], reads=[kog], writes=["MIX"])
                            attn_pend.append((2, later))
                        descs.append(dict(pre=pre, qv=q_, kq=kq_, kv=k_, kk=kk_, vfun=(lambda kt, dc, v_=v_: v_[:, kt, 0:128]), kvv=kv2, ndc=1,
                                          chunks=[(0, 128), (1, 64)], scale=192.0 ** -0.5, q0=q0, tail=(lambda tail_m=tail_m: tail_m(0))))
                    else:
                        s_ = tcount[0] % 2
                        tcount[0] += 1

                        def tail_d(si, h=h, q0=q0, s_=s_):
                            rd, krd = rden[si]
                            o1, ko1 = o1n[s_]
                            o2, ko2 = ot[s_]
                            dst, kdst = (o1, ko1) if si == 0 else (o2, ko2)
                            P.op("act", "activation", reads=[bk(6)], writes=[krd], out=rd, in_=bank(6), func=AF.Copy)
                            P.op("dve", "tensor_copy", reads=[bk(4)], writes=[kdst], out=dst[:, 0, :], in_=bank(4))
                            P.op("dve", "tensor_copy", reads=[bk(5)], writes=[kdst], out=dst[:, 1, :], in_=bank(5))

                            def norm_part():
                                P.op("dve", "reciprocal", reads=[krd], writes=[krd], out=rd, in_=rd)
                                for dc in range(2):
                                    P.op("dve", "tensor_tensor", reads=[kdst, krd], writes=[kdst], out=dst[:, dc, :], in0=dst[:, dc, :], in1=rd, op=ALU.mult)
                                if si == 1:
                                    for dc in range(2):
                                        P.op("dve", "scalar_tensor_tensor", reads=[ko2, ko1, "sm_neglam"], writes=[ko2], out=o2[:, dc, :], in0=o2[:, dc, :], scalar=sm[:, NEGLAM:NEGLAM + 1], in1=o1[:, dc, :], op0=ALU.mult, op1=ALU.add)
                            attn_pend.append((2, norm_part))
                            if si == 0:
                                return
                            sq_, ksq_ = sqd[s_]
                            og, kog = ostg[s_]
                            ln_, kln_ = lnd[s_]
                            rs_, krs_ = rsd[s_]

                            def sq_part():
                                for dc in range(2):
                                    P.op("act", "activation", reads=[ko2], writes=[ksq_], out=sq_[:, dc, :], in_=o2[:, dc, :], func=AF.Square)
                                for dc in range(2):
                                    P.op("pe", "matmul", reads=[ksq_, "cb"], writes=[bk(7)], out=bank(7), lhsT=ones, rhs=sq_[:, dc, :], start=dc == 0, stop=dc == 1)

                            def fin_part():
                                P.op("act", "activation", reads=[bk(7), "sm_eps5"], writes=[kln_], out=ln_, in_=bank(7), func=AF.Ln, bias=sm[:, EPS5:EPS5 + 1], scale=1.0 / 256)
                                P.op("act", "activation", reads=[kln_], writes=[krs_], out=rs_, in_=ln_, func=AF.Exp, scale=-0.5)
                                for dc in range(2):
                                    P.op("dve", "scalar_tensor_tensor", reads=[ko2, krs_, "sm_gs"], writes=[kog], out=og[:, dc, :], in0=o2[:, dc, :], scalar=sm[:, GS0 + dc:GS0 + dc + 1], in1=rs_, op0=ALU.mult, op1=ALU.mult)
                                P.dma("sp", MIX[8 + 2 * h:8 + 2 * h + 2, :, q0:q0 + 512].rearrange("c p t -> p c t"), og, reads=[kog], writes=["MIX"])
                            attn_pend.append((4, sq_part))
                            attn_pend.append((6, fin_part))
                        for si in range(2):
                            descs.append(dict(pre=(pre if si == 0 else None), qv=q_, kq=kq_, kv=k_, kk=kk_,
                                              vfun=(lambda kt, dc, v_=v_: v_[:, kt, dc * 128:(dc + 1) * 128]), kvv=kv2, ndc=2,
                                              chunks=[(si, 128)], scale=128.0 ** -0.5, q0=q0, tail=(lambda tail_d=tail_d, si=si: tail_d(si))))

            for g in range(4):
                d_ = descs[2 + g]
                d_["pre"] = (lambda p=d_["pre"], g=g: ((p() if p is not None else None), wo_prefetch(g)))
            run_softmax_pipeline(descs, PT)
            attn_flush()
            _phase(7)
            P.barrier()
            AR.reset()
            mixT, kmix = AR.bf(16, S)
            xt2 = [AR.f32(D) for _ in range(2)]
            for g in range(4):
                P.dma("sp", mixT[:, 4 * g:4 * g + 4, :], MIX[4 * g:4 * g + 4].rearrange("c p t -> p c t"), reads=["MIX"], writes=[f"{kmix}_{g}"])
            kmix_all = [f"{kmix}_{g}" for g in range(4)]
            P.dma("sp", xt2[0][0], x[0:128, :], writes=[xt2[0][1]])
            for t in range(NT):
                if t + 1 < NT:
                    P.dma("sp", xt2[(t + 1) % 2][0], x[(t + 1) * 128:(t + 2) * 128, :], writes=[xt2[(t + 1) % 2][1]])
                pst, pk = banks4(t % 2)
                for n in range(4):
                    for c in range(16):
                        P.op("pe", "matmul", reads=[kmix_all[c // 4], kwo_all[c // 4]], writes=[pk[n]], out=pst[:, n * 512:(n + 1) * 512], lhsT=mixT[:, c, t * 128:(t + 1) * 128], rhs=wo[:, c, n * 512:(n + 1) * 512], start=c == 0, stop=c == 15)
                xs, kxs = xt2[t % 2]
                P.op("dve", "tensor_tensor", reads=pk + [kxs], writes=[kxs], out=xs, in0=pst[:, :], in1=xs, op=ALU.add)
                P.dma("sp", X1[t * 128:(t + 1) * 128, :], xs, reads=[kxs], writes=["X1"])

            _phase(8)
            P.barrier()
            AR.reset()
            wxa, kwxa = AR.bf(16, 512)
            wxo, kwxo = AR.bf(4, D)
            xkT, kxk = AR.bf(4, MEM)
            xv, kxv = AR.bf(2, 512)
            hxT, khx = AR.bf(16, 512)
            xqT, kxq = AR.bf(4, 512)
            xoT, kxo = AR.bf(4, 512)
            hfs, khfs = AR.bf(16, 512)
            _pt = AR.bf(1024)
            PT = [_pt, _pt, _pt]
            xnb4 = [AR.bf(D) for _ in range(4)]
            xnb2 = xnb4[0:2]
            x1b2 = [AR.f32(4, D) for _ in range(2)]
            rden4 = [AR.f32(512) for _ in range(4)]
            xof = [AR.f32(512) for _ in range(4)]
            xt2 = [(x1b2[1][0][:, 0, :], x1b2[1][1] + "_0")]
            mark_p2 = AR.off
            hmT, khm = AR.bf(16, MEM)
            wxb, kwxb = AR.bf(16, 512)

            load_w(wxa, kwxa, wview(w_xk, 0, 16, 0, 512))
            load_w(wxb, kwxb, wview(w_xv, 0, 16, 0, 512))
            load_w(wxo, kwxo, wview(w_xo, 0, 4, 0, D))
            for mt in range(2):
                xs, kxs = xt2[0]
                P.dma("sp", xs, mem[mt * 128:(mt + 1) * 128, :], writes=[kxs])
                xn, kxn = xnb2[mt % 2]
                norm_to_T(xs, kxs, xn, kxn, C_GMEM, hmT[:, :, mt * 128:(mt + 1) * 128], khm)
            for h in range(4):
                for c in range(16):
                    P.op("pe", "matmul", reads=[kwxa, khm], writes=[bk(h % 2)], out=bank(h % 2)[:, 0:MEM], lhsT=wxa[:, c, h * 128:(h + 1) * 128], rhs=hmT[:, c, :], start=c == 0, stop=c == 15)
                P.op("act", "activation", reads=[bk(h % 2)], writes=[kxk], out=xkT[:, h, :], in_=bank(h % 2)[:, 0:MEM], func=AF.Copy)
            for mt in range(2):
                for c in range(16):
                    P.op("pe", "matmul", reads=[kwxb, khm], writes=[bk(2 + mt)], out=bank(2 + mt), lhsT=hmT[:, c, mt * 128:(mt + 1) * 128], rhs=wxb[:, c, :], start=c == 0, stop=c == 15)
                P.op("act", "activation", reads=[bk(2 + mt)], writes=[kxv], out=xv[:, mt, :], in_=bank(2 + mt), func=AF.Copy)
            load_w(wxa, kwxa, wview(w_xq, 0, 16, 0, 512))

            P.barrier()
            AR.off = mark_p2
            xnbh = [AR.bf(D) for _ in range(3)]
            XN_ENG = "dve"

            def A1(tb):
                xb, kxb = x1b2[tb % 2]
                for i in range(4):
                    t = tb * 4 + i
                    P.dma("sp", xb[:, i, :], X1[t * 128:(t + 1) * 128, :], reads=["X1"], writes=[f"{kxb}_{i}"])
                for i in range(4):
                    norm_s1(xb[:, i, :], f"{kxb}_{i}", xnb4[i][0], xnb4[i][1])

            def A2(tb, i, b0=None):
                norm_s2(xnb4[i][0], xnb4[i][1], C_GX, hxT[:, :, i * 128:(i + 1) * 128], khx, b0=b0)

            def XQ(h):
                for c in range(16):
                    P.op("pe", "matmul", reads=[kwxa, khx], writes=[bk(h % 2)], out=bank(h % 2), lhsT=wxa[:, c, h * 128:(h + 1) * 128], rhs=hxT[:, c, :], start=c == 0, stop=c == 15)
                P.op("act", "activation", reads=[bk(h % 2)], writes=[kxq], out=xqT[:, h, :], in_=bank(h % 2), func=AF.Copy)

            A1(0)
            for i in range(4):
                A2(0, i)
            for h in range(4):
                XQ(h)
            for tb in range(4):
                xb, kxb = x1b2[tb % 2]
                for h in range(4):
                    def tail_x(si, h=h):
                        rd, krd = rden4[h]
                        xo_, kxo_ = xof[h]
                        P.op("act", "activation", reads=[bk(6)], writes=[krd], out=rd, in_=bank(6), func=AF.Copy)
                        P.op("dve", "tensor_copy", reads=[bk(4)], writes=[kxo_], out=xo_, in_=bank(4))

                        def later(h=h, rd=rd, krd=krd, xo_=xo_, kxo_=kxo_):
                            P.op("act", "activation", reads=[krd], writes=[krd], out=rd, in_=rd, func=AF.Ln)
                            P.op("act", "activation", reads=[krd], writes=[krd], out=rd, in_=rd, func=AF.Exp, scale=-1.0)
                            P.op("dve", "tensor_tensor", reads=[kxo_, krd], writes=[f"{kxo}_{h}"], out=xoT[:, h, :], in0=xo_, in1=rd, op=ALU.mult)
                        attn_pend.append((99, later))
                    attention(xqT, kxq, xkT, kxk, 2, lambda kt, dc, h=h: xv[:, kt, h * 128:(h + 1) * 128], kxv, 1,
                              [[(h, 128)]], 128.0 ** -0.5, 0, tail_x, PT)
                attn_flush()
                if tb + 1 < 4:
                    A1(tb + 1)
                for i in range(4):
                    t = tb * 4 + i
                    pst, pk = banks4(i % 2)
                    for n in range(4):
                        for h in range(4):
                            P.op("pe", "matmul", reads=[f"{kxo}_{h}", kwxo], writes=[pk[n]], out=pst[:, n * 512:(n + 1) * 512], lhsT=xoT[:, h, i * 128:(i + 1) * 128], rhs=wxo[:, h, n * 512:(n + 1) * 512], start=h == 0, stop=h == 3)
                    if tb + 1 < 4:
                        A2(tb + 1, i, b0=4 * ((i + 1) % 2))
                    xs = xb[:, i, :]
                    kxs = f"{kxb}_{i}"
                    P.op("dve", "tensor_tensor", reads=pk + [kxs], writes=[kxs], out=xs, in0=pst[:, :], in1=xs, op=ALU.add)
                    P.dma("sp", X2[t * 128:(t + 1) * 128, :], xs, reads=[kxs], writes=["X2"])
                    norm_s1(xs, kxs, xnbh[i % 3][0], xnbh[i % 3][1], eng=XN_ENG)
                    if i >= 2:
                        j = i - 2
                        norm_s2(xnbh[j % 3][0], xnbh[j % 3][1], C_GFFN, hfs[:, :, j * 128:(j + 1) * 128], khfs, b0=4 * (i % 2) + 2)
                if tb + 1 < 4:
                    XQ(0)
                    XQ(1)
                norm_s2(xnbh[2][0], xnbh[2][1], C_GFFN, hfs[:, :, 2 * 128:3 * 128], khfs)
                if tb + 1 < 4:
                    XQ(2)
                norm_s2(xnbh[0][0], xnbh[0][1], C_GFFN, hfs[:, :, 3 * 128:4 * 128], khfs)
                if tb + 1 < 4:
                    XQ(3)
                P.dma("sp", HF[:, :, tb * 512:(tb + 1) * 512].rearrange("c p t -> p c t"), hfs, reads=[khfs], writes=["HF"])

            _phase(9)
            def final_norm(t, xs, kxs, junk, kjunk, gfin, kgf):
                P.dma("sp", xs, X2[t * 128:(t + 1) * 128, :], reads=["X2"], writes=[kxs])
                r, kr = rstd_cols(xs, junk, D, EPS6, [kxs], [kjunk])
                P.op("dve", "scalar_tensor_tensor", reads=[kxs, kr, kgf], writes=[kxs], out=xs, in0=xs, scalar=r, in1=gfin, op0=ALU.mult, op1=ALU.mult)
                P.dma("sp", y[t * 128:(t + 1) * 128, :], xs, reads=[kxs], writes=["y"])

            TB = 1024
            for ps_ in range(2):
                P.barrier()
                AR.reset()
                actT, kact = AR.bf(NFC, TB)
                fn_x = [AR.f32(D) for _ in range(2)]
                gfin, kgf = AR.f32(D)
                fn_junk = [AR.bf(D) for _ in range(1)]
                xblk = [AR.f32(512) for _ in range(8)]
                mark = AR.off
                hf, khf = AR.bf(16, TB)
                wg = [AR.bf(16, 256) for _ in range(2)]
                wu = [AR.bf(16, 256) for _ in range(2)]
                sgb = [AR.bf(512) for _ in range(2)]
                P.dma("sp", gfin, gfin_d.partition_broadcast(128), writes=[kgf])
                P.dma("sp", hf, HF[:, :, ps_ * TB:(ps_ + 1) * TB].rearrange("c p t -> p c t"), reads=["HF"], writes=[khf])

                def load_gu(fb):
                    load_w(wg[fb % 2][0], wg[fb % 2][1], wview(w_gate, 0, 16, fb * 256, 256))
                    load_w(wu[fb % 2][0], wu[fb % 2][1], wview(w_up, 0, 16, fb * 256, 256))
                load_gu(0)
                pend_norm = list(range((ps_ - 1) * 8, ps_ * 8)) if ps_ > 0 else []
                for fb in range(22):
                    if fb + 1 < 22:
                        load_gu(fb + 1)
                    (wg_, kwg), (wu_, kwu) = wg[fb % 2], wu[fb % 2]
                    for fc in range(2):
                        f = fb * 2 + fc
                        for tk in range(2):
                            it = (f * 2 + tk) % 2
                            bg, bu = 2 * it, 2 * it + 1
                            tsl = slice(tk * 512, (tk + 1) * 512)
                            for c in range(16):
                                P.op("pe", "matmul", reads=[kwg, khf], writes=[bk(bg)], out=bank(bg), lhsT=wg_[:, c, fc * 128:(fc + 1) * 128], rhs=hf[:, c, tsl], start=c == 0, stop=c == 15)
                            for c in range(16):
                                P.op("pe", "matmul", reads=[kwu, khf], writes=[bk(bu)], out=bank(bu), lhsT=wu_[:, c, fc * 128:(fc + 1) * 128], rhs=hf[:, c, tsl], start=c == 0, stop=c == 15)
                            sg_, ksg = sgb[it]
                            P.op("act", "activation", reads=[bk(bg)], writes=[ksg], out=sg_, in_=bank(bg), func=AF.Silu)
                            P.op("dve", "tensor_tensor", reads=[bk(bu), ksg], writes=[f"{kact}_{f}"], out=actT[:, f, tsl], in0=bank(bu), in1=sg_, op=ALU.mult)
                    if pend_norm and fb % 2 == 1:
                        t = pend_norm.pop(0)
                        final_norm(t, fn_x[t % 2][0], fn_x[t % 2][1], fn_junk[0][0], fn_junk[0][1], gfin, kgf)
                P.barrier()
                AR.off = mark
                wd = [AR.bf(11, 512) for _ in range(6)]

                wd_next = [0]

                def load_wd_upto(gmax):
                    while wd_next[0] < min(gmax, 16):
                        gi = wd_next[0]
                        n_, g = gi // 4, gi % 4
                        d_, kd_ = wd[gi % 6]
                        load_w(d_, kd_, wview(w_down, g * 11 * 128, 11, n_ * 512, 512))
                        wd_next[0] += 1
                kact_all = [f"{kact}_{f}" for f in range(NFC)]
                for n in range(4):
                    for g in range(4):
                        gi = n * 4 + g
                        load_wd_upto(gi + 6)
                        d_, kd_ = wd[gi % 6]
                        for t8 in range(8):
                            t = ps_ * 8 + t8
                            xb_, kxb = xblk[t8]
                            if g == 0:
                                P.dma("sp", xb_, X2[t * 128:(t + 1) * 128, n * 512:(n + 1) * 512], reads=["X2"], writes=[kxb])
                            for fi in range(11):
                                f = g * 11 + fi
                                P.op("pe", "matmul", reads=[kact_all[f], kd_], writes=[bk(t8)], out=bank(t8), lhsT=actT[:, f, t8 * 128:(t8 + 1) * 128], rhs=d_[:, fi, :], start=f == 0, stop=f == NFC - 1)
                    for t8 in range(8):
                        t = ps_ * 8 + t8
                        xb_, kxb = xblk[t8]
                        P.op("dve", "tensor_tensor", reads=[bk(t8), kxb], writes=[kxb], out=xb_, in0=bank(t8), in1=xb_, op=ALU.add)
                        P.dma("sp", X2[t * 128:(t + 1) * 128, n * 512:(n + 1) * 512], xb_, reads=[kxb], writes=["X2"])
                        if ps_ == 1 and n == 3:
                            final_norm(t, fn_x[t % 2][0], fn_x[t % 2][1], fn_junk[0][0], fn_junk[0][1], gfin, kgf)


        except _Stop:
            pass

        need = P.finalize()
        sems = {"eng": {}, "dma": {}}
        for e in ENGS:
            sems["eng"][e] = [nc.alloc_semaphore(name=f"s_{e}_{i}") for i in range(need[e])]
        for q in ("sp", "pool"):
            sems["dma"][q] = [nc.alloc_semaphore(name=f"d_{q}_{i}") for i in range(NDMASEM)]
        with nc.Block() as block:
            @block.tensor
            def _(e):
                P.run("pe", e, sems)

            @block.scalar
            def _(e):
                P.run("act", e, sems)

            @block.vector
            def _(e):
                P.run("dve", e, sems)

            @block.gpsimd
            def _(e):
                P.run("pool", e, sems)

            @block.sync
            def _(e):
                P.run("sp", e, sems)
    return nc


_NC = [None]


def _host_consts():
    p = np.arange(128)
    cf = np.zeros((128, 4), np.float32)
    cf[:, 0] = (10000.0 ** (-(2.0 * (p % 64)) / 128.0)) / (2 * math.pi)
    cf[:, 1] = (10000.0 ** (-(2.0 * (p % 32)) / 64.0)) / (2 * math.pi)
    cf[:, 2] = np.where(p < 64, -1.0, 1.0)
    cf[:, 3] = np.where((p % 64) < 32, -1.0, 1.0)
    cb = np.zeros((128, 512), np.float32)
    cb[:, 0:128] = np.eye(128)
    cb[:, 128:256] = 1.0
    m = np.arange(128)
    cb[(m + 64) % 128, 256 + m] = 1.0
    src = np.where((m % 64) < 32, m + 32, m - 32)
    cb[src, 384 + m] = 1.0
    return cf, cb


def kernel(x, mem, positions, g_mix, w_in, g_q_lat, w_uq, g_kv_lat, w_ukv,
           lambda_q1, lambda_k1, lambda_q2, lambda_k2, g_diff_sub, w_out,
           g_xattn, g_mem, w_xq, w_xk, w_xv, w_xo,
           g_ffn, w_gate, w_up, w_down, g_final):
    f32 = lambda a: np.ascontiguousarray(np.asarray(a), dtype=np.float32)
    if _NC[0] is None:
        _NC[0] = build_program()
    nc = _NC[0]
    cf, cb = _host_consts()

    def colT(g, n):
        return f32(g).reshape(n, 128).T

    cols = np.concatenate([colT(g_mix[0], 16), colT(g_xattn[0], 16), colT(g_mem[0], 16), colT(g_ffn[0], 16),
                           colT(g_q_lat[0], 4), colT(g_kv_lat[0], 2), colT(g_diff_sub[0], 2)], axis=1)
    cols = np.ascontiguousarray(cols, dtype=np.float32)
    lamv = np.ascontiguousarray(np.stack([f32(lambda_q1[0]), f32(lambda_k1[0]), f32(lambda_q2[0]), f32(lambda_k2[0])]))
    wuq = f32(w_uq[0]).reshape(512, 8, 192)
    wuq_p = np.ascontiguousarray(np.concatenate([wuq[:, :, :128].reshape(512, 1024), wuq[:, :, 128:].reshape(512, 512)], axis=1))
    wukv = f32(w_ukv[0]).reshape(256, 8, 256)
    wukv_p = np.ascontiguousarray(np.concatenate([wukv[:, :, :128].reshape(256, 1024), wukv[:, :, 128:].reshape(256, 1024)], axis=1))
    shared = {
        "w_in": f32(w_in[0]), "w_uq": wuq_p, "w_ukv": wukv_p, "w_out": f32(w_out[0]),
        "w_xq": f32(w_xq[0]), "w_xk": f32(w_xk[0]), "w_xv": f32(w_xv[0]), "w_xo": f32(w_xo[0]),
        "w_gate": f32(w_gate[0]), "w_up": f32(w_up[0]), "w_down": f32(w_down[0]),
        "cols": cols, "lamv": lamv, "gfin": f32(g_final).reshape(1, D), "cf": cf, "cb": cb,
    }
    xs = f32(x)
    ms = f32(mem)
    ps = np.ascontiguousarray(np.asarray(positions), dtype=np.int32)
    in_maps = []
    for b in range(8):
        d = dict(shared)
        d["x"] = xs[b]
        d["mem"] = ms[b]
        d["pos"] = ps[b:b + 1]
        in_maps.append(d)
    res = run_bass_kernel_spmd(nc, in_maps, core_ids=list(range(8)))
    return np.stack([np.asarray(r["y"], dtype=np.float32) for r in res.results], axis=0)
```
